# Optimizing a Trainium2 kernel written in Bass

```python
import jax, jax.numpy as jnp
from jax import lax
import numpy as np

D_MODEL = 2048
BATCH = 2
SEQ = 8192
DEPTH = 1
DEC_BATCH = 16
DEC_SEQ = 16
PAST_LEN = 4096

CHUNK = 64
Q_BLOCK = 128
EPS = 1e-6
MLA_HEADS = 8
Q_LORA = 512
KV_LORA = 512
QK_NOPE = 128
QK_ROPE = 64
V_HEAD = 128
ROPE_THETA = 10000.0
GDN_HEADS = 8
GDN_DK = 128
GDN_DV = 128
GDN_CONV = 4
GDN_CHUNK = 64
GDN_CONV_DIM = 2 * GDN_HEADS * GDN_DK + GDN_HEADS * GDN_DV
D_FF = 5632
FFN_CONV = 3
MLA_OUT = MLA_HEADS * V_HEAD
GDN_OUT = GDN_HEADS * GDN_DV
D_MIX = MLA_OUT + GDN_OUT
D_IN = Q_LORA + KV_LORA + QK_ROPE + GDN_CONV_DIM + GDN_OUT + 2 * GDN_HEADS

kernel_name = 'hybrid_mla_gdn_convffn_stream_step'


def rmsnorm(x, g):
    xf = x.astype(jnp.float32)
    y = xf * lax.rsqrt(jnp.mean(xf * xf, axis=-1, keepdims=True) + EPS)
    return (y * g.astype(jnp.float32)).astype(x.dtype)


def l2norm(x):
    xf = x.astype(jnp.float32)
    return xf * lax.rsqrt(jnp.sum(xf * xf, axis=-1, keepdims=True) + EPS)


def rope_cos_sin(pos):
    half = QK_ROPE // 2
    inv = 1.0 / (ROPE_THETA ** (jnp.arange(half, dtype=jnp.float32) / half))
    ang = pos.astype(jnp.float32)[:, None] * inv[None, :]
    return jnp.cos(ang), jnp.sin(ang)


def apply_rope(x, cos, sin):
    half = QK_ROPE // 2
    xf = x.astype(jnp.float32)
    x1, x2 = xf[..., :half], xf[..., half:]
    return jnp.concatenate([x1 * cos - x2 * sin, x2 * cos + x1 * sin], axis=-1).astype(x.dtype)


def causal_dwconv(x_hist, w):
    C = x_hist.shape[-1]
    return lax.conv_general_dilated(x_hist, w[:, None, :].astype(x_hist.dtype), window_strides=(1,),
                                    padding='VALID', dimension_numbers=('NWC', 'WIO', 'NWC'),
                                    feature_group_count=C)


def split_in(proj):
    offs = np.cumsum([Q_LORA, KV_LORA, QK_ROPE, GDN_CONV_DIM, GDN_OUT, GDN_HEADS]).tolist()
    return jnp.split(proj, offs, axis=-1)


def mla_attention(q_nope, q_rope, k_nope, k_rope, c_kv, w_uv, past):
    B, L, H, _ = q_nope.shape
    T = k_nope.shape[1]
    qb = min(Q_BLOCK, L)
    nb = L // qb
    scale = (QK_NOPE + QK_ROPE) ** -0.5
    k_chunk = jnp.arange(T) // CHUNK
    q_pos = past + jnp.arange(L)

    def block(args):
        qn, qr, qp = args
        s = (jnp.einsum('bqhn,bkhn->bhqk', qn, k_nope, preferred_element_type=jnp.float32)
             + jnp.einsum('bqhr,bkr->bhqk', qr, k_rope, preferred_element_type=jnp.float32)) * scale
        visible = k_chunk[None, :] <= (qp // CHUNK)[:, None]
        p = jax.nn.softmax(jnp.where(visible, s, -jnp.inf), axis=-1).astype(c_kv.dtype)
        o_lat = jnp.einsum('bhqk,bkc->bqhc', p, c_kv)
        return jnp.einsum('bqhc,chd->bqhd', o_lat, w_uv)

    qn_b = q_nope.reshape(B, nb, qb, H, QK_NOPE).transpose(1, 0, 2, 3, 4)
    qr_b = q_rope.reshape(B, nb, qb, H, QK_ROPE).transpose(1, 0, 2, 3, 4)
    out = lax.map(block, (qn_b, qr_b, q_pos.reshape(nb, qb)))
    return out.transpose(1, 0, 2, 3, 4).reshape(B, L, H * V_HEAD)


def gated_delta_rule(q, k, v, g, beta, S0):
    B, L, H, DK = k.shape
    DV = v.shape[-1]
    C = min(GDN_CHUNK, L)
    n = L // C
    ch4 = lambda t: t.reshape(B, n, C, H, t.shape[-1]).transpose(1, 0, 3, 2, 4)
    ch3 = lambda t: t.reshape(B, n, C, H).transpose(1, 0, 3, 2)
    q, k, v = ch4(q), ch4(k), ch4(v)
    gc = jnp.cumsum(ch3(g), axis=-1)
    beta = ch3(beta)
    idx = jnp.arange(C)
    lower_incl = idx[:, None] >= idx[None, :]
    strict = idx[:, None] > idx[None, :]
    decay = jnp.exp(jnp.where(lower_incl, gc[..., :, None] - gc[..., None, :], -jnp.inf))
    kk = jnp.einsum('nbhik,nbhjk->nbhij', k, k)
    A = jnp.where(strict, beta[..., :, None] * kk * decay, 0.0)
    rhs = jnp.concatenate([v * beta[..., None], k * (beta * jnp.exp(gc))[..., None]], axis=-1)
    sol = lax.linalg.triangular_solve(A + jnp.eye(C, dtype=A.dtype), rhs, left_side=True, lower=True,
                                      unit_diagonal=True)
    u, w = sol[..., :DV], sol[..., DV:]

    def step(S, inp):
        qc, kc, uc, wc, gcc, dc = inp
        v_new = uc - jnp.einsum('bhck,bhkv->bhcv', wc, S)
        intra = jnp.einsum('bhik,bhjk->bhij', qc, kc) * dc
        o = (jnp.einsum('bhck,bhkv->bhcv', qc * jnp.exp(gcc)[..., None], S)
             + jnp.einsum('bhij,bhjv->bhiv', intra, v_new))
        g_last = gcc[..., -1]
        S = (S * jnp.exp(g_last)[..., None, None]
             + jnp.einsum('bhck,bhcv->bhkv', kc * jnp.exp(g_last[..., None] - gcc)[..., None], v_new))
        return S, o

    S_fin, o = lax.scan(step, S0, (q, k, u, w, gc, decay))
    return o.transpose(1, 0, 3, 2, 4).reshape(B, L, H, DV), S_fin


def hybrid_layer(x, lat_past, krope_past, conv_past, s_past, ffn_past, lw):
    B, L, _ = x.shape
    P = lat_past.shape[1]
    f32 = jnp.float32
    h = rmsnorm(x, lw['g_attn_norm'])
    q_a, kv_a, k_r, qkv, z, a_logit, b_logit = split_in(h @ lw['w_in'])

    q = (rmsnorm(q_a, lw['g_q_lat']) @ lw['w_q_up']).reshape(B, L, MLA_HEADS, QK_NOPE + QK_ROPE)
    cos, sin = rope_cos_sin(P + jnp.arange(L))
    q_nope = rmsnorm(q[..., :QK_NOPE], lw['g_q_nope'])
    q_rope = apply_rope(rmsnorm(q[..., QK_NOPE:], lw['g_q_rope']), cos[:, None, :], sin[:, None, :])
    c_kv = rmsnorm(kv_a, lw['g_kv_lat'])
    k_rope = apply_rope(rmsnorm(k_r, lw['g_k_rope']), cos, sin)
    lat_all = jnp.concatenate([lat_past, c_kv], axis=1)
    kr_all = jnp.concatenate([krope_past, k_rope], axis=1)
    w_uk = lw['w_kv_up'][..., :QK_NOPE]
    w_uv = lw['w_kv_up'][..., QK_NOPE:]
    k_nope = rmsnorm(jnp.einsum('btc,chn->bthn', lat_all, w_uk), lw['g_k_nope'])
    o_a = mla_attention(q_nope, q_rope, k_nope, kr_all, lat_all, w_uv, P)

    qkv_hist = jnp.concatenate([conv_past, qkv], axis=1)
    qkv_c = jax.nn.silu(causal_dwconv(qkv_hist, lw['w_gdn_conv']))
    nk = GDN_HEADS * GDN_DK
    gq = l2norm(qkv_c[..., :nk].reshape(B, L, GDN_HEADS, GDN_DK)) * (GDN_DK ** -0.5)
    gk = l2norm(qkv_c[..., nk:2 * nk].reshape(B, L, GDN_HEADS, GDN_DK))
    gv = qkv_c[..., 2 * nk:].reshape(B, L, GDN_HEADS, GDN_DV).astype(f32)
    beta = jax.nn.sigmoid(b_logit.astype(f32))
    g = -jnp.exp(lw['a_log'].astype(f32)) * jax.nn.softplus(a_logit.astype(f32) + lw['dt_bias'].astype(f32))
    o_core, S_new = gated_delta_rule(gq, gk, gv, g, beta, s_past.astype(f32))
    o_b = rmsnorm(o_core, lw['g_gdn_out']) * jax.nn.silu(z.reshape(B, L, GDN_HEADS, GDN_DV).astype(f32))
    o_b = o_b.astype(x.dtype).reshape(B, L, GDN_OUT)

    x = x + jnp.concatenate([o_a, o_b], axis=-1) @ lw['w_out']

    h2 = rmsnorm(x, lw['g_ffn_norm'])
    gate = h2 @ lw['w_ffn_gate']
    gate_hist = jnp.concatenate([ffn_past, gate], axis=1)
    gate_c = causal_dwconv(gate_hist, lw['w_ffn_conv']) + lw['b_ffn_conv']
    y = x + (jax.nn.silu(gate_c) * (h2 @ lw['w_ffn_up'])) @ lw['w_ffn_down']

    new_state = (c_kv, k_rope, qkv_hist[:, -(GDN_CONV - 1):], S_new.astype(s_past.dtype),
                 gate_hist[:, -(FFN_CONV - 1):])
    return y, new_state


def setup_inputs(seed: int = 0) -> dict:
    key = jax.random.key(seed)
    ks = jax.random.split(key, 32)
    nrm = lambda k, shape, s: jax.random.normal(k, shape, jnp.float32) * s
    gain = lambda k, n: 1.0 + 0.02 * jax.random.normal(k, (DEPTH, n), jnp.float32)
    dt = jnp.exp(jax.random.uniform(ks[20], (DEPTH, GDN_HEADS), jnp.float32, minval=np.log(1e-3), maxval=np.log(1e-1)))
    return {
        'x_prompt': nrm(ks[0], (BATCH, SEQ, D_MODEL), 1.0),
        'x_sample': nrm(ks[1], (DEC_BATCH, DEC_SEQ, D_MODEL), 1.0),
        'cache_mla_latent': nrm(ks[2], (DEPTH, DEC_BATCH, PAST_LEN, KV_LORA), 1.0),
        'cache_mla_krope': nrm(ks[3], (DEPTH, DEC_BATCH, PAST_LEN, QK_ROPE), 1.0),
        'state_gdn_conv': nrm(ks[4], (DEPTH, DEC_BATCH, GDN_CONV - 1, GDN_CONV_DIM), 1.0),
        'state_gdn_S': nrm(ks[5], (DEPTH, DEC_BATCH, GDN_HEADS, GDN_DK, GDN_DV), 0.1),
        'state_ffn_conv': nrm(ks[6], (DEPTH, DEC_BATCH, FFN_CONV - 1, D_FF), 1.0),
        'g_attn_norm': gain(ks[7], D_MODEL),
        'w_in': nrm(ks[8], (DEPTH, D_MODEL, D_IN), D_MODEL ** -0.5),
        'g_q_lat': gain(ks[9], Q_LORA),
        'g_kv_lat': gain(ks[10], KV_LORA),
        'w_q_up': nrm(ks[11], (DEPTH, Q_LORA, MLA_HEADS * (QK_NOPE + QK_ROPE)), Q_LORA ** -0.5),
        'w_kv_up': nrm(ks[12], (DEPTH, KV_LORA, MLA_HEADS, QK_NOPE + V_HEAD), KV_LORA ** -0.5),
        'g_q_nope': gain(ks[13], QK_NOPE),
        'g_q_rope': gain(ks[14], QK_ROPE),
        'g_k_nope': gain(ks[15], QK_NOPE),
        'g_k_rope': gain(ks[16], QK_ROPE),
        'w_gdn_conv': nrm(ks[17], (DEPTH, GDN_CONV, GDN_CONV_DIM), GDN_CONV ** -0.5),
        'a_log': jnp.log(jax.random.uniform(ks[18], (DEPTH, GDN_HEADS), jnp.float32, minval=1.0, maxval=16.0)),
        'dt_bias': dt + jnp.log(-jnp.expm1(-dt)),
        'g_gdn_out': gain(ks[19], GDN_DV),
        'w_out': nrm(ks[21], (DEPTH, D_MIX, D_MODEL), D_MIX ** -0.5),
        'g_ffn_norm': gain(ks[22], D_MODEL),
        'w_ffn_gate': nrm(ks[23], (DEPTH, D_MODEL, D_FF), D_MODEL ** -0.5),
        'w_ffn_up': nrm(ks[24], (DEPTH, D_MODEL, D_FF), D_MODEL ** -0.5),
        'w_ffn_conv': nrm(ks[25], (DEPTH, FFN_CONV, D_FF), FFN_CONV ** -0.5),
        'b_ffn_conv': nrm(ks[26], (DEPTH, D_FF), 0.02),
        'w_ffn_down': nrm(ks[27], (DEPTH, D_FF, D_MODEL), D_FF ** -0.5),
    }


def reference(x_prompt, x_sample, cache_mla_latent, cache_mla_krope, state_gdn_conv, state_gdn_S, state_ffn_conv,
              g_attn_norm, w_in, g_q_lat, g_kv_lat, w_q_up, w_kv_up, g_q_nope, g_q_rope, g_k_nope, g_k_rope,
              w_gdn_conv, a_log, dt_bias, g_gdn_out, w_out, g_ffn_norm, w_ffn_gate, w_ffn_up, w_ffn_conv,
              b_ffn_conv, w_ffn_down):
    xp, xs = x_prompt, x_sample
    Bp, dtp = x_prompt.shape[0], x_prompt.dtype
    new_p, new_s = [], []
    for l in range(DEPTH):
        lw = dict(g_attn_norm=g_attn_norm[l], w_in=w_in[l], g_q_lat=g_q_lat[l], g_kv_lat=g_kv_lat[l],
                  w_q_up=w_q_up[l], w_kv_up=w_kv_up[l], g_q_nope=g_q_nope[l], g_q_rope=g_q_rope[l],
                  g_k_nope=g_k_nope[l], g_k_rope=g_k_rope[l], w_gdn_conv=w_gdn_conv[l], a_log=a_log[l],
                  dt_bias=dt_bias[l], g_gdn_out=g_gdn_out[l], w_out=w_out[l], g_ffn_norm=g_ffn_norm[l],
                  w_ffn_gate=w_ffn_gate[l], w_ffn_up=w_ffn_up[l], w_ffn_conv=w_ffn_conv[l],
                  b_ffn_conv=b_ffn_conv[l], w_ffn_down=w_ffn_down[l])
        xp, st_p = hybrid_layer(xp,
                                jnp.zeros((Bp, 0, KV_LORA), dtp), jnp.zeros((Bp, 0, QK_ROPE), dtp),
                                jnp.zeros((Bp, GDN_CONV - 1, GDN_CONV_DIM), dtp),
                                jnp.zeros((Bp, GDN_HEADS, GDN_DK, GDN_DV), dtp),
                                jnp.zeros((Bp, FFN_CONV - 1, D_FF), dtp), lw)
        xs, st_s = hybrid_layer(xs, cache_mla_latent[l], cache_mla_krope[l], state_gdn_conv[l],
                                state_gdn_S[l], state_ffn_conv[l], lw)
        new_p.append(st_p)
        new_s.append(st_s)
    p_latent, p_krope, p_gdn_conv, p_gdn_S, p_ffn_conv = [jnp.stack(t) for t in zip(*new_p)]
    s_latent, s_krope, s_gdn_conv, s_gdn_S, s_ffn_conv = [jnp.stack(t) for t in zip(*new_s)]
    return (xp, xs, p_latent, p_krope, p_gdn_conv, p_gdn_S, p_ffn_conv,
            s_latent, s_krope, s_gdn_conv, s_gdn_S, s_ffn_conv)
```

```python
import numpy as np
from contextlib import ExitStack
import concourse.bass as bass
import concourse.mybir as mybir
from concourse.bass_utils import run_bass_kernel_spmd

F32 = mybir.dt.float32
F32R = mybir.dt.float32r
BF16 = mybir.dt.bfloat16
AF = mybir.ActivationFunctionType
ALU = mybir.AluOpType
AX = mybir.AxisListType

D = 2048
NT = 64
OWN0 = 47
NOWN = NT - OWN0
EPS = 1e-6
H = 8
DIN = 5200
DFF = 5632
PAST = 4096
NEG = -30000.0


class V:
    def __init__(self, ap, h):
        self.ap = ap
        self.h = h

    def __getitem__(self, idx):
        return V(self.ap[idx], self.h)

    def sub(self, key):
        return V(self.ap, (self.h, key))

    def re(self, s, **kw):
        return V(self.ap.rearrange(s, **kw), self.h)

    def bc(self, shape):
        return V(self.ap.to_broadcast(list(shape)), self.h)

    def unsq(self, d):
        return V(self.ap.unsqueeze(d), self.h)

    def bitc(self, dt):
        return V(self.ap.bitcast(dt), self.h)

    def all2(self):
        return V(self.ap, ('__multi__', ((self.h, 0), (self.h, 1))))


class Sched:
    EPOCH = 16000
    R = 6

    def __init__(self, nc, es):
        self.nc = nc
        self.es = es
        self.engs = {'pe': nc.tensor, 'act': nc.scalar, 'dve': nc.vector, 'pool': nc.gpsimd, 'sp': nc.sync}
        self.cnt = {e: 0 for e in self.engs}
        self.sems = {}
        self.known = {e: {} for e in self.engs}
        self.lastw = {}
        self.readers = {}
        self.dma_cnt = {'sp': 0, 'pool': 0, 'act': 0}
        self.ninst = 0
        self.nwait = 0
        self.uid = 0

    def sem(self, name):
        if name not in self.sems:
            self.sems[name] = self.es.enter_context(self.nc.semaphore(name))
        return self.sems[name]

    def _wait(self, e, ev):
        name, val, src = ev
        if src == 'pe' and e == 'pe':
            return
        k = self.known[e]
        if k.get(name, 0) >= val:
            return
        self.engs[e].wait_ge(self.sem(name), val)
        self.nwait += 1
        k[name] = val

    def _deps(self, reads, writes):
        evs = []
        for h in reads:
            w = self.lastw.get(h)
            if w is not None:
                evs.append(w)
        for h in writes:
            w = self.lastw.get(h)
            if w is not None:
                evs.append(w)
            evs.extend(self.readers.get(h, {}).values())
        return evs

    def _record(self, ev, reads, writes):
        for h in writes:
            self.lastw[h] = ev
            self.readers[h] = {}
        for h in reads:
            if h in writes:
                continue
            self.readers.setdefault(h, {})[ev[2] + ev[0]] = ev

    @staticmethod
    def _flat(lst):
        out = []
        for r in lst:
            h = r.h if isinstance(r, V) else r
            if isinstance(h, tuple) and len(h) == 2 and h[0] == '__multi__':
                out.extend(h[1])
            else:
                out.append(h)
        return out

    def op(self, e, fn, reads, writes):
        reads = self._flat(reads)
        writes = self._flat(writes)
        for ev in self._deps(reads, writes):
            self._wait(e, ev)
        ins = fn(self.engs[e])
        n = self.cnt[e]
        name = "%s_%d" % (e, n // self.EPOCH)
        val = n % self.EPOCH + 1
        ins.then_inc(self.sem(name), 1)
        self.cnt[e] = n + 1
        self.ninst += 1
        self._record((name, val, e), reads, writes)

    def dma(self, q, out, in_, **kw):
        i = self.dma_cnt[q]
        name = "dq_%s_%d" % (q, i % self.R)
        prev = 16 * (i // self.R)
        if prev > 0:
            self._wait(q, (name, prev, 'dma'))
        reads = self._flat([in_])
        writes = self._flat([out])
        for ev in self._deps(reads, writes):
            self._wait(q, ev)
        self.engs[q].dma_start(out=out.ap, in_=in_.ap, **kw).then_inc(self.sem(name), 16)
        self.dma_cnt[q] = i + 1
        self.ninst += 1
        self._record((name, prev + 16, 'dma'), reads, writes)

    def barrier(self):
        evs = []
        for e, n in self.cnt.items():
            if n > 0:
                m = n - 1
                evs.append(("%s_%d" % (e, m // self.EPOCH), m % self.EPOCH + 1, 'x'))
        for q, i in self.dma_cnt.items():
            for r in range(self.R):
                cntr = (i - r + self.R - 1) // self.R
                if cntr > 0:
                    evs.append(("dq_%s_%d" % (q, r), 16 * cntr, 'dma'))
        for e in self.engs:
            for ev in evs:
                self._wait(e, ev)

    def finish(self):
        for q, i in self.dma_cnt.items():
            for r in range(min(i, self.R)):
                last = (i - 1 - r) // self.R + 1 if i - 1 >= r else 0
                cntr = (i - r + self.R - 1) // self.R
                if cntr > 0:
                    self._wait('sp', ("dq_%s_%d" % (q, r), 16 * cntr, 'dma'))


class KB:
    def __init__(self, nc, es):
        self.nc = nc
        self.es = es
        self.S = Sched(nc, es)
        self.n = 0

    def name(self, p):
        self.n += 1
        return "%s%d" % (p, self.n)

    def sb(self, shape, dt, es=None, name=None):
        nm = name or self.name("sb")
        t = (es or self.es).enter_context(self.nc.sbuf_tensor(nm, list(shape), dt))
        return V(t[:], nm)

    def ps(self, shape, dt, es=None, name=None):
        nm = name or self.name("ps")
        t = (es or self.es).enter_context(self.nc.psum_tensor(nm, list(shape), dt))
        return V(t[:], nm)

    def dram(self, name, shape, dt, kind="Internal"):
        t = self.nc.dram_tensor(name, list(shape), dt, kind=kind)
        return V(t.ap(), name)

    def mm(self, out, lhsT, rhs, start, stop):
        self.S.op('pe', lambda e: e.matmul(out.ap, lhsT.ap, rhs.ap, start=start, stop=stop),
                  [lhsT, rhs] + ([] if start else [out]), [out])

    def tr(self, out, in_, ident):
        self.S.op('pe', lambda e: e.transpose(out.ap, in_.ap, ident.ap), [in_, ident], [out])

    def act(self, out, in_, func, bias=None, scale=None, accum=None, eng='act'):
        kw = {}
        rd = [in_]
        if bias is not None:
            kw['bias'] = bias.ap if isinstance(bias, V) else bias
            if isinstance(bias, V):
                rd.append(bias)
        if scale is not None:
            kw['scale'] = scale.ap if isinstance(scale, V) else scale
            if isinstance(scale, V):
                rd.append(scale)
        wr = [out]
        if accum is not None:
            kw['accum_out'] = accum.ap
            wr.append(accum)
        self.S.op('act', lambda e: e.activation(out.ap, in_.ap, func, **kw), rd, wr)

    def tt(self, out, a, b, op, eng='dve'):
        self.S.op(eng, lambda e: e.tensor_tensor(out.ap, a.ap, b.ap, op), [a, b], [out])

    def ts(self, out, a, s1, s2, op0, op1=None, eng='dve'):
        rd = [a]
        if isinstance(s1, V):
            rd.append(s1)
        if isinstance(s2, V):
            rd.append(s2)
        v1 = s1.ap if isinstance(s1, V) else s1
        v2 = s2.ap if isinstance(s2, V) else s2
        if op1 is None:
            self.S.op(eng, lambda e: e.tensor_scalar(out.ap, a.ap, v1, None, op0), rd, [out])
        else:
            self.S.op(eng, lambda e: e.tensor_scalar(out.ap, a.ap, v1, v2, op0, op1), rd, [out])

    def stt(self, out, a, s, b, op0, op1):
        rd = [a, b]
        if isinstance(s, V):
            rd.append(s)
        sv = s.ap if isinstance(s, V) else s
        self.S.op('dve', lambda e: e.scalar_tensor_tensor(out.ap, a.ap, sv, b.ap, op0, op1), rd, [out])

    def cp(self, out, in_, eng='dve'):
        if eng == 'act':
            self.S.op('act', lambda e: e.activation(out.ap, in_.ap, AF.Copy), [in_], [out])
        else:
            self.S.op(eng, lambda e: e.tensor_copy(out.ap, in_.ap), [in_], [out])

    def recip(self, out, in_):
        self.S.op('dve', lambda e: e.reciprocal(out.ap, in_.ap), [in_], [out])

    def red(self, out, in_, op=ALU.add, axis=AX.X):
        self.S.op('dve', lambda e: e.tensor_reduce(out.ap, in_.ap, axis, op), [in_], [out])

    def memset(self, out, val, eng='dve'):
        self.S.op(eng, lambda e: e.memset(out.ap, val), [], [out])

    def dma(self, out, in_, q='sp', **kw):
        self.S.dma(q, out, in_, **kw)

    def rstd(self, out, ss, n, tmp):
        self.act(tmp, ss, AF.Ln, bias=EPS, scale=1.0 / n)
        self.act(out, tmp, AF.Exp, scale=-0.5)


class Seq:
    pass


class Phase(ExitStack):
    def __init__(self, kb):
        super().__init__()
        self.kb = kb

    def __exit__(self, *a):
        self.kb.S.barrier()
        return super().__exit__(*a)


def load_bc(kb, dst, src, q='sp'):
    parts = dst.ap.shape[0]
    kb.dma(dst, V(src.ap.partition_broadcast(parts), src.h), q=q)


WQ_ = 'pool'
CAST_DMA = True


class WLoader:
    def __init__(self, kb, stg, engs=('pool',)):
        self.kb, self.stg, self.engs, self.n = kb, stg, engs, 0

    def load(self, dst, src):
        a, c = dst.ap.shape[1], dst.ap.shape[2]
        if CAST_DMA:
            self.kb.dma(dst, src, q='pool')
            self.n += 1
            return
        st = self.stg[self.n % len(self.stg)]
        sv = st[:, 0:a * c].re("p (a c) -> p a c", a=a)
        self.kb.dma(sv, src, q=WQ_)
        self.kb.cp(dst, sv, eng=self.engs[self.n % len(self.engs)])
        self.n += 1


def build_program(stages=(1, 2, 3, 4), debug=False):
    nc = bass.Bass("TRN2", target_bir_lowering=False)
    es = ExitStack()
    kb = KB(nc, es)
    I = {}

    def inp(name, shape, dt=F32):
        I[name] = kb.dram(name, shape, dt, kind="ExternalInput")
        return I[name]

    def outp(name, shape, dt=F32):
        I[name] = kb.dram(name, shape, dt, kind="ExternalOutput")
        return I[name]

    def scr(name, shape, dt):
        if debug:
            return outp(name, shape, dt)
        return kb.dram(name, shape, dt)

    ident_in = inp("ident", [128, 128])
    tri_in = inp("tri", [4, 128, 128])
    g_attn = inp("g_attn_norm", [D])
    w_in = inp("w_in", [D, DIN])
    g_q_lat = inp("g_q_lat", [512])
    g_kv_lat = inp("g_kv_lat", [512])
    w_q_up = inp("w_q_up", [512, 1536])
    w_kv_up = inp("w_kv_up", [512, 8 * 256])
    g_q_nope = inp("g_q_nope", [128])
    g_q_rope = inp("g_q_rope", [64])
    g_k_nope = inp("g_k_nope", [128])
    g_k_rope = inp("g_k_rope", [64])
    w_gdn_conv = inp("w_gdn_conv", [4, 3072])
    a_log = inp("a_log", [8])
    dt_bias = inp("dt_bias", [8])
    g_gdn_out = inp("g_gdn_out", [128])
    w_out = inp("w_out", [D, D])
    g_ffn = inp("g_ffn_norm", [D])
    w_gate = inp("w_ffn_gate", [D, DFF])
    w_up = inp("w_ffn_up", [D, DFF])
    w_fconv = inp("w_ffn_conv", [3, DFF])
    b_fconv = inp("b_ffn_conv", [DFF])
    w_down = inp("w_ffn_down", [DFF, D])

    seqs = []
    for si, nm in enumerate(['p', 's0', 's1']):
        q = Seq()
        q.name = nm
        q.prompt = (si == 0)
        if q.prompt:
            q.ntc, q.ncache, q.own0 = NT, 0, OWN0
            q.xt = [(t, 128) for t in range(NT)]
            q.ntok = NT * 128
            q.halo = True
        else:
            q.ntc, q.ncache, q.own0 = 33, 32, 32
            q.xt = [(32, 16)]
            q.ntok = 16
            q.halo = False
        q.nown = q.ntc - q.own0
        q.nkeys = q.ncache * 128 + q.ntok
        q.x = inp(nm + "_x", [q.ntok, D])
        q.cs = inp(nm + "_cs", [q.ntok, 64])
        q.sn = inp(nm + "_sn", [q.ntok, 64])
        if q.prompt:
            q.kmask = inp(nm + "_kmask", [128, NT])
        else:
            q.c_lat = inp(nm + "_clat", [PAST, 512])
            q.c_kr = inp(nm + "_ckr", [PAST, 64])
            q.c_gconv = inp(nm + "_cgconv", [3, 3072])
            q.c_S = inp(nm + "_cS", [8, 128, 128])
            q.c_fconv = inp(nm + "_cfconv", [2, DFF])
        q.o_lat = outp(nm + "_olat", [q.ntok, 512])
        q.o_kr = outp(nm + "_okr", [q.ntok, 64])
        q.o_gconv = outp(nm + "_ogconv", [3, 3072])
        q.o_S = outp(nm + "_oS", [8, 128, 128])
        q.o_fconv = outp(nm + "_ofconv", [2, DFF])
        q.ny = (q.nown - 1) * 128 if q.prompt else 16
        q.o_y = outp(nm + "_oy", [q.ny, D])
        nx = len(q.xt)
        q.hT_s = scr(nm + "_hT", [nx, 128, 16 * 128], BF16)
        q.ckvT_s = scr(nm + "_ckvT", [128, 4, q.ntc * 128], BF16)
        q.krT_s = scr(nm + "_krT", [64, q.ntc * 128], BF16)
        q.gb_s = scr(nm + "_gb", [nx * 128, 16], F32)
        q.qnT_s = scr(nm + "_qnT", [128, 8, q.nown * 128], BF16)
        q.qrT_s = scr(nm + "_qrT", [64, 8, q.nown * 128], BF16)
        q.z_s = scr(nm + "_z", [q.nown * 128, 1024], F32)
        q.oT_s = scr(nm + "_oT", [16, 128, q.nown * 128], BF16)
        q.xn_s = scr(nm + "_xn", [q.nown * 128, D], F32)
        seqs.append(q)

    ident_f = kb.sb([128, 128], F32)
    ident = kb.sb([128, 128], BF16)
    kb.dma(ident_f, ident_in)
    kb.cp(ident, ident_f)
    ones_f = kb.sb([128, 128], F32)
    kb.memset(ones_f, 1.0)
    ones_b = kb.sb([128, 128], BF16)
    kb.memset(ones_b, 1.0)

    psA = [kb.ps([128, 512], F32) for _ in range(6)]
    psT = [kb.ps([128, 1024], BF16) for _ in range(2)]
    w_in_v = w_in.re("(kt p) c -> p kt c", p=128)
    rr = [0]

    def evac_eng():
        rr[0] += 1
        return 'act' if rr[0] % 2 else 'dve'

    if 1 in stages:
      with Phase(kb) as p1:
        WA = kb.sb([128, 16, 2128], BF16, es=p1)
        WQ = kb.sb([128, 4, 1536], BF16, es=p1)
        stg1 = [kb.sb([128, 2048 if not CAST_DMA else 2], F32, es=p1) for _ in range(2)]
        wl = WLoader(kb, stg1, ('pool', 'dve', 'act'))
        WAkv, WAown = WA.sub('kv'), WA.sub('own')
        for k2 in range(8):
            wl.load(WAkv[:, 2 * k2:2 * k2 + 2, 512:1088], w_in_v[:, 2 * k2:2 * k2 + 2, 512:1088])
        for k8 in range(2):
            wl.load(WAkv[:, 8 * k8:8 * k8 + 8, 2112:2128], w_in_v[:, 8 * k8:8 * k8 + 8, 5184:5200])
        wl.engs = ('pool',)
        for k4 in range(4):
            wl.load(WAown[:, 4 * k4:4 * k4 + 4, 0:512], w_in_v[:, 4 * k4:4 * k4 + 4, 0:512])
        for k2 in range(8):
            wl.load(WAown[:, 2 * k2:2 * k2 + 2, 1088:2112], w_in_v[:, 2 * k2:2 * k2 + 2, 4160:5184])
        wqv = w_q_up.re("(kt p) c -> p kt c", p=128)
        for k1 in range(4):
            wl.load(WQ[:, k1:k1 + 1, :], wqv[:, k1:k1 + 1, :])
        g_attn_bc = kb.sb([128, D], F32, es=p1)
        load_bc(kb, g_attn_bc, g_attn)
        g_kv_bc = kb.sb([128, 512], F32, es=p1)
        load_bc(kb, g_kv_bc, g_kv_lat)
        g_ql_bc = kb.sb([128, 512], F32, es=p1)
        load_bc(kb, g_ql_bc, g_q_lat)
        g_kr_bc = kb.sb([128, 64], F32, es=p1)
        load_bc(kb, g_kr_bc, g_k_rope)
        g_qn_bc = kb.sb([128, 128], F32, es=p1)
        load_bc(kb, g_qn_bc, g_q_nope)
        g_qr_bc = kb.sb([128, 64], F32, es=p1)
        load_bc(kb, g_qr_bc, g_q_rope)
        alog_bc = kb.sb([128, 8], F32, es=p1)
        load_bc(kb, alog_bc, a_log)
        dtb_bc = kb.sb([128, 8], F32, es=p1)
        load_bc(kb, dtb_bc, dt_bias)
        negA = kb.sb([128, 8], F32, es=p1)
        kb.act(negA, alog_bc, AF.Exp)
        kb.ts(negA, negA, -1.0, None, ALU.mult)

        xt = [kb.sb([128, D], F32, es=p1) for _ in range(2)]
        junk = kb.sb([128, D], BF16, es=p1)
        hb = [kb.sb([128, D], BF16, es=p1) for _ in range(2)]
        hT = [kb.sb([128, 16, 128], BF16, es=p1) for _ in range(2)]
        sm = [kb.sb([128, 8], F32, es=p1) for _ in range(4)]
        ckv = [kb.sb([128, 512], F32, es=p1) for _ in range(2)]
        ckvb = kb.sb([128, 512], BF16, es=p1)
        ckvT = [kb.sb([128, 4, 128], BF16, es=p1) for _ in range(2)]
        krn = kb.sb([128, 64], F32, es=p1)
        kro = [kb.sb([128, 64], F32, es=p1) for _ in range(2)]
        krt = kb.sb([128, 64], F32, es=p1)
        krb = kb.sb([128, 64], BF16, es=p1)
        krT = [kb.sb([64, 128], BF16, es=p1) for _ in range(2)]
        gbt = [kb.sb([128, 16], F32, es=p1) for _ in range(2)]
        gtmp = kb.sb([128, 8], F32, es=p1)
        cst = [kb.sb([128, 64], F32, es=p1) for _ in range(3)]
        snt = [kb.sb([128, 64], F32, es=p1) for _ in range(3)]
        qan = kb.sb([128, 512], BF16, es=p1)
        qanT = kb.sb([128, 4, 128], BF16, es=p1)
        qf = kb.sb([128, 8, 192], F32, es=p1)
        qsq = kb.sb([128, 8, 192], F32, es=p1)
        qst = [kb.sb([128, 8], F32, es=p1) for _ in range(6)]
        qnf = kb.sb([128, 8, 128], F32, es=p1)
        qnb = kb.sb([128, 8, 128], BF16, es=p1)
        qrf = kb.sb([128, 8, 64], F32, es=p1)
        qrt = kb.sb([128, 8, 64], F32, es=p1)
        qro = kb.sb([128, 8, 64], F32, es=p1)
        qrb = kb.sb([128, 8, 64], BF16, es=p1)
        qnT = [kb.sb([128, 8, 128], BF16, es=p1) for _ in range(2)]
        qrT = [kb.sb([64, 8, 128], BF16, es=p1) for _ in range(2)]
        zs = [kb.sb([128, 1024], F32, es=p1) for _ in range(2)]

        def stageA(q, ti):
            t, nt = q.xt[ti]
            b = ti % 2
            r0 = ti * 128
            kb.dma(xt[b][:nt], q.x[r0:r0 + nt, :])
            kb.dma(cst[ti % 3][:nt], q.cs[r0:r0 + nt, :])
            kb.dma(snt[ti % 3][:nt], q.sn[r0:r0 + nt, :])
            ss, tmp, rs = sm[0][:nt, 0:1], sm[0][:nt, 1:2], sm[0][:nt, 2:3]
            kb.act(junk[:nt], xt[b][:nt], AF.Square, accum=ss)
            kb.rstd(rs, ss, D, tmp)
            kb.stt(hb[b][:nt], xt[b][:nt], rs, g_attn_bc[:nt], ALU.mult, ALU.mult)

        def stageB(q, ti):
            t, nt = q.xt[ti]
            b = ti % 2
            r0 = ti * 128
            for half in range(2):
                for k in range(8):
                    kt = half * 8 + k
                    kb.tr(psT[half][:, k * 128:k * 128 + nt], hb[b][:nt, kt * 128:(kt + 1) * 128], ident[:nt, :nt])
                kb.cp(hT[b][:, half * 8:half * 8 + 8, :nt], psT[half].re("p (k c) -> p k c", k=8)[:, :, :nt],
                      eng='act' if half == 0 else 'dve')
            kb.dma(q.hT_s[ti].re("p (k c) -> p k c", k=16)[:, :, :nt], hT[b][:, :, :nt])
            pk0, pk1 = (psA[0], psA[1]) if ti % 2 == 0 else (psA[2], psA[3])
            for kt in range(16):
                kb.mm(pk0[:nt], hT[b][:, kt, :nt], WAkv[:, kt, 512:1024], kt == 0, kt == 15)
            for kt in range(16):
                kb.mm(pk1[:nt, 0:64], hT[b][:, kt, :nt], WAkv[:, kt, 1024:1088], kt == 0, kt == 15)
            for kt in range(16):
                kb.mm(pk1[:nt, 64:80], hT[b][:, kt, :nt], WAkv[:, kt, 2112:2128], kt == 0, kt == 15)

        def stageB2(q, ti):
            t, nt = q.xt[ti]
            b = ti % 2
            b3 = ti % 3
            r0 = ti * 128
            pk0, pk1 = (psA[0], psA[1]) if ti % 2 == 0 else (psA[2], psA[3])
            ss, tmp, rs = sm[1][:nt, 0:1], sm[1][:nt, 1:2], sm[1][:nt, 2:3]
            kb.act(junk[:nt, 0:512], pk0[:nt], AF.Square, accum=ss)
            kb.rstd(rs, ss, 512, tmp)
            kb.stt(ckv[b][:nt], pk0[:nt], rs, g_kv_bc[:nt], ALU.mult, ALU.mult)
            kb.dma(q.o_lat[r0:r0 + nt, :], ckv[b][:nt])
            kb.cp(ckvb[:nt], ckv[b][:nt], eng='act')
            for k in range(4):
                kb.tr(psT[0][:, k * 128:k * 128 + nt], ckvb[:nt, k * 128:(k + 1) * 128], ident[:nt, :nt])
            kb.cp(ckvT[b][:, :, :nt], psT[0][:, 0:512].re("p (k c) -> p k c", k=4)[:, :, :nt], eng='act')
            kb.dma(q.ckvT_s[:, :, t * 128:t * 128 + nt], ckvT[b][:, :, :nt])
            ss, tmp, rs = sm[2][:nt, 0:1], sm[2][:nt, 1:2], sm[2][:nt, 2:3]
            kb.act(junk[:nt, 512:576], pk1[:nt, 0:64], AF.Square, accum=ss)
            kb.rstd(rs, ss, 64, tmp)
            kb.stt(krn[:nt], pk1[:nt, 0:64], rs, g_kr_bc[:nt], ALU.mult, ALU.mult)
            kb.tt(kro[b][:nt], krn[:nt], cst[b3][:nt], ALU.mult)
            kb.tt(krt[:nt, 0:32], krn[:nt, 32:64], snt[b3][:nt, 0:32], ALU.mult)
            kb.tt(krt[:nt, 32:64], krn[:nt, 0:32], snt[b3][:nt, 32:64], ALU.mult)
            kb.tt(kro[b][:nt], kro[b][:nt], krt[:nt], ALU.add)
            kb.dma(q.o_kr[r0:r0 + nt, :], kro[b][:nt])
            kb.cp(krb[:nt], kro[b][:nt], eng='act')
            kb.tr(psT[1][0:64, 0:nt], krb[:nt], ident[:nt, :nt])
            kb.cp(krT[b][:, :nt], psT[1][0:64, 0:nt], eng='act')
            kb.dma(q.krT_s[:, t * 128:t * 128 + nt], krT[b][:, :nt])
            kb.tt(gtmp[:nt], pk1[:nt, 64:72], dtb_bc[:nt], ALU.add)
            kb.act(gtmp[:nt], gtmp[:nt], AF.Exp)
            kb.act(gtmp[:nt], gtmp[:nt], AF.Ln, bias=1.0)
            kb.tt(gbt[b][:nt, 0:8], gtmp[:nt], negA[:nt], ALU.mult)
            kb.act(gbt[b][:nt, 8:16], pk1[:nt, 72:80], AF.Sigmoid)
            kb.dma(q.gb_s[r0:r0 + nt, :], gbt[b][:nt])
            if t < q.own0:
                return
            ot = t - q.own0
            c0 = ot * 128
            pz0, pz1 = psA[4], psA[5]
            for hh, pz in enumerate((pz0, pz1)):
                for kt in range(16):
                    kb.mm(pz[:nt], hT[b][:, kt, :nt], WAown[:, kt, 1088 + hh * 512:1088 + (hh + 1) * 512], kt == 0, kt == 15)
            kb.act(zs[b][:nt, 0:512], pz0[:nt], AF.Silu)
            kb.act(zs[b][:nt, 512:1024], pz1[:nt], AF.Silu)
            kb.dma(q.z_s[c0:c0 + nt, :], zs[b][:nt])
            pq = psA[4]
            for kt in range(16):
                kb.mm(pq[:nt], hT[b][:, kt, :nt], WAown[:, kt, 0:512], kt == 0, kt == 15)
            ss, tmp, rs = sm[3][:nt, 0:1], sm[3][:nt, 1:2], sm[3][:nt, 2:3]
            kb.act(junk[:nt, 0:512], pq[:nt], AF.Square, accum=ss)
            kb.rstd(rs, ss, 512, tmp)
            kb.stt(qan[:nt], pq[:nt], rs, g_ql_bc[:nt], ALU.mult, ALU.mult)
            for k in range(4):
                kb.tr(psT[0][:, k * 128:k * 128 + nt], qan[:nt, k * 128:(k + 1) * 128], ident[:nt, :nt])
            kb.cp(qanT[:, :, :nt], psT[0][:, 0:512].re("p (k c) -> p k c", k=4)[:, :, :nt], eng='dve')
            qfl = qf.re("p h c -> p (h c)")
            for j, pb in enumerate((psA[5], psA[4], psA[5])):
                for kt in range(4):
                    kb.mm(pb[:nt], qanT[:, kt, :nt], WQ[:, kt, j * 512:(j + 1) * 512], kt == 0, kt == 3)
                kb.cp(qfl[:nt, j * 512:(j + 1) * 512], pb[:nt], eng='act' if j % 2 == 0 else 'dve')
            kb.tt(qsq[:nt], qf[:nt], qf[:nt], ALU.mult, eng='pool')
            kb.red(qst[0][:nt], qsq[:nt, :, 0:128])
            kb.red(qst[1][:nt], qsq[:nt, :, 128:192])
            kb.rstd(qst[2][:nt], qst[0][:nt], 128, qst[4][:nt])
            kb.rstd(qst[3][:nt], qst[1][:nt], 64, qst[5][:nt])
            kb.tt(qnf[:nt], qf[:nt, :, 0:128], qst[2][:nt].unsq(2).bc([nt, 8, 128]), ALU.mult)
            kb.tt(qnb[:nt], qnf[:nt], g_qn_bc[:nt].unsq(1).bc([nt, 8, 128]), ALU.mult)
            kb.tt(qrf[:nt], qf[:nt, :, 128:192], qst[3][:nt].unsq(2).bc([nt, 8, 64]), ALU.mult)
            kb.tt(qrf[:nt], qrf[:nt], g_qr_bc[:nt].unsq(1).bc([nt, 8, 64]), ALU.mult)
            kb.tt(qro[:nt], qrf[:nt], cst[b3][:nt].unsq(1).bc([nt, 8, 64]), ALU.mult)
            kb.tt(qrt[:nt, :, 0:32], qrf[:nt, :, 32:64], snt[b3][:nt, 0:32].unsq(1).bc([nt, 8, 32]), ALU.mult)
            kb.tt(qrt[:nt, :, 32:64], qrf[:nt, :, 0:32], snt[b3][:nt, 32:64].unsq(1).bc([nt, 8, 32]), ALU.mult)
            kb.tt(qrb[:nt], qro[:nt], qrt[:nt], ALU.add)
            for hh in range(8):
                kb.tr(psT[0][:, hh * 128:hh * 128 + nt], qnb[:nt, hh, :], ident[:nt, :nt])
            kb.cp(qnT[b][:, :, :nt], psT[0].re("p (k c) -> p k c", k=8)[:, :, :nt], eng='act')
            kb.dma(q.qnT_s[:, :, c0:c0 + nt], qnT[b][:, :, :nt])
            for hh in range(8):
                kb.tr(psT[1][0:64, hh * 128:hh * 128 + nt], qrb[:nt, hh, :], ident[:nt, :nt])
            kb.cp(qrT[b][:, :, :nt], psT[1][0:64, :].re("p (k c) -> p k c", k=8)[:, :, :nt], eng='dve')
            kb.dma(q.qrT_s[:, :, c0:c0 + nt], qrT[b][:, :, :nt])

        ck4 = [kb.sb([128, 4, 512], BF16, es=p1) for _ in range(2)]
        ckT4 = [kb.sb([128, 4, 512], BF16, es=p1) for _ in range(2)]
        kr4 = [kb.sb([128, 4, 64], BF16, es=p1) for _ in range(2)]
        krT4 = [kb.sb([64, 512], BF16, es=p1) for _ in range(2)]
        psK = psA[5].bitc(BF16)

        def cached_group(q, gi):
            b = gi % 2
            t0 = gi * 4
            r0, r1 = t0 * 128, (t0 + 4) * 128
            kb.dma(ck4[b], q.c_lat[r0:r1, :].re("(t p) c -> p t c", p=128), q='pool')
            kb.dma(kr4[b], q.c_kr[r0:r1, :].re("(t p) c -> p t c", p=128), q='pool')
            for half in range(2):
                for kk in range(2):
                    k = half * 2 + kk
                    for j in range(4):
                        kb.tr(psT[half][:, kk * 512 + j * 128:kk * 512 + (j + 1) * 128], ck4[b][:, j, k * 128:(k + 1) * 128], ident)
                kb.cp(ckT4[b][:, half * 2:half * 2 + 2, :], psT[half].re("p (k c) -> p k c", k=2), eng='act' if half == 0 else 'dve')
            kb.dma(q.ckvT_s[:, :, r0:r1], ckT4[b])
            for j in range(4):
                kb.tr(psK[0:64, j * 128:(j + 1) * 128], kr4[b][:, j, :], ident)
            kb.cp(krT4[b], psK[0:64, 0:512], eng='dve')
            kb.dma(q.krT_s[:, r0:r1], krT4[b])

        for q in seqs:
            if q.name not in SEQS_ON:
                continue
            for gi in range(q.ncache // 4):
                cached_group(q, gi)
            n = len(q.xt)
            stageA(q, 0)
            if n > 1:
                stageA(q, 1)
            stageB(q, 0)
            for ti in range(n):
                if ti + 2 < n:
                    stageA(q, ti + 2)
                if ti + 1 < n:
                    stageB(q, ti + 1)
                stageB2(q, ti)

    if 2 in stages:
      with Phase(kb) as p2:
        WG = kb.sb([128, 16, 3072], BF16, es=p2)
        WG_PENDING = True
        tri = kb.sb([128, 4, 128], F32, es=p2)
        kb.dma(tri, tri_in.re("k p c -> p k c"))
        LinclT, Umat, Mincl, Mstrict = tri[:, 0, :], tri[:, 1, :], tri[:, 2, :], tri[:, 3, :]
        g_go_bc = kb.sb([128, 128], F32, es=p2)
        load_bc(kb, g_go_bc, g_gdn_out)
        h3o = kb.sb([4, 3072], F32, es=p2)
        wc4 = h3o
        kb.dma(wc4, w_gdn_conv)
        wconv = kb.sb([128, 24, 4], F32, es=p2)
        for ct in range(24):
            kb.tr(psA[ct % 2][:, 0:4], wc4[:, ct * 128:(ct + 1) * 128], ident_f[:4, :4])
            kb.cp(wconv[:, ct, :], psA[ct % 2][:, 0:4], eng=evac_eng())
        hist3 = kb.sb([128, 24, 3], F32, es=p2)
        Sm = kb.sb([128, 8, 128], F32, es=p2)
        Sb = kb.sb([128, 8, 128], BF16, es=p2)
        hTb = [kb.sb([128, 16, 256], BF16, es=p2)] * 2
        raw = [kb.sb([128, 260], F32, es=p2) for _ in range(2)]
        acc = [kb.sb([128, 256], F32, es=p2) for _ in range(2)]
        qkvT = kb.sb([128, 24, 256], BF16, es=p2)
        stg2 = [qkvT.re("p a c -> p (a c)").bitc(F32)]
        wl2 = WLoader(kb, stg2, ('dve', 'act', 'pool'))
        for k2 in range(8):
            for hc in range(2):
                wl2.load(WG[:, 2 * k2:2 * k2 + 2, hc * 1536:(hc + 1) * 1536],
                         w_in_v[:, 2 * k2:2 * k2 + 2, 1088 + hc * 1536:1088 + (hc + 1) * 1536])
        ktm = kb.sb([128, 8, 128], BF16, es=p2)
        vtm = kb.sb([128, 8, 128], BF16, es=p2)
        qtm = kb.sb([128, 8, 128], BF16, es=p2)
        st = [kb.sb([128, 8], F32, es=p2) for _ in range(8)]
        kn = kb.sb([128, 8, 128], BF16, es=p2)
        qn = kb.sb([128, 8, 128], BF16, es=p2)
        qg = qtm
        knT = kb.sb([128, 8, 128], BF16, es=p2)
        qnT2 = kb.sb([128, 8, 128], BF16, es=p2)
        qgT = kb.sb([128, 8, 128], BF16, es=p2)
        gbt2 = [kb.sb([128, 16], F32, es=p2) for _ in range(2)]
        gs = kb.sb([128, 8, 8], F32, es=p2)
        rhsg = kb.sb([128, 8, 128], F32, es=p2)
        sq = rhsg
        Dm = kb.sb([128, 8, 128], F32, es=p2)
        NB = rhsg
        NbR = [kb.sb([128, 8, 128], F32R, es=p2) for _ in range(2)]
        YbR = [kb.sb([128, 8, 128], F32R, es=p2) for _ in range(2)]
        Nb16 = [kb.sb([128, 8, 16], BF16, es=p2) for _ in range(2)]
        Yb16 = [kb.sb([128, 8, 16], BF16, es=p2) for _ in range(2)]
        ident_r = kb.sb([128, 128], F32R, es=p2)
        kb.cp(ident_r, ident_f)
        intra = kb.sb([128, 8, 128], BF16, es=p2)
        intraT = kb.sb([128, 8, 128], BF16, es=p2)
        DmA = Dm.all2()
        DmH = [Dm[:, 0:4, :].sub(0), Dm[:, 4:8, :].sub(1)]
        TT = Dm
        onf = DmA
        TTbR = kb.sb([128, 8, 128], F32R, es=p2)
        RuR = kb.sb([128, 8, 128], F32R, es=p2)
        RwR = kb.sb([128, 8, 128], F32R, es=p2)
        TTb16 = kb.sb([128, 8, 16], BF16, es=p2)
        Ru16 = kb.sb([128, 8, 128], BF16, es=p2)
        Rw16 = kb.sb([128, 8, 128], BF16, es=p2)
        ub = rhsg
        wT = Rw16
        vn = ktm
        kd = Ru16
        zt = [ub.re('p h c -> p (h c)')] * 2
        ob = intra
        obT = [qnT2] * 2

        def ps3(bank, nt_, n=4):
            return bank.re("p (h j) -> p h j", h=n)

        def headmm(banks, fn):
            for hh in range(8):
                fn(hh, banks[hh // 4], (hh % 4) * 128)

        def evac2(dst, banks, nt_, cols, eng0=None):
            for half in range(2):
                kb.cp(dst[:nt_, half * 4:half * 4 + 4, :cols], ps3(banks[half], nt_)[:nt_, :, :cols],
                      eng=('act' if half == 0 else 'dve') if eng0 is None else eng0)

        def trans8(dst, src, nt_in, nparts_out, bank):
            for hh in range(8):
                kb.tr(bank[:nparts_out, hh * 128:hh * 128 + nt_in], src[:nt_in, hh, :nparts_out], ident[:nt_in, :nt_in])
            kb.cp(dst[:nparts_out, :, :nt_in], bank.re("p (h c) -> p h c", h=8)[:nparts_out, :, :nt_in], eng=evac_eng())

        def gdn_tile(q, t, nt, j, use_q, hb_, filler=None):
            ti = t - q.ncache
            hp = (nt == 128)
            Nb, Yb = (NbR, YbR) if hp else (Nb16, Yb16)
            TTb, Ru, Rw = (TTbR, RuR, RwR) if hp else (TTb16, Ru16, Rw16)
            rd = (lambda v: v.bitc(F32)) if hp else (lambda v: v)
            gb = gbt2[ti % 2]
            kb.dma(gb[:nt], q.gb_s[ti * 128:ti * 128 + nt, :])
            g = gb[:, 0:8]
            beta = gb[:, 8:16]
            groups = [(8, ktm, psT[0]), (16, vtm, psT[1])]
            if use_q:
                groups.append((0, qtm, psT[0]))
            for c0_, dst, bank in groups:
                for hh in range(8):
                    kb.tr(bank[:nt, hh * 128:(hh + 1) * 128], qkvT[:, c0_ + hh, j * 128:j * 128 + nt], ident)
                kb.cp(dst[:nt], bank.re("p (h c) -> p h c", h=8)[:nt], eng=evac_eng())
            kb.tt(sq[:nt], ktm[:nt], ktm[:nt], ALU.mult, eng='pool')
            kb.red(st[0][:nt], sq[:nt])
            kb.rstd(st[1][:nt], st[0][:nt], 1.0, st[2][:nt])
            kb.tt(kn[:nt], ktm[:nt], st[1][:nt].unsq(2).bc([nt, 8, 128]), ALU.mult)
            if use_q:
                kb.tt(sq[:nt], qtm[:nt], qtm[:nt], ALU.mult, eng='pool')
                kb.red(st[3][:nt], sq[:nt])
                kb.rstd(st[4][:nt], st[3][:nt], 1.0, st[5][:nt])
                kb.ts(st[4][:nt], st[4][:nt], 128.0 ** -0.5, None, ALU.mult)
                kb.tt(qn[:nt], qtm[:nt], st[4][:nt].unsq(2).bc([nt, 8, 128]), ALU.mult)
            pg = psA[0]
            kb.mm(pg[:nt, 0:8], LinclT[:nt, :nt], g[:nt], True, True)
            kb.mm(pg[:, 8:16], ones_f[:nt, :], g[:nt], True, True)
            gc, egc, gtot, elast, kdec, nbeta, bexp, gtmp2 = [gs[:, k_, :] for k_ in range(8)]
            kb.cp(gc[:nt], pg[:nt, 0:8], eng='dve')
            kb.cp(gtot, pg[:, 8:16], eng='dve')
            kb.act(egc[:nt], gc[:nt], AF.Exp)
            kb.act(elast, gtot, AF.Exp)
            kb.tt(gtmp2[:nt], gtot[:nt], gc[:nt], ALU.subtract)
            kb.act(kdec[:nt], gtmp2[:nt], AF.Exp)
            kb.ts(nbeta[:nt], beta[:nt], -1.0, None, ALU.mult)
            kb.tt(bexp[:nt], beta[:nt], egc[:nt], ALU.mult)
            kb.tt(rhsg[:nt, :, :nt], Umat[:nt, :nt].unsq(1).bc([nt, 8, nt]), g[:nt].unsq(2).bc([nt, 8, nt]), ALU.mult,
                  eng='pool')
            pd = (psA[2], psA[3])
            for half in range(2):
                kb.mm(ps3(pd[half], nt)[:nt, :, :nt], LinclT[:nt, :nt], rhsg[:nt, half * 4:half * 4 + 4, :nt], True, True)
                kb.act(DmH[half][:nt, :, :nt], ps3(pd[half], nt)[:nt, :, :nt], AF.Exp)
            kb.tt(DmA[:nt, :, :nt], DmA[:nt, :, :nt], Mincl[:nt, :nt].unsq(1).bc([nt, 8, nt]), ALU.mult, eng='pool')
            kb.tt(NB[:nt, :, :nt], DmA[:nt, :, :nt], Mstrict[:nt, :nt].unsq(1).bc([nt, 8, nt]), ALU.mult, eng='pool')
            kb.tt(NB[:nt, :, :nt], NB[:nt, :, :nt], nbeta[:nt].unsq(2).bc([nt, 8, nt]), ALU.mult, eng='pool')
            trans8(knT, kn, nt, 128, psT[1])
            pkk = (psA[4], psA[5])
            headmm(pkk, lambda hh, bank, c: kb.mm(bank[:nt, c:c + nt], knT[:, hh, :nt], knT[:, hh, :nt], True, True))
            hv = lambda v: [v[:, 0:4, :].sub(0), v[:, 4:8, :].sub(1)]
            NbH = [hv(Nb[0]), hv(Nb[1])]
            YbH = [hv(Yb[0]), hv(Yb[1])]
            TTH = hv(TT)
            TTbH = hv(TTb)
            for half in range(2):
                kb.tt(NbH[0][half][:nt, :, :nt], ps3(pkk[half], nt)[:nt, :, :nt],
                      NB[:nt, half * 4:half * 4 + 4, :nt], ALU.mult)
            if hp:
                for hh in range(8):
                    kb.mm(psA[hh // 4][:nt, (hh % 4) * 128:(hh % 4) * 128 + nt], NbH[0][hh // 4][:nt, hh % 4, :nt],
                          ident_r[:nt, :nt], True, True)
                for half in range(2):
                    kb.cp(YbH[0][half][:nt, :, :nt], ps3(psA[half], nt)[:nt, :, :nt], eng='act' if half == 0 else 'dve')
            else:
                for hh in range(8):
                    kb.tr(psT[0][:nt, hh * 128:hh * 128 + nt], NbH[0][hh // 4][:nt, hh % 4, :nt], ident[:nt, :nt])
                for half in range(2):
                    kb.cp(YbH[0][half][:nt, :, :nt], psT[0].re("p (h c) -> p h c", h=8)[:nt, half * 4:half * 4 + 4, :nt],
                          eng='act' if half == 0 else 'dve')
            if use_q:
                trans8(qnT2, qn, nt, 128, psT[1])
                pqk = (psA[2], psA[3])
                headmm(pqk, lambda hh, bank, c: kb.mm(bank[:nt, c:c + nt], qnT2[:, hh, :nt], knT[:, hh, :nt], True, True))
                for half in range(2):
                    kb.tt(intra[:nt, half * 4:half * 4 + 4, :nt], ps3(pqk[half], nt)[:nt, :, :nt],
                          DmH[half][:nt, :, :nt], ALU.mult)
                trans8(intraT, intra, nt, nt, psT[0])
                kb.tt(qg[:nt], qn[:nt], egc[:nt].unsq(2).bc([nt, 8, 128]), ALU.mult, eng='pool')
                trans8(qgT, qg, nt, 128, psT[1])
            for half in range(2):
                kb.tt(TTH[half][:nt, :, :nt], rd(YbH[0][half])[:nt, :, :nt],
                      ident_f[:nt, :nt].unsq(1).bc([nt, 4, nt]), ALU.add, eng='dve' if half == 0 else 'pool')
                kb.cp(TTbH[half][:nt, :, :nt], TTH[half][:nt, :, :nt], eng='act')
            nlev = 0
            while (1 << (nlev + 1)) < nt:
                nlev += 1
            cur = 0
            for lev in range(1, nlev + 1):
                nxt = 1 - cur
                last = (lev == nlev)
                pN = (psA[0], psA[1])
                pY = (psA[2], psA[3])
                pT = (psA[4], psA[5])
                for half in range(2):
                    for j in range(4):
                        kb.mm(pN[half][:nt, j * 128:j * 128 + nt], YbH[cur][half][:nt, j, :nt], NbH[cur][half][:nt, j, :nt], True, True)
                    if not last:
                        for j in range(4):
                            kb.mm(pY[half][:nt, j * 128:j * 128 + nt], NbH[cur][half][:nt, j, :nt], YbH[cur][half][:nt, j, :nt], True, True)
                for half in range(2):
                    kb.cp(NbH[nxt][half][:nt, :, :nt], ps3(pN[half], nt)[:nt, :, :nt], eng='act')
                    if not last:
                        kb.cp(YbH[nxt][half][:nt, :, :nt], ps3(pY[half], nt)[:nt, :, :nt], eng='dve')
                for half in range(2):
                    for j in range(4):
                        kb.mm(pT[half][:nt, j * 128:j * 128 + nt], NbH[nxt][half][:nt, j, :nt], TTbH[half][:nt, j, :nt], True, True)
                for half in range(2):
                    kb.tt(TTH[half][:nt, :, :nt], TTH[half][:nt, :, :nt], ps3(pT[half], nt)[:nt, :, :nt], ALU.add)
                    kb.cp(TTbH[half][:nt, :, :nt], TTH[half][:nt, :, :nt], eng='act')
                cur = nxt
                if filler is not None:
                    filler()
            kb.tt(Ru[:nt], vtm[:nt], beta[:nt].unsq(2).bc([nt, 8, 128]), ALU.mult, eng='dve' if hp else 'pool')
            kb.tt(Rw[:nt], kn[:nt], bexp[:nt].unsq(2).bc([nt, 8, 128]), ALU.mult, eng='dve' if hp else 'pool')
            pu = (psA[0], psA[1])
            pw = (psA[2], psA[3])
            headmm(pu, lambda hh, bank, c: kb.mm(bank[:nt, c:c + 128], TTbH[hh // 4][:nt, hh % 4, :nt], Ru[:nt, hh, :], True, True))
            headmm(pw, lambda hh, bank, c: kb.mm(bank[:, c:c + nt], Rw[:nt, hh, :], TTbH[hh // 4][:nt, hh % 4, :nt], True, True))
            evac2(ub, pu, nt, 128)
            evac2(wT, pw, 128, nt)
            pws = (psA[4], psA[5])
            headmm(pws, lambda hh, bank, c: kb.mm(bank[:nt, c:c + 128], wT[:, hh, :nt], Sb[:, hh, :], True, True))
            for half in range(2):
                kb.tt(vn[:nt, half * 4:half * 4 + 4, :], ub[:nt, half * 4:half * 4 + 4, :], ps3(pws[half], nt)[:nt], ALU.subtract)
            if use_q:
                po = (psA[0], psA[1])

                def omm(hh, bank, c):
                    kb.mm(bank[:nt, c:c + 128], intraT[:nt, hh, :nt], vn[:nt, hh, :], True, False)
                    kb.mm(bank[:nt, c:c + 128], qgT[:, hh, :nt], Sb[:, hh, :], False, True)
                headmm(po, omm)
            kb.tt(kd[:nt], kn[:nt], kdec[:nt].unsq(2).bc([nt, 8, 128]), ALU.mult, eng='pool')
            pds = (psA[2], psA[3])
            headmm(pds, lambda hh, bank, c: kb.mm(bank[:, c:c + 128], kd[:nt, hh, :], vn[:nt, hh, :], True, True))
            kb.tt(Sm, Sm, elast.unsq(2).bc([128, 8, 128]), ALU.mult, eng='pool')
            for half in range(2):
                kb.tt(Sm[:, half * 4:half * 4 + 4, :], Sm[:, half * 4:half * 4 + 4, :], ps3(pds[half], 128), ALU.add)
            kb.cp(Sb, Sm, eng='act')
            if filler is not None:
                filler()
            if use_q:
                ot = t - q.own0
                c0 = ot * 128
                z = zt[ot % 2]
                kb.dma(z[:nt], q.z_s[c0:c0 + nt, :])
                for half in range(2):
                    kb.act(onf[:nt, half * 4:half * 4 + 4, :], ps3(po[half], nt)[:nt], AF.Square)
                kb.red(st[6][:nt], onf[:nt])
                kb.rstd(st[7][:nt], st[6][:nt], 128.0, st[5][:nt])
                for half in range(2):
                    kb.tt(onf[:nt, half * 4:half * 4 + 4, :], ps3(po[half], nt)[:nt],
                          st[7][:nt, half * 4:half * 4 + 4].unsq(2).bc([nt, 4, 128]), ALU.mult)
                kb.tt(onf[:nt], onf[:nt], g_go_bc[:nt].unsq(1).bc([nt, 8, 128]), ALU.mult, eng='pool')
                kb.tt(ob[:nt], onf[:nt], z[:nt].re("p (h c) -> p h c", h=8), ALU.mult, eng='pool')
                oT_ = obT[ot % 2]
                trans8(oT_, ob, nt, 128, psT[0])
                kb.dma(q.oT_s[8:16].re("h p c -> p h c")[:, :, c0:c0 + nt], oT_[:, :, :nt])

        for q in seqs:
            if q.name not in SEQS_ON:
                continue
            if q.prompt:
                kb.memset(hist3, 0.0)
                kb.memset(Sm, 0.0)
                kb.memset(Sb, 0.0)
            else:
                kb.dma(h3o[:3], q.c_gconv)
                for ct in range(24):
                    kb.tr(psA[ct % 2][:, 0:3], h3o[:3, ct * 128:(ct + 1) * 128], ident_f[:3, :3])
                    kb.cp(hist3[:, ct, :], psA[ct % 2][:, 0:3], eng=evac_eng())
                kb.dma(Sm, q.c_S.re("h k v -> k h v"))
                kb.cp(Sb, Sm, eng='act')
            blocks = [q.xt[i:i + 2] for i in range(0, len(q.xt), 2)]
            own_blk = min(bi for bi, bl in enumerate(blocks) if any(t >= q.own0 for t, _ in bl))
            psP = [psT[0].bitc(F32), psT[1].bitc(F32)]

            def proj_items(q, bi):
                bl = blocks[bi]
                hb_ = hTb[bi % 2]
                ntok = sum(nt for _, nt in bl)
                compute_q = bi >= own_blk - 1
                cts = list(range(8, 24)) + (list(range(8)) if compute_q else [])
                items = []

                def ld():
                    for j, (t, nt) in enumerate(bl):
                        kb.dma(hb_[:, :, j * 128:j * 128 + nt],
                               q.hT_s[t - q.ncache].re("p (k c) -> p k c", k=16)[:, :, :nt])

                def mk(n_, ct):
                    def f():
                        if n_ == 0:
                            ld()
                        pb = psP[n_ % 2]
                        for kt in range(16):
                            kb.mm(pb[:, :ntok], WG[:, kt, ct * 128:(ct + 1) * 128], hb_[:, kt, :ntok], kt == 0, kt == 15)
                        rw = raw[n_ % 2]
                        ac = acc[n_ % 2]
                        kb.cp(rw[:, 0:3], hist3[:, ct, :], eng='pool')
                        kb.cp(rw[:, 3:3 + ntok], pb[:, :ntok], eng='act')
                        kb.cp(hist3[:, ct, :], rw[:, ntok:ntok + 3], eng='pool')
                        kb.ts(ac[:, :ntok], rw[:, 0:ntok], wconv[:, ct, 0:1], None, ALU.mult)
                        for k_ in range(1, 4):
                            kb.stt(ac[:, :ntok], rw[:, k_:k_ + ntok], wconv[:, ct, k_:k_ + 1], ac[:, :ntok], ALU.mult, ALU.add)
                        kb.act(qkvT[:, ct, :ntok], ac[:, :ntok], AF.Silu)
                    return f
                for n_, ct in enumerate(cts):
                    items.append(mk(n_, ct))
                return items

            for f in proj_items(q, 0):
                f()
            for bi, bl in enumerate(blocks):
                nxt_items = proj_items(q, bi + 1) if bi + 1 < len(blocks) else []
                for j, (t, nt) in enumerate(bl):
                    if j == len(bl) - 1 and nxt_items and INTERLEAVE:
                        NH = 7
                        per = (len(nxt_items) + NH - 1) // NH

                        def filler(items=nxt_items, per=per):
                            for _ in range(per):
                                if items:
                                    items.pop(0)()
                    else:
                        filler = None
                    gdn_tile(q, t, nt, j, t >= q.own0, hTb[bi % 2], filler)
                while nxt_items:
                    nxt_items.pop(0)()
            kb.dma(q.o_S.re("h k v -> k h v"), Sm)
            for ct in range(24):
                pb = psA[(ct // 4) % 2]
                kb.tr(pb[:3, (ct % 4) * 128:(ct % 4 + 1) * 128], hist3[:, ct, :], ident_f)
                if ct % 4 == 3:
                    kb.cp(h3o[:3, (ct - 3) * 128:(ct + 1) * 128], pb[:3, :], eng=evac_eng())
            kb.dma(q.o_gconv, h3o[:3])

    SCALE = 192.0 ** -0.5
    if 3 in stages:
      with Phase(kb) as p3:
        WUK = kb.sb([128, 4, 2048], BF16, es=p3)
        stg3 = [kb.sb([128, 2048 if not CAST_DMA else 2], F32, es=p3) for _ in range(2)]
        wl3 = WLoader(kb, stg3, ('pool', 'dve'))
        wkv_v = w_kv_up.re("(kt p) c -> p kt c", p=128)
        for kt in range(4):
            wl3.load(WUK[:, kt:kt + 1, :], wkv_v[:, kt:kt + 1, :])
        gk_col = kb.sb([128, 1], F32, es=p3)
        kb.dma(gk_col, g_k_nope.re("(p o) -> p o", o=1))
        ckvT3 = kb.sb([128, 4, NT * 128], BF16, es=p3)
        krT3 = kb.sb([64, NT * 128], BF16, es=p3)
        KT = kb.sb([128, NT * 128], BF16, es=p3)
        Vh = kb.sb([128, NT, 128], BF16, es=p3)
        qn_h = kb.sb([128, NOWN * 128], BF16, es=p3)
        qr_h = kb.sb([64, NOWN * 128], BF16, es=p3)
        kmask_sb = kb.sb([128, NT], F32, es=p3)
        sqb = [kb.sb([128, 512], BF16, es=p3) for _ in range(2)]
        rst = [kb.sb([128, 512], F32, es=p3) for _ in range(2)]
        PT = [kb.sb([128, 512], BF16, es=p3) for _ in range(3)]
        oT3 = [kb.sb([128, 512], BF16, es=p3) for _ in range(2)]
        den = kb.sb([128, 512], F32, es=p3)
        dacc = kb.sb([128, 512], F32, es=p3)
        cnt3 = [0, 0, 0]
        for q in seqs:
            if q.name not in SEQS_ON:
                continue
            nkeys = q.nkeys
            for c in range(4):
                kb.dma(ckvT3[:, c, :nkeys], q.ckvT_s[:, c, :nkeys])
            kb.dma(krT3[:, :nkeys], q.krT_s[:, :nkeys])
            if q.prompt:
                kb.dma(kmask_sb, q.kmask)
            else:
                kb.memset(kmask_sb, 0.0)
            ktiles = [(t, min(128, nkeys - t * 128)) for t in range(q.ntc)]
            if q.prompt:
                groups = [[(q.own0, 128)]] + [[(t, 128) for t in range(a, a + 4)] for a in range(q.own0 + 1, q.ntc, 4)]
            else:
                groups = [[(q.own0, 16)]]
            nq_all = sum(nt for g_ in groups for _, nt in g_)
            for hh in range(8):
                for bi, k0 in enumerate(range(0, nkeys, 512)):
                    kw = min(512, nkeys - k0)
                    pk = psA[bi % 2]
                    for c in range(4):
                        kb.mm(pk[:, :kw], WUK[:, c, hh * 256:hh * 256 + 128], ckvT3[:, c, k0:k0 + kw], c == 0, c == 3)
                    sq_ = sqb[bi % 2]
                    kb.act(sq_[:, :kw], pk[:, :kw], AF.Square)
                    pss = psA[2 + bi % 2]
                    kb.mm(pss[:, :kw], ones_b, sq_[:, :kw], True, True)
                    rs_ = rst[bi % 2]
                    kb.ts(rs_[:, :kw], pss[:, :kw], 1.0 / 128, EPS, ALU.mult, ALU.add)
                    kb.act(rs_[:, :kw], rs_[:, :kw], AF.Sqrt)
                    kb.recip(rs_[:, :kw], rs_[:, :kw])
                    kb.stt(KT[:, k0:k0 + kw], pk[:, :kw], gk_col, rs_[:, :kw], ALU.mult, ALU.mult)
                full = [t for t, nk in ktiles if nk == 128]
                for gi, a in enumerate(range(0, len(full), 4)):
                    grp = full[a:a + 4]
                    pv = psA[4 + gi % 2]
                    for sl, t in enumerate(grp):
                        for c in range(4):
                            kb.mm(pv[:, sl * 128:(sl + 1) * 128], ckvT3[:, c, t * 128:(t + 1) * 128],
                                  WUK[:, c, hh * 256 + 128:hh * 256 + 256], c == 0, c == 3)
                    kb.cp(Vh[:, grp[0]:grp[0] + len(grp), :].re("p t c -> p (t c)"), pv[:, :len(grp) * 128], eng=evac_eng())
                for t, nk in ktiles:
                    if nk < 128:
                        pv = psA[4]
                        for c in range(4):
                            kb.mm(pv[:nk, 0:128], ckvT3[:, c, t * 128:t * 128 + nk],
                                  WUK[:, c, hh * 256 + 128:hh * 256 + 256], c == 0, c == 3)
                        kb.cp(Vh[:nk, t, :], pv[:nk, 0:128], eng=evac_eng())
                kb.dma(qn_h[:, :nq_all], q.qnT_s[:, hh, :nq_all])
                kb.dma(qr_h[:, :nq_all], q.qrT_s[:, hh, :nq_all])
                for G in groups:
                    ntq = sum(nt for _, nt in G)
                    c0 = (G[0][0] - q.own0) * 128
                    vis = [kt for kt in ktiles if (kt[0] <= G[-1][0] or not q.prompt)]
                    pnum, pden = psA[2], psA[3]

                    def geom(vi):
                        kt, nk = vis[vi]
                        off = (max(G[0][0], kt) - G[0][0]) * 128 if q.prompt else 0
                        return kt, nk, off, ntq - off

                    def emit_qk(vi):
                        kt, nk, off, nq = geom(vi)
                        pst = psA[vi % 2]
                        kb.mm(pst[:nk, :nq], KT[:, kt * 128:kt * 128 + nk], qn_h[:, c0 + off:c0 + off + nq], True, False)
                        kb.mm(pst[:nk, :nq], krT3[:, kt * 128:kt * 128 + nk], qr_h[:, c0 + off:c0 + off + nq], False, True)

                    emit_qk(0)
                    for vi in range(len(vis)):
                        kt, nk, off, nq = geom(vi)
                        if vi + 1 < len(vis):
                            emit_qk(vi + 1)
                        pst = psA[vi % 2]
                        pt = PT[cnt3[1] % 3]
                        cnt3[1] += 1
                        kb.act(pt[:nk, :nq], pst[:nk, :nq], AF.Exp, bias=kmask_sb[:nk, kt:kt + 1], scale=SCALE)
                        if q.prompt and kt >= G[0][0]:
                            kb.memset(pt[64:128, 0:64], 0.0, eng='pool')
                        if vi == 0:
                            kb.cp(dacc[:, :ntq], pt[:, :ntq], eng='dve')
                        else:
                            kb.tt(dacc[:nk, off:off + nq], dacc[:nk, off:off + nq], pt[:nk, :nq], ALU.add)
                        kb.mm(pnum[:, off:off + nq], Vh[:nk, kt, :], pt[:nk, :nq], vi == 0, vi == len(vis) - 1)
                    kb.mm(pden[:, :ntq], ones_f, dacc[:, :ntq], True, True)
                    kb.ts(den[:, :ntq], pden[:, :ntq], 1e-30, None, ALU.max)
                    kb.recip(den[:, :ntq], den[:, :ntq])
                    o_ = oT3[cnt3[2] % 2]
                    cnt3[2] += 1
                    kb.tt(o_[:, :ntq], pnum[:, :ntq], den[:, :ntq], ALU.mult)
                    kb.dma(q.oT_s[hh, :, c0:c0 + ntq], o_[:, :ntq])

    if 4 in stages:
      cols0 = {}
      tot = 0
      for q in seqs:
          cols0[q.name] = tot
          tot += sum(nt for t, nt in q.xt if t >= q.own0)
      TOT = tot
      hid_s = scr("hid_s", [44, 128, TOT], BF16)
      with Phase(kb) as p4:
        h2T = kb.sb([128, 16, TOT], BF16, es=p4)
        with Phase(kb) as p4a:
            WO = kb.sb([128, 16, 2048], BF16, es=p4a)
            stg4 = [kb.sb([128, 2048 if not CAST_DMA else 2], F32, es=p4a) for _ in range(2)]
            wl4 = WLoader(kb, stg4, ('pool', 'dve', 'act'))
            wo_v = w_out.re("(kt p) c -> p kt c", p=128)
            WOd = [WO.sub(dc) for dc in range(4)]
            for dc in range(4):
                for k4 in range(4):
                    wl4.load(WOd[dc][:, 4 * k4:4 * k4 + 4, dc * 512:(dc + 1) * 512], wo_v[:, 4 * k4:4 * k4 + 4, dc * 512:(dc + 1) * 512])
                wl4.engs = ('pool',)
            g_ffn_bc = kb.sb([128, D], F32, es=p4a)
            load_bc(kb, g_ffn_bc, g_ffn)
            oTt = [kb.sb([128, 16, 128], BF16, es=p4a) for _ in range(2)]
            xt4 = [kb.sb([128, D], F32, es=p4a) for _ in range(2)]
            xn = [kb.sb([128, D], F32, es=p4a) for _ in range(2)]
            junk4 = kb.sb([128, D], BF16, es=p4a)
            h2 = [kb.sb([128, D], BF16, es=p4a) for _ in range(2)]
            sm4 = kb.sb([128, 4], F32, es=p4a)
            n4 = 0
            tl4 = []
            for q in seqs:
                if q.name not in SEQS_ON:
                    continue
                for (t, nt) in q.xt:
                    if t >= q.own0:
                        tl4.append((q, t, nt))

            def p4_head(i):
                q, t, nt = tl4[i]
                b = i % 2
                ot = t - q.own0
                c0 = ot * 128
                xr0 = (t - q.ncache) * 128
                kb.dma(oTt[b][:, :, :nt], q.oT_s[:, :, c0:c0 + nt].re("k p c -> p k c"), q='pool')
                kb.dma(xt4[b][:nt], q.x[xr0:xr0 + nt, :], q='pool')
                for dc in range(4):
                    pb = psA[dc]
                    for kt in range(16):
                        kb.mm(pb[:nt], oTt[b][:, kt, :nt], WOd[dc][:, kt, dc * 512:(dc + 1) * 512], kt == 0, kt == 15)
                    kb.tt(xn[b][:nt, dc * 512:(dc + 1) * 512], pb[:nt], xt4[b][:nt, dc * 512:(dc + 1) * 512], ALU.add)
                kb.dma(q.xn_s[c0:c0 + nt, :], xn[b][:nt])
                ss, tmp, rs = sm4[:nt, 0:1], sm4[:nt, 1:2], sm4[:nt, 2:3]
                kb.act(junk4[:nt], xn[b][:nt], AF.Square, accum=ss)
                kb.rstd(rs, ss, D, tmp)
                kb.stt(h2[b][:nt], xn[b][:nt], rs, g_ffn_bc[:nt], ALU.mult, ALU.mult)

            def p4_tail(i):
                q, t, nt = tl4[i]
                b = i % 2
                col = cols0[q.name] + (t - q.own0) * 128
                for half in range(2):
                    for k in range(8):
                        kt = half * 8 + k
                        kb.tr(psT[half][:, k * 128:k * 128 + nt], h2[b][:nt, kt * 128:(kt + 1) * 128], ident[:nt, :nt])
                    kb.cp(h2T[:, half * 8:half * 8 + 8, col:col + nt],
                          psT[half].re("p (k c) -> p k c", k=8)[:, :, :nt], eng='act' if half == 0 else 'dve')

            for i in range(len(tl4)):
                p4_head(i)
                if i > 0:
                    p4_tail(i - 1)
            p4_tail(len(tl4) - 1)
        with Phase(kb) as p4b:
            Wg = [kb.sb([128, 16, 512], BF16, es=p4b) for _ in range(2)]
            Wu = [kb.sb([128, 16, 512], BF16, es=p4b) for _ in range(2)]
            cw4 = [kb.sb([4, 512], F32, es=p4b) for _ in range(2)]
            taps = [kb.sb([128, 4], F32, es=p4b) for _ in range(2)]
            cf2 = {q.name: kb.sb([2, 512], F32, es=p4b) for q in seqs if not q.prompt}
            hist2 = kb.sb([128, 2], F32, es=p4b)
            graw = [kb.sb([128, 516], F32, es=p4b) for _ in range(2)]
            gcv = [kb.sb([128, 512], F32, es=p4b) for _ in range(2)]
            sg = [kb.sb([128, 512], F32, es=p4b) for _ in range(2)]
            hid = [kb.sb([128, 512], BF16, es=p4b) for _ in range(3)]
            fout = {q.name: kb.sb([128, 4, 2], F32, es=p4b) for q in seqs}
            fo2 = [kb.sb([2, 512], F32, es=p4b) for _ in range(2)]
            wgv = w_gate.re("(kt p) c -> p kt c", p=128)
            wuv = w_up.re("(kt p) c -> p kt c", p=128)
            n5 = 0
            stg5 = [kb.sb([128, 4096 if not CAST_DMA else 2], F32, es=p4b) for _ in range(2)]
            wl5 = WLoader(kb, stg5, ('pool',))
            def load_fc(fc):
                wb = fc % 2
                for k8 in range(2):
                    wl5.load(Wg[wb][:, 8 * k8:8 * k8 + 8, :], wgv[:, 8 * k8:8 * k8 + 8, fc * 512:(fc + 1) * 512])
                    wl5.load(Wu[wb][:, 8 * k8:8 * k8 + 8, :], wuv[:, 8 * k8:8 * k8 + 8, fc * 512:(fc + 1) * 512])
                kb.dma(cw4[wb][0:3, :], w_fconv[:, fc * 512:(fc + 1) * 512], q='pool')
                kb.dma(cw4[wb][3:4, :], b_fconv.re("(o c) -> o c", o=1)[:, fc * 512:(fc + 1) * 512], q='pool')

            load_fc(0)
            for fc in range(11):
                wb = fc % 2
                if fc + 1 < 11:
                    load_fc(fc + 1)
                for q in seqs:
                    if not q.prompt and q.name in SEQS_ON:
                        pass
                for ffi in range(4):
                    ff = fc * 4 + ffi
                    tp = taps[ff % 2]
                    kb.tr(psA[5][:, 0:4], cw4[wb][:4, ffi * 128:(ffi + 1) * 128], ident_f[:4, :4])
                    kb.cp(tp, psA[5][:, 0:4], eng='dve')
                    for q in seqs:
                        if q.name not in SEQS_ON:
                            continue
                        if q.prompt:
                            blocks = [(0, 128, True)] + [(128 + 512 * i, 512, False) for i in range((q.nown - 1) // 4)]
                            kb.memset(hist2, 0.0, eng='pool')
                        else:
                            blocks = [(0, 16, False)]
                            if ffi == 0:
                                kb.dma(cf2[q.name], q.c_fconv[:, fc * 512:(fc + 1) * 512])
                                q.cf2cur = cf2[q.name]
                            kb.tr(psA[5][:, 8:10], q.cf2cur[:2, ffi * 128:(ffi + 1) * 128], ident_f[:2, :2])
                            kb.cp(hist2, psA[5][:, 8:10], eng='dve')
                        for (cb, ntok, halo) in blocks:
                            col = cols0[q.name] + cb
                            n5 += 1
                            pg = psA[n5 % 2]
                            pu_ = psA[2 + n5 % 2]
                            for kt in range(16):
                                kb.mm(pg[:, :ntok], Wg[wb][:, kt, ffi * 128:(ffi + 1) * 128], h2T[:, kt, col:col + ntok], kt == 0, kt == 15)
                            if not halo:
                                for kt in range(16):
                                    kb.mm(pu_[:, :ntok], Wu[wb][:, kt, ffi * 128:(ffi + 1) * 128], h2T[:, kt, col:col + ntok], kt == 0, kt == 15)
                            gr = graw[n5 % 2]
                            kb.cp(gr[:, 0:2], hist2, eng='pool')
                            kb.cp(gr[:, 2:2 + ntok], pg[:, :ntok], eng='act')
                            kb.cp(hist2, gr[:, ntok:ntok + 2], eng='pool')
                            if halo:
                                continue
                            gv = gcv[n5 % 2]
                            kb.ts(gv[:, :ntok], gr[:, 0:ntok], tp[:, 0:1], None, ALU.mult)
                            kb.stt(gv[:, :ntok], gr[:, 1:1 + ntok], tp[:, 1:2], gv[:, :ntok], ALU.mult, ALU.add)
                            kb.stt(gv[:, :ntok], gr[:, 2:2 + ntok], tp[:, 2:3], gv[:, :ntok], ALU.mult, ALU.add)
                            sg_ = sg[n5 % 2]
                            kb.act(sg_[:, :ntok], gv[:, :ntok], AF.Silu, bias=tp[:, 3:4])
                            hd = hid[n5 % 3]
                            kb.tt(hd[:, :ntok], sg_[:, :ntok], pu_[:, :ntok], ALU.mult)
                            kb.dma(hid_s[ff, :, col:col + ntok], hd[:, :ntok])
                        kb.cp(fout[q.name][:, ffi, :], hist2, eng='pool')
                for q in seqs:
                    if q.name not in SEQS_ON:
                        continue
                    n5 += 1
                    pb = psA[4]
                    for ffi in range(4):
                        kb.tr(pb[:2, ffi * 128:(ffi + 1) * 128], fout[q.name][:, ffi, :], ident_f)
                    kb.cp(fo2[n5 % 2], pb[:2, :], eng='dve')
                    kb.dma(q.o_fconv[:, fc * 512:(fc + 1) * 512], fo2[n5 % 2])
      with Phase(kb) as p4c:
        Wd = [kb.sb([128, 44, 512], BF16, es=p4c) for _ in range(2)]
        hidt = [kb.sb([128, 44, 512], BF16, es=p4c) for _ in range(2)]
        n7 = [0]
        xnt = [kb.sb([128, 512], F32, es=p4c) for _ in range(2)]
        yt = [kb.sb([128, 512], F32, es=p4c) for _ in range(2)]
        wdv = w_down.re("(kt p) c -> p kt c", p=128)
        n6 = 0
        stg6 = [kb.sb([128, 4096 if not CAST_DMA else 2], F32, es=p4c) for _ in range(2)]
        wl6 = WLoader(kb, stg6, ('pool',))
        def load_dc(dc):
            wb = dc % 2
            for k8 in range(0, 44, 8):
                k9 = min(44, k8 + 8)
                wl6.load(Wd[wb][:, k8:k9, :], wdv[:, k8:k9, dc * 512:(dc + 1) * 512])

        load_dc(0)
        for dc in range(4):
            wb = dc % 2
            if dc + 1 < 4:
                load_dc(dc + 1)
            for q in seqs:
                if q.name not in SEQS_ON:
                    continue
                tl = [(t, nt) for (t, nt) in q.xt if t >= q.own0 and not (q.halo and t == q.own0)]
                for bi in range(0, len(tl), 4):
                    blk = tl[bi:bi + 4]
                    nb_ = sum(nt for _, nt in blk)
                    n6 += 1
                    hb6 = hidt[n6 % 2]
                    colb = cols0[q.name] + (blk[0][0] - q.own0) * 128
                    kb.dma(hb6[:, :, :nb_], hid_s[:, :, colb:colb + nb_].re("k p c -> p k c"), q='pool')
                    for j, (t, nt) in enumerate(blk):
                        n7[0] += 1
                        b = n7[0] % 2
                        ot = t - q.own0
                        c0 = ot * 128
                        yr0 = (ot - 1) * 128 if q.halo else 0
                        kb.dma(xnt[b][:nt], q.xn_s[c0:c0 + nt, dc * 512:(dc + 1) * 512], q='pool')
                        pb = psA[n7[0] % 4]
                        for kt in range(44):
                            kb.mm(pb[:nt], hb6[:, kt, j * 128:j * 128 + nt], Wd[wb][:, kt, :], kt == 0, kt == 43)
                        kb.tt(yt[b][:nt], pb[:nt], xnt[b][:nt], ALU.add)
                        kb.dma(q.o_y[yr0:yr0 + nt, dc * 512:(dc + 1) * 512], yt[b][:nt])

    return nc, kb, es, seqs, I, locals()


SEQS_ON = ('p', 's0', 's1')
INTERLEAVE = True


def rope_tables(pos):
    half = 32
    inv = (1.0 / (10000.0 ** (np.arange(half, dtype=np.float32) / np.float32(half)))).astype(np.float32)
    ang = pos.astype(np.float32)[:, None] * inv[None, :]
    c = np.cos(ang).astype(np.float32)
    s = np.sin(ang).astype(np.float32)
    return np.concatenate([c, c], 1), np.concatenate([-s, s], 1)


def prep_inputs(inp):
    maps = []
    L = NT * 128
    idx = np.arange(128)
    tri = np.stack([
        (idx[:, None] <= idx[None, :]),
        (idx[:, None] > idx[None, :]),
        (idx[:, None] >= idx[None, :]),
        (idx[:, None] > idx[None, :]),
    ]).astype(np.float32)
    wnames = ['g_attn_norm', 'w_in', 'g_q_lat', 'g_kv_lat', 'w_q_up', 'w_kv_up', 'g_q_nope', 'g_q_rope', 'g_k_nope',
              'g_k_rope', 'w_gdn_conv', 'a_log', 'dt_bias', 'g_gdn_out', 'w_out', 'g_ffn_norm', 'w_ffn_gate',
              'w_ffn_up', 'w_ffn_conv', 'b_ffn_conv', 'w_ffn_down']
    W = {k: np.ascontiguousarray(inp[k][0]) for k in wnames}
    W['w_kv_up'] = W['w_kv_up'].reshape(512, 8 * 256)
    cs_s, sn_s = rope_tables(PAST + np.arange(16))
    for c in range(8):
        b, q = c // 4, c % 4
        nreal = 2048 * (q + 1)
        pad = L - nreal
        m = dict(W)
        xcv = np.zeros((L, D), np.float32)
        xcv[pad:] = inp['x_prompt'][b, :nreal]
        m['p_x'] = xcv
        pos = np.maximum(np.arange(L) - pad, 0)
        cs, sn = rope_tables(pos)
        m['p_cs'] = cs
        m['p_sn'] = sn
        km = np.where(np.arange(L) >= pad, 0.0, NEG).astype(np.float32)
        m['p_kmask'] = np.ascontiguousarray(km.reshape(NT, 128).T)
        m['ident'] = np.eye(128, dtype=np.float32)
        m['tri'] = tri
        for j in range(2):
            sb_ = 2 * c + j
            nm = 's%d' % j
            m[nm + '_x'] = np.ascontiguousarray(inp['x_sample'][sb_])
            m[nm + '_cs'] = cs_s
            m[nm + '_sn'] = sn_s
            m[nm + '_clat'] = np.ascontiguousarray(inp['cache_mla_latent'][0, sb_])
            m[nm + '_ckr'] = np.ascontiguousarray(inp['cache_mla_krope'][0, sb_])
            m[nm + '_cgconv'] = np.ascontiguousarray(inp['state_gdn_conv'][0, sb_])
            m[nm + '_cS'] = np.ascontiguousarray(inp['state_gdn_S'][0, sb_])
            m[nm + '_cfconv'] = np.ascontiguousarray(inp['state_ffn_conv'][0, sb_])
        maps.append(m)
    return maps


_CACHE = {}


def kernel(**inputs):
    inp = {k: np.asarray(v) for k, v in inputs.items()}
    if 'prog' not in _CACHE:
        r_ = build_program(stages=(1, 2, 3, 4))
        nc, kb, es, seqs, I = r_[:5]
        kb.S.finish()
        es.close()
        _CACHE['prog'] = (nc, I)
    nc, I = _CACHE['prog']
    maps = prep_inputs(inp)
    maps = [{k: v for k, v in m.items() if k in I} for m in maps]
    res = run_bass_kernel_spmd(nc, maps, core_ids=list(range(8)))
    r = res.results
    f32 = np.float32
    y_p = np.zeros((2, 8192, D), f32)
    y_s = np.zeros((16, 16, D), f32)
    p_lat = np.zeros((1, 2, 8192, 512), f32)
    p_kr = np.zeros((1, 2, 8192, 64), f32)
    p_gc = np.zeros((1, 2, 3, 3072), f32)
    p_S = np.zeros((1, 2, 8, 128, 128), f32)
    p_fc = np.zeros((1, 2, 2, DFF), f32)
    s_lat = np.zeros((1, 16, 16, 512), f32)
    s_kr = np.zeros((1, 16, 16, 64), f32)
    s_gc = np.zeros((1, 16, 3, 3072), f32)
    s_S = np.zeros((1, 16, 8, 128, 128), f32)
    s_fc = np.zeros((1, 16, 2, DFF), f32)
    own_r0 = (NT - 16) * 128
    for c in range(8):
        b, q = c // 4, c % 4
        rc = r[c]
        sl = slice(q * 2048, (q + 1) * 2048)
        y_p[b, sl] = np.asarray(rc['p_oy'])
        p_lat[0, b, sl] = np.asarray(rc['p_olat'])[own_r0:]
        p_kr[0, b, sl] = np.asarray(rc['p_okr'])[own_r0:]
        if q == 3:
            p_gc[0, b] = np.asarray(rc['p_ogconv'])
            p_S[0, b] = np.asarray(rc['p_oS'])
            p_fc[0, b] = np.asarray(rc['p_ofconv'])
        for j in range(2):
            sb_ = 2 * c + j
            nm = 's%d' % j
            y_s[sb_] = np.asarray(rc[nm + '_oy'])
            s_lat[0, sb_] = np.asarray(rc[nm + '_olat'])
            s_kr[0, sb_] = np.asarray(rc[nm + '_okr'])
            s_gc[0, sb_] = np.asarray(rc[nm + '_ogconv'])
            s_S[0, sb_] = np.asarray(rc[nm + '_oS'])
            s_fc[0, sb_] = np.asarray(rc[nm + '_ofconv'])
    return (y_p, y_s, p_lat, p_kr, p_gc, p_S, p_fc, s_lat, s_kr, s_gc, s_S, s_fc)
```

```python
import numpy as np
from contextlib import ExitStack
import concourse.bass as bass
import concourse.mybir as mybir
from concourse.bass_utils import run_bass_kernel_spmd

F32 = mybir.dt.float32
F32R = mybir.dt.float32r
BF16 = mybir.dt.bfloat16
AF = mybir.ActivationFunctionType
ALU = mybir.AluOpType
AX = mybir.AxisListType

D = 2048
NT = 64
OWN0 = 47
NOWN = NT - OWN0
EPS = 1e-6
H = 8
DIN = 5200
DFF = 5632
PAST = 4096
NEG = -30000.0


class V:
    def __init__(self, ap, h):
        self.ap = ap
        self.h = h

    def __getitem__(self, idx):
        return V(self.ap[idx], self.h)

    def sub(self, key):
        return V(self.ap, (self.h, key))

    def re(self, s, **kw):
        return V(self.ap.rearrange(s, **kw), self.h)

    def bc(self, shape):
        return V(self.ap.to_broadcast(list(shape)), self.h)

    def unsq(self, d):
        return V(self.ap.unsqueeze(d), self.h)

    def bitc(self, dt):
        return V(self.ap.bitcast(dt), self.h)

    def all2(self):
        return V(self.ap, ('__multi__', ((self.h, 0), (self.h, 1))))


class Sched:
    EPOCH = 16000
    R = 6

    def __init__(self, nc, es):
        self.nc = nc
        self.es = es
        self.engs = {'pe': nc.tensor, 'act': nc.scalar, 'dve': nc.vector, 'pool': nc.gpsimd, 'sp': nc.sync}
        self.cnt = {e: 0 for e in self.engs}
        self.sems = {}
        self.known = {e: {} for e in self.engs}
        self.lastw = {}
        self.readers = {}
        self.dma_cnt = {'sp': 0, 'pool': 0, 'act': 0}
        self.ninst = 0
        self.nwait = 0
        self.uid = 0
        self.rec = None

    def sem(self, name):
        if name not in self.sems:
            self.sems[name] = self.es.enter_context(self.nc.semaphore(name))
        return self.sems[name]

    def _wait(self, e, ev):
        name, val, src = ev
        if src == 'pe' and e == 'pe':
            return
        k = self.known[e]
        if k.get(name, 0) >= val:
            return
        self.engs[e].wait_ge(self.sem(name), val)
        self.nwait += 1
        k[name] = val

    def _deps(self, reads, writes):
        evs = []
        for h in reads:
            w = self.lastw.get(h)
            if w is not None:
                evs.append(w)
        for h in writes:
            w = self.lastw.get(h)
            if w is not None:
                evs.append(w)
            evs.extend(self.readers.get(h, {}).values())
        return evs

    def _record(self, ev, reads, writes):
        for h in writes:
            self.lastw[h] = ev
            self.readers[h] = {}
        for h in reads:
            if h in writes:
                continue
            self.readers.setdefault(h, {})[ev[2] + ev[0]] = ev

    @staticmethod
    def _flat(lst):
        out = []
        for r in lst:
            h = r.h if isinstance(r, V) else r
            if isinstance(h, tuple) and len(h) == 2 and h[0] == '__multi__':
                out.extend(h[1])
            else:
                out.append(h)
        return out

    def op(self, e, fn, reads, writes):
        if self.rec is not None:
            self.rec.append(('op', (e, fn, reads, writes)))
            return
        reads = self._flat(reads)
        writes = self._flat(writes)
        for ev in self._deps(reads, writes):
            self._wait(e, ev)
        ins = fn(self.engs[e])
        n = self.cnt[e]
        name = "%s_%d" % (e, n // self.EPOCH)
        val = n % self.EPOCH + 1
        ins.then_inc(self.sem(name), 1)
        self.cnt[e] = n + 1
        self.ninst += 1
        self._record((name, val, e), reads, writes)

    def record(self):
        self.rec = []

    def stop_record(self):
        r, self.rec = self.rec, None
        return r

    def replay(self, items):
        for kind, args in items:
            if kind == 'op':
                self.op(*args)
            else:
                q, out, in_, kw = args
                self.dma(q, out, in_, **kw)

    def dma(self, q, out, in_, **kw):
        if self.rec is not None:
            self.rec.append(('dma', (q, out, in_, kw)))
            return
        i = self.dma_cnt[q]
        name = "dq_%s_%d" % (q, i % self.R)
        prev = 16 * (i // self.R)
        if prev > 0:
            self._wait(q, (name, prev, 'dma'))
        reads = self._flat([in_])
        writes = self._flat([out])
        for ev in self._deps(reads, writes):
            self._wait(q, ev)
        self.engs[q].dma_start(out=out.ap, in_=in_.ap, **kw).then_inc(self.sem(name), 16)
        self.dma_cnt[q] = i + 1
        self.ninst += 1
        self._record((name, prev + 16, 'dma'), reads, writes)

    def barrier(self):
        evs = []
        for e, n in self.cnt.items():
            if n > 0:
                m = n - 1
                evs.append(("%s_%d" % (e, m // self.EPOCH), m % self.EPOCH + 1, 'x'))
        for q, i in self.dma_cnt.items():
            for r in range(self.R):
                cntr = (i - r + self.R - 1) // self.R
                if cntr > 0:
                    evs.append(("dq_%s_%d" % (q, r), 16 * cntr, 'dma'))
        for e in self.engs:
            for ev in evs:
                self._wait(e, ev)

    def finish(self):
        for q, i in self.dma_cnt.items():
            for r in range(min(i, self.R)):
                last = (i - 1 - r) // self.R + 1 if i - 1 >= r else 0
                cntr = (i - r + self.R - 1) // self.R
                if cntr > 0:
                    self._wait('sp', ("dq_%s_%d" % (q, r), 16 * cntr, 'dma'))


class KB:
    def __init__(self, nc, es):
        self.nc = nc
        self.es = es
        self.S = Sched(nc, es)
        self.n = 0

    def name(self, p):
        self.n += 1
        return "%s%d" % (p, self.n)

    def sb(self, shape, dt, es=None, name=None):
        nm = name or self.name("sb")
        t = (es or self.es).enter_context(self.nc.sbuf_tensor(nm, list(shape), dt))
        return V(t[:], nm)

    def ps(self, shape, dt, es=None, name=None):
        nm = name or self.name("ps")
        t = (es or self.es).enter_context(self.nc.psum_tensor(nm, list(shape), dt))
        return V(t[:], nm)

    def dram(self, name, shape, dt, kind="Internal"):
        t = self.nc.dram_tensor(name, list(shape), dt, kind=kind)
        return V(t.ap(), name)

    def mm(self, out, lhsT, rhs, start, stop):
        self.S.op('pe', lambda e: e.matmul(out.ap, lhsT.ap, rhs.ap, start=start, stop=stop),
                  [lhsT, rhs] + ([] if start else [out]), [out])

    def tr(self, out, in_, ident):
        self.S.op('pe', lambda e: e.transpose(out.ap, in_.ap, ident.ap), [in_, ident], [out])

    def act(self, out, in_, func, bias=None, scale=None, accum=None, eng='act'):
        kw = {}
        rd = [in_]
        if bias is not None:
            kw['bias'] = bias.ap if isinstance(bias, V) else bias
            if isinstance(bias, V):
                rd.append(bias)
        if scale is not None:
            kw['scale'] = scale.ap if isinstance(scale, V) else scale
            if isinstance(scale, V):
                rd.append(scale)
        wr = [out]
        if accum is not None:
            kw['accum_out'] = accum.ap
            wr.append(accum)
        self.S.op('act', lambda e: e.activation(out.ap, in_.ap, func, **kw), rd, wr)

    def tt(self, out, a, b, op, eng='dve'):
        self.S.op(eng, lambda e: e.tensor_tensor(out.ap, a.ap, b.ap, op), [a, b], [out])

    def ts(self, out, a, s1, s2, op0, op1=None, eng='dve'):
        rd = [a]
        if isinstance(s1, V):
            rd.append(s1)
        if isinstance(s2, V):
            rd.append(s2)
        v1 = s1.ap if isinstance(s1, V) else s1
        v2 = s2.ap if isinstance(s2, V) else s2
        if op1 is None:
            self.S.op(eng, lambda e: e.tensor_scalar(out.ap, a.ap, v1, None, op0), rd, [out])
        else:
            self.S.op(eng, lambda e: e.tensor_scalar(out.ap, a.ap, v1, v2, op0, op1), rd, [out])

    def stt(self, out, a, s, b, op0, op1):
        rd = [a, b]
        if isinstance(s, V):
            rd.append(s)
        sv = s.ap if isinstance(s, V) else s
        self.S.op('dve', lambda e: e.scalar_tensor_tensor(out.ap, a.ap, sv, b.ap, op0, op1), rd, [out])

    def cp(self, out, in_, eng='dve'):
        if eng == 'act':
            self.S.op('act', lambda e: e.activation(out.ap, in_.ap, AF.Copy), [in_], [out])
        else:
            self.S.op(eng, lambda e: e.tensor_copy(out.ap, in_.ap), [in_], [out])

    def recip(self, out, in_):
        self.S.op('dve', lambda e: e.reciprocal(out.ap, in_.ap), [in_], [out])

    def red(self, out, in_, op=ALU.add, axis=AX.X):
        self.S.op('dve', lambda e: e.tensor_reduce(out.ap, in_.ap, axis, op), [in_], [out])

    def memset(self, out, val, eng='dve'):
        self.S.op(eng, lambda e: e.memset(out.ap, val), [], [out])

    def dma(self, out, in_, q='sp', **kw):
        self.S.dma(q, out, in_, **kw)

    def rstd(self, out, ss, n, tmp):
        self.act(tmp, ss, AF.Ln, bias=EPS, scale=1.0 / n)
        self.act(out, tmp, AF.Exp, scale=-0.5)


class Seq:
    pass


class Phase(ExitStack):
    def __init__(self, kb):
        super().__init__()
        self.kb = kb

    def __exit__(self, *a):
        self.kb.S.barrier()
        return super().__exit__(*a)


def load_bc(kb, dst, src, q='sp'):
    parts = dst.ap.shape[0]
    kb.dma(dst, V(src.ap.partition_broadcast(parts), src.h), q=q)


WQ_ = 'pool'
CAST_DMA = True


class WLoader:
    def __init__(self, kb, stg, engs=('pool',)):
        self.kb, self.stg, self.engs, self.n = kb, stg, engs, 0

    def load(self, dst, src):
        a, c = dst.ap.shape[1], dst.ap.shape[2]
        if CAST_DMA:
            self.kb.dma(dst, src, q='pool')
            self.n += 1
            return
        st = self.stg[self.n % len(self.stg)]
        sv = st[:, 0:a * c].re("p (a c) -> p a c", a=a)
        self.kb.dma(sv, src, q=WQ_)
        self.kb.cp(dst, sv, eng=self.engs[self.n % len(self.engs)])
        self.n += 1


def build_program(stages=(1, 2, 3, 4), debug=False):
    nc = bass.Bass("TRN2", target_bir_lowering=False)
    es = ExitStack()
    kb = KB(nc, es)
    I = {}

    def inp(name, shape, dt=F32):
        I[name] = kb.dram(name, shape, dt, kind="ExternalInput")
        return I[name]

    def outp(name, shape, dt=F32):
        I[name] = kb.dram(name, shape, dt, kind="ExternalOutput")
        return I[name]

    def scr(name, shape, dt):
        if debug:
            return outp(name, shape, dt)
        return kb.dram(name, shape, dt)

    ident_in = inp("ident", [128, 128])
    tri_in = inp("tri", [4, 128, 128])
    g_attn = inp("g_attn_norm", [D])
    w_in = inp("w_in", [D, DIN])
    g_q_lat = inp("g_q_lat", [512])
    g_kv_lat = inp("g_kv_lat", [512])
    w_q_up = inp("w_q_up", [512, 1536])
    w_kv_up = inp("w_kv_up", [512, 8 * 256])
    g_q_nope = inp("g_q_nope", [128])
    g_q_rope = inp("g_q_rope", [64])
    g_k_nope = inp("g_k_nope", [128])
    g_k_rope = inp("g_k_rope", [64])
    w_gdn_conv = inp("w_gdn_conv", [4, 3072])
    a_log = inp("a_log", [8])
    dt_bias = inp("dt_bias", [8])
    g_gdn_out = inp("g_gdn_out", [128])
    w_out = inp("w_out", [D, D])
    g_ffn = inp("g_ffn_norm", [D])
    w_gate = inp("w_ffn_gate", [D, DFF])
    w_up = inp("w_ffn_up", [D, DFF])
    w_fconv = inp("w_ffn_conv", [3, DFF])
    b_fconv = inp("b_ffn_conv", [DFF])
    w_down = inp("w_ffn_down", [DFF, D])

    seqs = []
    for si, nm in enumerate(['p', 's0', 's1']):
        q = Seq()
        q.name = nm
        q.prompt = (si == 0)
        if q.prompt:
            q.ntc, q.ncache, q.own0 = NT, 0, OWN0
            q.xt = [(t, 128) for t in range(NT)]
            q.ntok = NT * 128
            q.halo = True
        else:
            q.ntc, q.ncache, q.own0 = 33, 32, 32
            q.xt = [(32, 16)]
            q.ntok = 16
            q.halo = False
        q.nown = q.ntc - q.own0
        q.nkeys = q.ncache * 128 + q.ntok
        q.x = inp(nm + "_x", [q.ntok, D])
        q.cs = inp(nm + "_cs", [q.ntok, 64])
        q.sn = inp(nm + "_sn", [q.ntok, 64])
        if q.prompt:
            q.kmask = inp(nm + "_kmask", [128, NT])
        else:
            q.c_lat = inp(nm + "_clat", [PAST, 512])
            q.c_kr = inp(nm + "_ckr", [PAST, 64])
            q.c_gconv = inp(nm + "_cgconv", [3, 3072])
            q.c_S = inp(nm + "_cS", [8, 128, 128])
            q.c_fconv = inp(nm + "_cfconv", [2, DFF])
        q.o_lat = outp(nm + "_olat", [q.ntok, 512])
        q.o_kr = outp(nm + "_okr", [q.ntok, 64])
        q.o_gconv = outp(nm + "_ogconv", [3, 3072])
        q.o_S = outp(nm + "_oS", [8, 128, 128])
        q.o_fconv = outp(nm + "_ofconv", [2, DFF])
        q.ny = (q.nown - 1) * 128 if q.prompt else 16
        q.o_y = outp(nm + "_oy", [q.ny, D])
        nx = len(q.xt)
        q.hT_s = scr(nm + "_hT", [nx, 128, 16 * 128], BF16)
        q.ckvT_s = scr(nm + "_ckvT", [128, 4, q.ntc * 128], BF16)
        q.krT_s = scr(nm + "_krT", [64, q.ntc * 128], BF16)
        q.gb_s = scr(nm + "_gb", [nx * 128, 16], F32)
        q.qnT_s = scr(nm + "_qnT", [128, 8, q.nown * 128], BF16)
        q.qrT_s = scr(nm + "_qrT", [64, 8, q.nown * 128], BF16)
        q.z_s = scr(nm + "_z", [q.nown * 128, 1024], F32)
        q.oT_s = scr(nm + "_oT", [16, 128, q.nown * 128], BF16)
        q.xn_s = scr(nm + "_xn", [q.nown * 128, D], F32)
        seqs.append(q)

    ident_f = kb.sb([128, 128], F32)
    ident = kb.sb([128, 128], BF16)
    kb.dma(ident_f, ident_in)
    kb.cp(ident, ident_f)
    ones_f = kb.sb([128, 128], F32)
    kb.memset(ones_f, 1.0)
    ones_b = kb.sb([128, 128], BF16)
    kb.memset(ones_b, 1.0)

    psA = [kb.ps([128, 512], F32) for _ in range(6)]
    psT = [kb.ps([128, 1024], BF16) for _ in range(2)]
    w_in_v = w_in.re("(kt p) c -> p kt c", p=128)
    rr = [0]

    def evac_eng():
        rr[0] += 1
        return 'act' if rr[0] % 2 else 'dve'

    if 1 in stages:
      with Phase(kb) as p1:
        WA = kb.sb([128, 16, 2128], BF16, es=p1)
        WQ = kb.sb([128, 4, 1536], BF16, es=p1)
        stg1 = [kb.sb([128, 2048 if not CAST_DMA else 2], F32, es=p1) for _ in range(2)]
        wl = WLoader(kb, stg1, ('pool', 'dve', 'act'))
        WAkv, WAown = WA.sub('kv'), WA.sub('own')
        for k2 in range(8):
            wl.load(WAkv[:, 2 * k2:2 * k2 + 2, 512:1088], w_in_v[:, 2 * k2:2 * k2 + 2, 512:1088])
        for k8 in range(2):
            wl.load(WAkv[:, 8 * k8:8 * k8 + 8, 2112:2128], w_in_v[:, 8 * k8:8 * k8 + 8, 5184:5200])
        wl.engs = ('pool',)
        for k4 in range(4):
            wl.load(WAown[:, 4 * k4:4 * k4 + 4, 0:512], w_in_v[:, 4 * k4:4 * k4 + 4, 0:512])
        for k2 in range(8):
            wl.load(WAown[:, 2 * k2:2 * k2 + 2, 1088:2112], w_in_v[:, 2 * k2:2 * k2 + 2, 4160:5184])
        wqv = w_q_up.re("(kt p) c -> p kt c", p=128)
        for k1 in range(4):
            wl.load(WQ[:, k1:k1 + 1, :], wqv[:, k1:k1 + 1, :])
        g_attn_bc = kb.sb([128, D], F32, es=p1)
        load_bc(kb, g_attn_bc, g_attn)
        g_kv_bc = kb.sb([128, 512], F32, es=p1)
        load_bc(kb, g_kv_bc, g_kv_lat)
        g_ql_bc = kb.sb([128, 512], F32, es=p1)
        load_bc(kb, g_ql_bc, g_q_lat)
        g_kr_bc = kb.sb([128, 64], F32, es=p1)
        load_bc(kb, g_kr_bc, g_k_rope)
        g_qn_bc = kb.sb([128, 128], F32, es=p1)
        load_bc(kb, g_qn_bc, g_q_nope)
        g_qr_bc = kb.sb([128, 64], F32, es=p1)
        load_bc(kb, g_qr_bc, g_q_rope)
        alog_bc = kb.sb([128, 8], F32, es=p1)
        load_bc(kb, alog_bc, a_log)
        dtb_bc = kb.sb([128, 8], F32, es=p1)
        load_bc(kb, dtb_bc, dt_bias)
        negA = kb.sb([128, 8], F32, es=p1)
        kb.act(negA, alog_bc, AF.Exp)
        kb.ts(negA, negA, -1.0, None, ALU.mult)

        xt = [kb.sb([128, D], F32, es=p1) for _ in range(2)]
        junk_2 = [kb.sb([128, D], BF16, es=p1) for _ in range(2)]
        hb = [kb.sb([128, D], BF16, es=p1) for _ in range(2)]
        hT = [kb.sb([128, 16, 128], BF16, es=p1) for _ in range(2)]
        sm_2 = [[kb.sb([128, 8], F32, es=p1) for _ in range(4)] for _ in range(2)]
        ckv = [kb.sb([128, 512], F32, es=p1) for _ in range(2)]
        ckvb_2 = [kb.sb([128, 512], BF16, es=p1) for _ in range(2)]
        ckvT = [kb.sb([128, 4, 128], BF16, es=p1) for _ in range(2)]
        krn_2 = [kb.sb([128, 64], F32, es=p1) for _ in range(2)]
        kro = [kb.sb([128, 64], F32, es=p1) for _ in range(2)]
        krt_2 = [kb.sb([128, 64], F32, es=p1) for _ in range(2)]
        krb_2 = [kb.sb([128, 64], BF16, es=p1) for _ in range(2)]
        krT = [kb.sb([64, 128], BF16, es=p1) for _ in range(2)]
        gbt = [kb.sb([128, 16], F32, es=p1) for _ in range(2)]
        gtmp_2 = [kb.sb([128, 8], F32, es=p1) for _ in range(2)]
        cst = [kb.sb([128, 64], F32, es=p1) for _ in range(2)]
        snt = [kb.sb([128, 64], F32, es=p1) for _ in range(2)]
        qan_2 = [kb.sb([128, 512], BF16, es=p1) for _ in range(2)]
        qanT_2 = [kb.sb([128, 4, 128], BF16, es=p1) for _ in range(2)]
        qf_2 = [kb.sb([128, 8, 192], F32, es=p1) for _ in range(2)]
        qsq_2 = [kb.sb([128, 8, 192], F32, es=p1) for _ in range(2)]
        qst_2 = [[kb.sb([128, 8], F32, es=p1) for _ in range(6)] for _ in range(2)]
        qnf_2 = [v[:, :, 0:128] for v in qsq_2]
        qnb_2 = [kb.sb([128, 8, 128], BF16, es=p1) for _ in range(2)]
        qrf_2 = [kb.sb([128, 8, 64], F32, es=p1) for _ in range(2)]
        qrt_2 = [kb.sb([128, 8, 64], F32, es=p1) for _ in range(2)]
        qro_2 = [kb.sb([128, 8, 64], F32, es=p1) for _ in range(2)]
        qrb_2 = [kb.sb([128, 8, 64], BF16, es=p1) for _ in range(2)]
        qnT = [kb.sb([128, 8, 128], BF16, es=p1) for _ in range(2)]
        qrT = [kb.sb([64, 8, 128], BF16, es=p1) for _ in range(2)]
        zs = [kb.sb([128, 1024], F32, es=p1) for _ in range(2)]

        def stageA(q, ti):
            t, nt = q.xt[ti]
            b = ti % 2
            junk, ckvb, krn, krt, krb, gtmp, qan, qanT, qf, qsq, qnf, qnb, qrf, qrt, qro, qrb, sm, qst = [v[b] for v in (junk_2, ckvb_2, krn_2, krt_2, krb_2, gtmp_2, qan_2, qanT_2, qf_2, qsq_2, qnf_2, qnb_2, qrf_2, qrt_2, qro_2, qrb_2, sm_2, qst_2)]
            r0 = ti * 128
            kb.dma(xt[b][:nt], q.x[r0:r0 + nt, :])
            kb.dma(cst[ti % 2][:nt], q.cs[r0:r0 + nt, :])
            kb.dma(snt[ti % 2][:nt], q.sn[r0:r0 + nt, :])
            ss, tmp, rs = sm[0][:nt, 0:1], sm[0][:nt, 1:2], sm[0][:nt, 2:3]
            kb.act(junk[:nt], xt[b][:nt], AF.Square, accum=ss)
            kb.rstd(rs, ss, D, tmp)
            kb.stt(hb[b][:nt], xt[b][:nt], rs, g_attn_bc[:nt], ALU.mult, ALU.mult)

        def stageB(q, ti):
            t, nt = q.xt[ti]
            b = ti % 2
            junk, ckvb, krn, krt, krb, gtmp, qan, qanT, qf, qsq, qnf, qnb, qrf, qrt, qro, qrb, sm, qst = [v[b] for v in (junk_2, ckvb_2, krn_2, krt_2, krb_2, gtmp_2, qan_2, qanT_2, qf_2, qsq_2, qnf_2, qnb_2, qrf_2, qrt_2, qro_2, qrb_2, sm_2, qst_2)]
            r0 = ti * 128
            for half in range(2):
                for k in range(8):
                    kt = half * 8 + k
                    kb.tr(psT[b][:, k * 128:k * 128 + nt], hb[b][:nt, kt * 128:(kt + 1) * 128], ident[:nt, :nt])
                kb.cp(hT[b][:, half * 8:half * 8 + 8, :nt], psT[b].re("p (k c) -> p k c", k=8)[:, :, :nt],
                      eng='act' if half == 0 else 'dve')
            kb.dma(q.hT_s[ti].re("p (k c) -> p k c", k=16)[:, :, :nt], hT[b][:, :, :nt])
            pk0, pk1 = (psA[0], psA[1]) if ti % 2 == 0 else (psA[2], psA[3])
            for kt in range(16):
                kb.mm(pk0[:nt], hT[b][:, kt, :nt], WAkv[:, kt, 512:1024], kt == 0, kt == 15)
            for kt in range(16):
                kb.mm(pk1[:nt, 0:64], hT[b][:, kt, :nt], WAkv[:, kt, 1024:1088], kt == 0, kt == 15)
            for kt in range(16):
                kb.mm(pk1[:nt, 64:80], hT[b][:, kt, :nt], WAkv[:, kt, 2112:2128], kt == 0, kt == 15)

        def stageB2(q, ti):
            t, nt = q.xt[ti]
            b = ti % 2
            junk, ckvb, krn, krt, krb, gtmp, qan, qanT, qf, qsq, qnf, qnb, qrf, qrt, qro, qrb, sm, qst = [v[b] for v in (junk_2, ckvb_2, krn_2, krt_2, krb_2, gtmp_2, qan_2, qanT_2, qf_2, qsq_2, qnf_2, qnb_2, qrf_2, qrt_2, qro_2, qrb_2, sm_2, qst_2)]
            b3 = ti % 2
            r0 = ti * 128
            pk0, pk1 = (psA[0], psA[1]) if ti % 2 == 0 else (psA[2], psA[3])
            ss, tmp, rs = sm[1][:nt, 0:1], sm[1][:nt, 1:2], sm[1][:nt, 2:3]
            kb.act(junk[:nt, 0:512], pk0[:nt], AF.Square, accum=ss)
            kb.rstd(rs, ss, 512, tmp)
            kb.stt(ckv[b][:nt], pk0[:nt], rs, g_kv_bc[:nt], ALU.mult, ALU.mult)
            kb.dma(q.o_lat[r0:r0 + nt, :], ckv[b][:nt])
            kb.cp(ckvb[:nt], ckv[b][:nt], eng='act')
            for k in range(4):
                kb.tr(psT[b][:, k * 128:k * 128 + nt], ckvb[:nt, k * 128:(k + 1) * 128], ident[:nt, :nt])
            kb.cp(ckvT[b][:, :, :nt], psT[b][:, 0:512].re("p (k c) -> p k c", k=4)[:, :, :nt], eng='act')
            kb.dma(q.ckvT_s[:, :, t * 128:t * 128 + nt], ckvT[b][:, :, :nt])
            ss, tmp, rs = sm[2][:nt, 0:1], sm[2][:nt, 1:2], sm[2][:nt, 2:3]
            kb.act(junk[:nt, 512:576], pk1[:nt, 0:64], AF.Square, accum=ss)
            kb.rstd(rs, ss, 64, tmp)
            kb.stt(krn[:nt], pk1[:nt, 0:64], rs, g_kr_bc[:nt], ALU.mult, ALU.mult)
            kb.tt(kro[b][:nt], krn[:nt], cst[b3][:nt], ALU.mult)
            kb.tt(krt[:nt, 0:32], krn[:nt, 32:64], snt[b3][:nt, 0:32], ALU.mult)
            kb.tt(krt[:nt, 32:64], krn[:nt, 0:32], snt[b3][:nt, 32:64], ALU.mult)
            kb.tt(kro[b][:nt], kro[b][:nt], krt[:nt], ALU.add)
            kb.dma(q.o_kr[r0:r0 + nt, :], kro[b][:nt])
            kb.cp(krb[:nt], kro[b][:nt], eng='act')
            kb.tr(psT[b][0:64, 0:nt], krb[:nt], ident[:nt, :nt])
            kb.cp(krT[b][:, :nt], psT[b][0:64, 0:nt], eng='act')
            kb.dma(q.krT_s[:, t * 128:t * 128 + nt], krT[b][:, :nt])
            kb.tt(gtmp[:nt], pk1[:nt, 64:72], dtb_bc[:nt], ALU.add)
            kb.act(gtmp[:nt], gtmp[:nt], AF.Exp)
            kb.act(gtmp[:nt], gtmp[:nt], AF.Ln, bias=1.0)
            kb.tt(gbt[b][:nt, 0:8], gtmp[:nt], negA[:nt], ALU.mult)
            kb.act(gbt[b][:nt, 8:16], pk1[:nt, 72:80], AF.Sigmoid)
            kb.dma(q.gb_s[r0:r0 + nt, :], gbt[b][:nt])
            if t < q.own0:
                return
            ot = t - q.own0
            c0 = ot * 128
            pown = psA[4 + b]
            for hh in range(2):
                for kt in range(16):
                    kb.mm(pown[:nt], hT[b][:, kt, :nt], WAown[:, kt, 1088 + hh * 512:1088 + (hh + 1) * 512], kt == 0, kt == 15)
                kb.act(zs[b][:nt, hh * 512:(hh + 1) * 512], pown[:nt], AF.Silu)
            kb.dma(q.z_s[c0:c0 + nt, :], zs[b][:nt])
            pq = pown
            for kt in range(16):
                kb.mm(pq[:nt], hT[b][:, kt, :nt], WAown[:, kt, 0:512], kt == 0, kt == 15)
            ss, tmp, rs = sm[3][:nt, 0:1], sm[3][:nt, 1:2], sm[3][:nt, 2:3]
            kb.act(junk[:nt, 0:512], pq[:nt], AF.Square, accum=ss)
            kb.rstd(rs, ss, 512, tmp)
            kb.stt(qan[:nt], pq[:nt], rs, g_ql_bc[:nt], ALU.mult, ALU.mult)
            for k in range(4):
                kb.tr(psT[b][:, k * 128:k * 128 + nt], qan[:nt, k * 128:(k + 1) * 128], ident[:nt, :nt])
            kb.cp(qanT[:, :, :nt], psT[b][:, 0:512].re("p (k c) -> p k c", k=4)[:, :, :nt], eng='dve')
            qfl = qf.re("p h c -> p (h c)")
            for j, pb in enumerate((pown, pown, pown)):
                for kt in range(4):
                    kb.mm(pb[:nt], qanT[:, kt, :nt], WQ[:, kt, j * 512:(j + 1) * 512], kt == 0, kt == 3)
                kb.cp(qfl[:nt, j * 512:(j + 1) * 512], pb[:nt], eng='act' if j % 2 == 0 else 'dve')
            kb.tt(qsq[:nt], qf[:nt], qf[:nt], ALU.mult, eng='pool')
            kb.red(qst[0][:nt], qsq[:nt, :, 0:128])
            kb.red(qst[1][:nt], qsq[:nt, :, 128:192])
            kb.rstd(qst[2][:nt], qst[0][:nt], 128, qst[4][:nt])
            kb.rstd(qst[3][:nt], qst[1][:nt], 64, qst[5][:nt])
            kb.tt(qnf[:nt], qf[:nt, :, 0:128], qst[2][:nt].unsq(2).bc([nt, 8, 128]), ALU.mult)
            kb.tt(qnb[:nt], qnf[:nt], g_qn_bc[:nt].unsq(1).bc([nt, 8, 128]), ALU.mult)
            kb.tt(qrf[:nt], qf[:nt, :, 128:192], qst[3][:nt].unsq(2).bc([nt, 8, 64]), ALU.mult)
            kb.tt(qrf[:nt], qrf[:nt], g_qr_bc[:nt].unsq(1).bc([nt, 8, 64]), ALU.mult)
            kb.tt(qro[:nt], qrf[:nt], cst[b3][:nt].unsq(1).bc([nt, 8, 64]), ALU.mult)
            kb.tt(qrt[:nt, :, 0:32], qrf[:nt, :, 32:64], snt[b3][:nt, 0:32].unsq(1).bc([nt, 8, 32]), ALU.mult)
            kb.tt(qrt[:nt, :, 32:64], qrf[:nt, :, 0:32], snt[b3][:nt, 32:64].unsq(1).bc([nt, 8, 32]), ALU.mult)
            kb.tt(qrb[:nt], qro[:nt], qrt[:nt], ALU.add)
            for hh in range(8):
                kb.tr(psT[b][:, hh * 128:hh * 128 + nt], qnb[:nt, hh, :], ident[:nt, :nt])
            kb.cp(qnT[b][:, :, :nt], psT[b].re("p (k c) -> p k c", k=8)[:, :, :nt], eng='act')
            kb.dma(q.qnT_s[:, :, c0:c0 + nt], qnT[b][:, :, :nt])
            for hh in range(8):
                kb.tr(psT[b][0:64, hh * 128:hh * 128 + nt], qrb[:nt, hh, :], ident[:nt, :nt])
            kb.cp(qrT[b][:, :, :nt], psT[b][0:64, :].re("p (k c) -> p k c", k=8)[:, :, :nt], eng='dve')
            kb.dma(q.qrT_s[:, :, c0:c0 + nt], qrT[b][:, :, :nt])

        ck4 = [v.bitc(BF16)[:, 0:2048].re("p (t c) -> p t c", t=4) for v in xt]
        ckT4 = [v.re("p (t c) -> p t c", t=4) for v in hb]
        kr4 = [v[:, 0:256].re("p (t c) -> p t c", t=4) for v in junk_2]
        krT4 = [v[0:64, 512:1024] for v in junk_2]
        psK = psA[5].bitc(BF16)

        def cached_group(q, gi):
            b = gi % 2
            t0 = gi * 4
            r0, r1 = t0 * 128, (t0 + 4) * 128
            kb.dma(ck4[b], q.c_lat[r0:r1, :].re("(t p) c -> p t c", p=128), q='pool')
            kb.dma(kr4[b], q.c_kr[r0:r1, :].re("(t p) c -> p t c", p=128), q='pool')
            for half in range(2):
                for kk in range(2):
                    k = half * 2 + kk
                    for j in range(4):
                        kb.tr(psT[half][:, kk * 512 + j * 128:kk * 512 + (j + 1) * 128], ck4[b][:, j, k * 128:(k + 1) * 128], ident)
                kb.cp(ckT4[b][:, half * 2:half * 2 + 2, :], psT[half].re("p (k c) -> p k c", k=2), eng='act' if half == 0 else 'dve')
            kb.dma(q.ckvT_s[:, :, r0:r1], ckT4[b])
            for j in range(4):
                kb.tr(psK[0:64, j * 128:(j + 1) * 128], kr4[b][:, j, :], ident)
            kb.cp(krT4[b], psK[0:64, 0:512], eng='dve')
            kb.dma(q.krT_s[:, r0:r1], krT4[b])

        for q in seqs:
            if q.name not in SEQS_ON:
                continue
            for gi in range(q.ncache // 4):
                cached_group(q, gi)
            n = len(q.xt)
            streams = [[], []]
            for ti in range(n):
                kb.S.record()
                stageA(q, ti)
                stageB(q, ti)
                stageB2(q, ti)
                streams[ti % 2].extend(kb.S.stop_record())
            s0, s1 = streams
            off = min(len(s1), 40)
            merged = list(s0[:off])
            i0, i1 = off, 0
            while i0 < len(s0) or i1 < len(s1):
                if i1 < len(s1):
                    merged.append(s1[i1])
                    i1 += 1
                if i0 < len(s0):
                    merged.append(s0[i0])
                    i0 += 1
            kb.S.replay(merged)

    if 2 in stages:
      with Phase(kb) as p2:
        WG = kb.sb([128, 16, 3072], BF16, es=p2)
        WG_PENDING = True
        tri = kb.sb([128, 4, 128], F32, es=p2)
        kb.dma(tri, tri_in.re("k p c -> p k c"))
        LinclT, Umat, Mincl, Mstrict = tri[:, 0, :], tri[:, 1, :], tri[:, 2, :], tri[:, 3, :]
        g_go_bc = kb.sb([128, 128], F32, es=p2)
        load_bc(kb, g_go_bc, g_gdn_out)
        h3o = kb.sb([4, 3072], F32, es=p2)
        wc4 = h3o
        kb.dma(wc4, w_gdn_conv)
        wconv = kb.sb([128, 24, 4], F32, es=p2)
        for ct in range(24):
            kb.tr(psA[ct % 2][:, 0:4], wc4[:, ct * 128:(ct + 1) * 128], ident_f[:4, :4])
            kb.cp(wconv[:, ct, :], psA[ct % 2][:, 0:4], eng=evac_eng())
        hist3 = kb.sb([128, 24, 3], F32, es=p2)
        Sm = kb.sb([128, 8, 128], F32, es=p2)
        Sb = kb.sb([128, 8, 128], BF16, es=p2)
        hTb = [kb.sb([128, 16, 256], BF16, es=p2)] * 2
        raw = [kb.sb([128, 260], F32, es=p2) for _ in range(2)]
        acc = [kb.sb([128, 256], F32, es=p2) for _ in range(2)]
        qkvT = kb.sb([128, 24, 256], BF16, es=p2)
        stg2 = [qkvT.re("p a c -> p (a c)").bitc(F32)]
        wl2 = WLoader(kb, stg2, ('dve', 'act', 'pool'))
        for k2 in range(8):
            for hc in range(2):
                wl2.load(WG[:, 2 * k2:2 * k2 + 2, hc * 1536:(hc + 1) * 1536],
                         w_in_v[:, 2 * k2:2 * k2 + 2, 1088 + hc * 1536:1088 + (hc + 1) * 1536])
        ktm = kb.sb([128, 8, 128], BF16, es=p2)
        vtm = kb.sb([128, 8, 128], BF16, es=p2)
        qtm = kb.sb([128, 8, 128], BF16, es=p2)
        st = [kb.sb([128, 8], F32, es=p2) for _ in range(8)]
        kn = kb.sb([128, 8, 128], BF16, es=p2)
        qn = kb.sb([128, 8, 128], BF16, es=p2)
        qg = qtm
        knT = kb.sb([128, 8, 128], BF16, es=p2)
        qnT2 = kb.sb([128, 8, 128], BF16, es=p2)
        qgT = kb.sb([128, 8, 128], BF16, es=p2)
        gbt2 = [kb.sb([128, 16], F32, es=p2) for _ in range(2)]
        gs = kb.sb([128, 8, 8], F32, es=p2)
        rhsg = kb.sb([128, 8, 128], F32, es=p2)
        sq = rhsg
        Dm = kb.sb([128, 8, 128], F32, es=p2)
        NB = rhsg
        NbR = [kb.sb([128, 8, 128], F32R, es=p2) for _ in range(2)]
        YbR = [kb.sb([128, 8, 128], F32R, es=p2) for _ in range(2)]
        Nb16 = [kb.sb([128, 8, 16], BF16, es=p2) for _ in range(2)]
        Yb16 = [kb.sb([128, 8, 16], BF16, es=p2) for _ in range(2)]
        ident_r = kb.sb([128, 128], F32R, es=p2)
        kb.cp(ident_r, ident_f)
        intra = kb.sb([128, 8, 128], BF16, es=p2)
        intraT = kb.sb([128, 8, 128], BF16, es=p2)
        DmA = Dm.all2()
        DmH = [Dm[:, 0:4, :].sub(0), Dm[:, 4:8, :].sub(1)]
        TT = Dm
        onf = DmA
        TTbR = kb.sb([128, 8, 128], F32R, es=p2)
        RuR = kb.sb([128, 8, 128], F32R, es=p2)
        RwR = kb.sb([128, 8, 128], F32R, es=p2)
        TTb16 = kb.sb([128, 8, 16], BF16, es=p2)
        Ru16 = kb.sb([128, 8, 128], BF16, es=p2)
        Rw16 = kb.sb([128, 8, 128], BF16, es=p2)
        ub = rhsg
        wT = Rw16
        vn = ktm
        kd = Ru16
        zt = [ub.re('p h c -> p (h c)')] * 2
        ob = intra
        obT = [qnT2] * 2

        def ps3(bank, nt_, n=4):
            return bank.re("p (h j) -> p h j", h=n)

        def headmm(banks, fn):
            for hh in range(8):
                fn(hh, banks[hh // 4], (hh % 4) * 128)

        def evac2(dst, banks, nt_, cols, eng0=None):
            for half in range(2):
                kb.cp(dst[:nt_, half * 4:half * 4 + 4, :cols], ps3(banks[half], nt_)[:nt_, :, :cols],
                      eng=('act' if half == 0 else 'dve') if eng0 is None else eng0)

        def trans8(dst, src, nt_in, nparts_out, bank):
            for hh in range(8):
                kb.tr(bank[:nparts_out, hh * 128:hh * 128 + nt_in], src[:nt_in, hh, :nparts_out], ident[:nt_in, :nt_in])
            kb.cp(dst[:nparts_out, :, :nt_in], bank.re("p (h c) -> p h c", h=8)[:nparts_out, :, :nt_in], eng=evac_eng())

        def gdn_tile(q, t, nt, j, use_q, hb_):
            ti = t - q.ncache
            hp = (nt == 128)
            Nb, Yb = (NbR, YbR) if hp else (Nb16, Yb16)
            TTb, Ru, Rw = (TTbR, RuR, RwR) if hp else (TTb16, Ru16, Rw16)
            rd = (lambda v: v.bitc(F32)) if hp else (lambda v: v)
            gb = gbt2[ti % 2]
            kb.dma(gb[:nt], q.gb_s[ti * 128:ti * 128 + nt, :])
            g = gb[:, 0:8]
            beta = gb[:, 8:16]
            groups = [(8, ktm, psT[0]), (16, vtm, psT[1])]
            if use_q:
                groups.append((0, qtm, psT[0]))
            for c0_, dst, bank in groups:
                for hh in range(8):
                    kb.tr(bank[:nt, hh * 128:(hh + 1) * 128], qkvT[:, c0_ + hh, j * 128:j * 128 + nt], ident)
                kb.cp(dst[:nt], bank.re("p (h c) -> p h c", h=8)[:nt], eng=evac_eng())
            kb.tt(sq[:nt], ktm[:nt], ktm[:nt], ALU.mult, eng='pool')
            kb.red(st[0][:nt], sq[:nt])
            kb.rstd(st[1][:nt], st[0][:nt], 1.0, st[2][:nt])
            kb.tt(kn[:nt], ktm[:nt], st[1][:nt].unsq(2).bc([nt, 8, 128]), ALU.mult)
            if use_q:
                kb.tt(sq[:nt], qtm[:nt], qtm[:nt], ALU.mult, eng='pool')
                kb.red(st[3][:nt], sq[:nt])
                kb.rstd(st[4][:nt], st[3][:nt], 1.0, st[5][:nt])
                kb.ts(st[4][:nt], st[4][:nt], 128.0 ** -0.5, None, ALU.mult)
                kb.tt(qn[:nt], qtm[:nt], st[4][:nt].unsq(2).bc([nt, 8, 128]), ALU.mult)
            pg = psA[0]
            kb.mm(pg[:nt, 0:8], LinclT[:nt, :nt], g[:nt], True, True)
            kb.mm(pg[:, 8:16], ones_f[:nt, :], g[:nt], True, True)
            gc, egc, gtot, elast, kdec, nbeta, bexp, gtmp2 = [gs[:, k_, :] for k_ in range(8)]
            kb.cp(gc[:nt], pg[:nt, 0:8], eng='dve')
            kb.cp(gtot, pg[:, 8:16], eng='dve')
            kb.act(egc[:nt], gc[:nt], AF.Exp)
            kb.act(elast, gtot, AF.Exp)
            kb.tt(gtmp2[:nt], gtot[:nt], gc[:nt], ALU.subtract)
            kb.act(kdec[:nt], gtmp2[:nt], AF.Exp)
            kb.ts(nbeta[:nt], beta[:nt], -1.0, None, ALU.mult)
            kb.tt(bexp[:nt], beta[:nt], egc[:nt], ALU.mult)
            kb.tt(rhsg[:nt, :, :nt], Umat[:nt, :nt].unsq(1).bc([nt, 8, nt]), g[:nt].unsq(2).bc([nt, 8, nt]), ALU.mult,
                  eng='pool')
            pd = (psA[2], psA[3])
            for half in range(2):
                kb.mm(ps3(pd[half], nt)[:nt, :, :nt], LinclT[:nt, :nt], rhsg[:nt, half * 4:half * 4 + 4, :nt], True, True)
                kb.act(DmH[half][:nt, :, :nt], ps3(pd[half], nt)[:nt, :, :nt], AF.Exp)
            kb.tt(DmA[:nt, :, :nt], DmA[:nt, :, :nt], Mincl[:nt, :nt].unsq(1).bc([nt, 8, nt]), ALU.mult, eng='pool')
            kb.tt(NB[:nt, :, :nt], DmA[:nt, :, :nt], Mstrict[:nt, :nt].unsq(1).bc([nt, 8, nt]), ALU.mult, eng='pool')
            kb.tt(NB[:nt, :, :nt], NB[:nt, :, :nt], nbeta[:nt].unsq(2).bc([nt, 8, nt]), ALU.mult, eng='pool')
            trans8(knT, kn, nt, 128, psT[1])
            pkk = (psA[4], psA[5])
            headmm(pkk, lambda hh, bank, c: kb.mm(bank[:nt, c:c + nt], knT[:, hh, :nt], knT[:, hh, :nt], True, True))
            hv = lambda v: [v[:, 0:4, :].sub(0), v[:, 4:8, :].sub(1)]
            NbH = [hv(Nb[0]), hv(Nb[1])]
            YbH = [hv(Yb[0]), hv(Yb[1])]
            TTH = hv(TT)
            TTbH = hv(TTb)
            for half in range(2):
                kb.tt(NbH[0][half][:nt, :, :nt], ps3(pkk[half], nt)[:nt, :, :nt],
                      NB[:nt, half * 4:half * 4 + 4, :nt], ALU.mult)
            if hp:
                for hh in range(8):
                    kb.mm(psA[hh // 4][:nt, (hh % 4) * 128:(hh % 4) * 128 + nt], NbH[0][hh // 4][:nt, hh % 4, :nt],
                          ident_r[:nt, :nt], True, True)
                for half in range(2):
                    kb.cp(YbH[0][half][:nt, :, :nt], ps3(psA[half], nt)[:nt, :, :nt], eng='act' if half == 0 else 'dve')
            else:
                for hh in range(8):
                    kb.tr(psT[0][:nt, hh * 128:hh * 128 + nt], NbH[0][hh // 4][:nt, hh % 4, :nt], ident[:nt, :nt])
                for half in range(2):
                    kb.cp(YbH[0][half][:nt, :, :nt], psT[0].re("p (h c) -> p h c", h=8)[:nt, half * 4:half * 4 + 4, :nt],
                          eng='act' if half == 0 else 'dve')
            if use_q:
                trans8(qnT2, qn, nt, 128, psT[1])
                pqk = (psA[2], psA[3])
                headmm(pqk, lambda hh, bank, c: kb.mm(bank[:nt, c:c + nt], qnT2[:, hh, :nt], knT[:, hh, :nt], True, True))
                for half in range(2):
                    kb.tt(intra[:nt, half * 4:half * 4 + 4, :nt], ps3(pqk[half], nt)[:nt, :, :nt],
                          DmH[half][:nt, :, :nt], ALU.mult)
                trans8(intraT, intra, nt, nt, psT[0])
                kb.tt(qg[:nt], qn[:nt], egc[:nt].unsq(2).bc([nt, 8, 128]), ALU.mult, eng='pool')
                trans8(qgT, qg, nt, 128, psT[1])
            for half in range(2):
                kb.tt(TTH[half][:nt, :, :nt], rd(YbH[0][half])[:nt, :, :nt],
                      ident_f[:nt, :nt].unsq(1).bc([nt, 4, nt]), ALU.add, eng='dve' if half == 0 else 'pool')
                kb.cp(TTbH[half][:nt, :, :nt], TTH[half][:nt, :, :nt], eng='act')
            nlev = 0
            while (1 << (nlev + 1)) < nt:
                nlev += 1
            cur = 0
            for lev in range(1, nlev + 1):
                nxt = 1 - cur
                last = (lev == nlev)
                pN = (psA[0], psA[1])
                pY = (psA[2], psA[3])
                pT = (psA[4], psA[5])
                for half in range(2):
                    for j in range(4):
                        kb.mm(pN[half][:nt, j * 128:j * 128 + nt], YbH[cur][half][:nt, j, :nt], NbH[cur][half][:nt, j, :nt], True, True)
                    if not last:
                        for j in range(4):
                            kb.mm(pY[half][:nt, j * 128:j * 128 + nt], NbH[cur][half][:nt, j, :nt], YbH[cur][half][:nt, j, :nt], True, True)
                for half in range(2):
                    kb.cp(NbH[nxt][half][:nt, :, :nt], ps3(pN[half], nt)[:nt, :, :nt], eng='act')
                    if not last:
                        kb.cp(YbH[nxt][half][:nt, :, :nt], ps3(pY[half], nt)[:nt, :, :nt], eng='dve')
                for half in range(2):
                    for j in range(4):
                        kb.mm(pT[half][:nt, j * 128:j * 128 + nt], NbH[nxt][half][:nt, j, :nt], TTbH[half][:nt, j, :nt], True, True)
                for half in range(2):
                    kb.tt(TTH[half][:nt, :, :nt], TTH[half][:nt, :, :nt], ps3(pT[half], nt)[:nt, :, :nt], ALU.add)
                    kb.cp(TTbH[half][:nt, :, :nt], TTH[half][:nt, :, :nt], eng='act')
                cur = nxt
            kb.tt(Ru[:nt], vtm[:nt], beta[:nt].unsq(2).bc([nt, 8, 128]), ALU.mult, eng='dve' if hp else 'pool')
            kb.tt(Rw[:nt], kn[:nt], bexp[:nt].unsq(2).bc([nt, 8, 128]), ALU.mult, eng='dve' if hp else 'pool')
            pu = (psA[0], psA[1])
            pw = (psA[2], psA[3])
            headmm(pu, lambda hh, bank, c: kb.mm(bank[:nt, c:c + 128], TTbH[hh // 4][:nt, hh % 4, :nt], Ru[:nt, hh, :], True, True))
            headmm(pw, lambda hh, bank, c: kb.mm(bank[:, c:c + nt], Rw[:nt, hh, :], TTbH[hh // 4][:nt, hh % 4, :nt], True, True))
            evac2(ub, pu, nt, 128)
            evac2(wT, pw, 128, nt)
            pws = (psA[4], psA[5])
            headmm(pws, lambda hh, bank, c: kb.mm(bank[:nt, c:c + 128], wT[:, hh, :nt], Sb[:, hh, :], True, True))
            for half in range(2):
                kb.tt(vn[:nt, half * 4:half * 4 + 4, :], ub[:nt, half * 4:half * 4 + 4, :], ps3(pws[half], nt)[:nt], ALU.subtract)
            if use_q:
                po = (psA[0], psA[1])

                def omm(hh, bank, c):
                    kb.mm(bank[:nt, c:c + 128], intraT[:nt, hh, :nt], vn[:nt, hh, :], True, False)
                    kb.mm(bank[:nt, c:c + 128], qgT[:, hh, :nt], Sb[:, hh, :], False, True)
                headmm(po, omm)
            kb.tt(kd[:nt], kn[:nt], kdec[:nt].unsq(2).bc([nt, 8, 128]), ALU.mult, eng='pool')
            pds = (psA[2], psA[3])
            headmm(pds, lambda hh, bank, c: kb.mm(bank[:, c:c + 128], kd[:nt, hh, :], vn[:nt, hh, :], True, True))
            kb.tt(Sm, Sm, elast.unsq(2).bc([128, 8, 128]), ALU.mult, eng='pool')
            for half in range(2):
                kb.tt(Sm[:, half * 4:half * 4 + 4, :], Sm[:, half * 4:half * 4 + 4, :], ps3(pds[half], 128), ALU.add)
            kb.cp(Sb, Sm, eng='act')
            if use_q:
                ot = t - q.own0
                c0 = ot * 128
                z = zt[ot % 2]
                kb.dma(z[:nt], q.z_s[c0:c0 + nt, :])
                for half in range(2):
                    kb.act(onf[:nt, half * 4:half * 4 + 4, :], ps3(po[half], nt)[:nt], AF.Square)
                kb.red(st[6][:nt], onf[:nt])
                kb.rstd(st[7][:nt], st[6][:nt], 128.0, st[5][:nt])
                for half in range(2):
                    kb.tt(onf[:nt, half * 4:half * 4 + 4, :], ps3(po[half], nt)[:nt],
                          st[7][:nt, half * 4:half * 4 + 4].unsq(2).bc([nt, 4, 128]), ALU.mult)
                kb.tt(onf[:nt], onf[:nt], g_go_bc[:nt].unsq(1).bc([nt, 8, 128]), ALU.mult, eng='pool')
                kb.tt(ob[:nt], onf[:nt], z[:nt].re("p (h c) -> p h c", h=8), ALU.mult, eng='pool')
                oT_ = obT[ot % 2]
                trans8(oT_, ob, nt, 128, psT[0])
                kb.dma(q.oT_s[8:16].re("h p c -> p h c")[:, :, c0:c0 + nt], oT_[:, :, :nt])

        for q in seqs:
            if q.name not in SEQS_ON:
                continue
            if q.prompt:
                kb.memset(hist3, 0.0)
                kb.memset(Sm, 0.0)
                kb.memset(Sb, 0.0)
            else:
                kb.dma(h3o[:3], q.c_gconv)
                for ct in range(24):
                    kb.tr(psA[ct % 2][:, 0:3], h3o[:3, ct * 128:(ct + 1) * 128], ident_f[:3, :3])
                    kb.cp(hist3[:, ct, :], psA[ct % 2][:, 0:3], eng=evac_eng())
                kb.dma(Sm, q.c_S.re("h k v -> k h v"))
                kb.cp(Sb, Sm, eng='act')
            blocks = [q.xt[i:i + 2] for i in range(0, len(q.xt), 2)]
            own_blk = min(bi for bi, bl in enumerate(blocks) if any(t >= q.own0 for t, _ in bl))
            for bi, bl in enumerate(blocks):
                hb_ = hTb[bi % 2]
                ntok = sum(nt for _, nt in bl)
                for j, (t, nt) in enumerate(bl):
                    kb.dma(hb_[:, :, j * 128:j * 128 + nt],
                           q.hT_s[t - q.ncache].re("p (k c) -> p k c", k=16)[:, :, :nt])
                compute_q = bi >= own_blk - 1
                cts = list(range(8, 24)) + (list(range(8)) if compute_q else [])
                for n_, ct in enumerate(cts):
                    pb = psA[n_ % 2]
                    for kt in range(16):
                        kb.mm(pb[:, :ntok], WG[:, kt, ct * 128:(ct + 1) * 128], hb_[:, kt, :ntok], kt == 0, kt == 15)
                    rw = raw[n_ % 2]
                    ac = acc[n_ % 2]
                    kb.cp(rw[:, 0:3], hist3[:, ct, :], eng='pool')
                    kb.cp(rw[:, 3:3 + ntok], pb[:, :ntok], eng='act')
                    kb.cp(hist3[:, ct, :], rw[:, ntok:ntok + 3], eng='pool')
                    kb.ts(ac[:, :ntok], rw[:, 0:ntok], wconv[:, ct, 0:1], None, ALU.mult)
                    for k_ in range(1, 4):
                        kb.stt(ac[:, :ntok], rw[:, k_:k_ + ntok], wconv[:, ct, k_:k_ + 1], ac[:, :ntok], ALU.mult, ALU.add)
                    kb.act(qkvT[:, ct, :ntok], ac[:, :ntok], AF.Silu)
                for j, (t, nt) in enumerate(bl):
                    gdn_tile(q, t, nt, j, t >= q.own0, hb_)
            kb.dma(q.o_S.re("h k v -> k h v"), Sm)
            for ct in range(24):
                pb = psA[(ct // 4) % 2]
                kb.tr(pb[:3, (ct % 4) * 128:(ct % 4 + 1) * 128], hist3[:, ct, :], ident_f)
                if ct % 4 == 3:
                    kb.cp(h3o[:3, (ct - 3) * 128:(ct + 1) * 128], pb[:3, :], eng=evac_eng())
            kb.dma(q.o_gconv, h3o[:3])

    SCALE = 192.0 ** -0.5
    if 3 in stages:
      with Phase(kb) as p3:
        WUK = kb.sb([128, 4, 2048], BF16, es=p3)
        stg3 = [kb.sb([128, 2048 if not CAST_DMA else 2], F32, es=p3) for _ in range(2)]
        wl3 = WLoader(kb, stg3, ('pool', 'dve'))
        wkv_v = w_kv_up.re("(kt p) c -> p kt c", p=128)
        for kt in range(4):
            wl3.load(WUK[:, kt:kt + 1, :], wkv_v[:, kt:kt + 1, :])
        gk_col = kb.sb([128, 1], F32, es=p3)
        kb.dma(gk_col, g_k_nope.re("(p o) -> p o", o=1))
        ckvT3 = kb.sb([128, 4, NT * 128], BF16, es=p3)
        krT3 = kb.sb([64, NT * 128], BF16, es=p3)
        KT = kb.sb([128, NT * 128], BF16, es=p3)
        Vh = kb.sb([128, NT, 128], BF16, es=p3)
        qn_h = kb.sb([128, NOWN * 128], BF16, es=p3)
        qr_h = kb.sb([64, NOWN * 128], BF16, es=p3)
        kmask_sb = kb.sb([128, NT], F32, es=p3)
        sqb = [kb.sb([128, 512], BF16, es=p3) for _ in range(2)]
        rst = [kb.sb([128, 512], F32, es=p3) for _ in range(2)]
        PT = [kb.sb([128, 512], BF16, es=p3) for _ in range(3)]
        oT3 = [kb.sb([128, 512], BF16, es=p3) for _ in range(2)]
        den = kb.sb([128, 512], F32, es=p3)
        dacc = kb.sb([128, 512], F32, es=p3)
        cnt3 = [0, 0, 0]
        for q in seqs:
            if q.name not in SEQS_ON:
                continue
            nkeys = q.nkeys
            for c in range(4):
                kb.dma(ckvT3[:, c, :nkeys], q.ckvT_s[:, c, :nkeys])
            kb.dma(krT3[:, :nkeys], q.krT_s[:, :nkeys])
            if q.prompt:
                kb.dma(kmask_sb, q.kmask)
            else:
                kb.memset(kmask_sb, 0.0)
            ktiles = [(t, min(128, nkeys - t * 128)) for t in range(q.ntc)]
            if q.prompt:
                groups = [[(q.own0, 128)]] + [[(t, 128) for t in range(a, a + 4)] for a in range(q.own0 + 1, q.ntc, 4)]
            else:
                groups = [[(q.own0, 16)]]
            nq_all = sum(nt for g_ in groups for _, nt in g_)
            for hh in range(8):
                for bi, k0 in enumerate(range(0, nkeys, 512)):
                    kw = min(512, nkeys - k0)
                    pk = psA[bi % 2]
                    for c in range(4):
                        kb.mm(pk[:, :kw], WUK[:, c, hh * 256:hh * 256 + 128], ckvT3[:, c, k0:k0 + kw], c == 0, c == 3)
                    sq_ = sqb[bi % 2]
                    kb.act(sq_[:, :kw], pk[:, :kw], AF.Square)
                    pss = psA[2 + bi % 2]
                    kb.mm(pss[:, :kw], ones_b, sq_[:, :kw], True, True)
                    rs_ = rst[bi % 2]
                    kb.ts(rs_[:, :kw], pss[:, :kw], 1.0 / 128, EPS, ALU.mult, ALU.add)
                    kb.act(rs_[:, :kw], rs_[:, :kw], AF.Sqrt)
                    kb.recip(rs_[:, :kw], rs_[:, :kw])
                    kb.stt(KT[:, k0:k0 + kw], pk[:, :kw], gk_col, rs_[:, :kw], ALU.mult, ALU.mult)
                full = [t for t, nk in ktiles if nk == 128]
                for gi, a in enumerate(range(0, len(full), 4)):
                    grp = full[a:a + 4]
                    pv = psA[4 + gi % 2]
                    for sl, t in enumerate(grp):
                        for c in range(4):
                            kb.mm(pv[:, sl * 128:(sl + 1) * 128], ckvT3[:, c, t * 128:(t + 1) * 128],
                                  WUK[:, c, hh * 256 + 128:hh * 256 + 256], c == 0, c == 3)
                    kb.cp(Vh[:, grp[0]:grp[0] + len(grp), :].re("p t c -> p (t c)"), pv[:, :len(grp) * 128], eng=evac_eng())
                for t, nk in ktiles:
                    if nk < 128:
                        pv = psA[4]
                        for c in range(4):
                            kb.mm(pv[:nk, 0:128], ckvT3[:, c, t * 128:t * 128 + nk],
                                  WUK[:, c, hh * 256 + 128:hh * 256 + 256], c == 0, c == 3)
                        kb.cp(Vh[:nk, t, :], pv[:nk, 0:128], eng=evac_eng())
                kb.dma(qn_h[:, :nq_all], q.qnT_s[:, hh, :nq_all])
                kb.dma(qr_h[:, :nq_all], q.qrT_s[:, hh, :nq_all])
                for G in groups:
                    ntq = sum(nt for _, nt in G)
                    c0 = (G[0][0] - q.own0) * 128
                    vis = [kt for kt in ktiles if (kt[0] <= G[-1][0] or not q.prompt)]
                    pnum, pden = psA[2], psA[3]

                    def geom(vi):
                        kt, nk = vis[vi]
                        off = (max(G[0][0], kt) - G[0][0]) * 128 if q.prompt else 0
                        return kt, nk, off, ntq - off

                    def emit_qk(vi):
                        kt, nk, off, nq = geom(vi)
                        pst = psA[vi % 2]
                        kb.mm(pst[:nk, :nq], KT[:, kt * 128:kt * 128 + nk], qn_h[:, c0 + off:c0 + off + nq], True, False)
                        kb.mm(pst[:nk, :nq], krT3[:, kt * 128:kt * 128 + nk], qr_h[:, c0 + off:c0 + off + nq], False, True)

                    emit_qk(0)
                    for vi in range(len(vis)):
                        kt, nk, off, nq = geom(vi)
                        if vi + 1 < len(vis):
                            emit_qk(vi + 1)
                        pst = psA[vi % 2]
                        pt = PT[cnt3[1] % 3]
                        cnt3[1] += 1
                        kb.act(pt[:nk, :nq], pst[:nk, :nq], AF.Exp, bias=kmask_sb[:nk, kt:kt + 1], scale=SCALE)
                        if q.prompt and kt >= G[0][0]:
                            kb.memset(pt[64:128, 0:64], 0.0, eng='pool')
                        if vi == 0:
                            kb.cp(dacc[:, :ntq], pt[:, :ntq], eng='dve')
                        else:
                            kb.tt(dacc[:nk, off:off + nq], dacc[:nk, off:off + nq], pt[:nk, :nq], ALU.add)
                        kb.mm(pnum[:, off:off + nq], Vh[:nk, kt, :], pt[:nk, :nq], vi == 0, vi == len(vis) - 1)
                    kb.mm(pden[:, :ntq], ones_f, dacc[:, :ntq], True, True)
                    kb.ts(den[:, :ntq], pden[:, :ntq], 1e-30, None, ALU.max)
                    kb.recip(den[:, :ntq], den[:, :ntq])
                    o_ = oT3[cnt3[2] % 2]
                    cnt3[2] += 1
                    kb.tt(o_[:, :ntq], pnum[:, :ntq], den[:, :ntq], ALU.mult)
                    kb.dma(q.oT_s[hh, :, c0:c0 + ntq], o_[:, :ntq])

    if 4 in stages:
      cols0 = {}
      tot = 0
      for q in seqs:
          cols0[q.name] = tot
          tot += sum(nt for t, nt in q.xt if t >= q.own0)
      TOT = tot
      hid_s = scr("hid_s", [44, 128, TOT], BF16)
      with Phase(kb) as p4:
        h2T = kb.sb([128, 16, TOT], BF16, es=p4)
        with Phase(kb) as p4a:
            WO = kb.sb([128, 16, 2048], BF16, es=p4a)
            stg4 = [kb.sb([128, 2048 if not CAST_DMA else 2], F32, es=p4a) for _ in range(2)]
            wl4 = WLoader(kb, stg4, ('pool', 'dve', 'act'))
            wo_v = w_out.re("(kt p) c -> p kt c", p=128)
            WOd = [WO.sub(dc) for dc in range(4)]
            for dc in range(4):
                for k4 in range(4):
                    wl4.load(WOd[dc][:, 4 * k4:4 * k4 + 4, dc * 512:(dc + 1) * 512], wo_v[:, 4 * k4:4 * k4 + 4, dc * 512:(dc + 1) * 512])
                wl4.engs = ('pool',)
            g_ffn_bc = kb.sb([128, D], F32, es=p4a)
            load_bc(kb, g_ffn_bc, g_ffn)
            oTt = [kb.sb([128, 16, 128], BF16, es=p4a) for _ in range(2)]
            xt4 = [kb.sb([128, D], F32, es=p4a) for _ in range(2)]
            xn = [kb.sb([128, D], F32, es=p4a) for _ in range(2)]
            junk4 = kb.sb([128, D], BF16, es=p4a)
            h2 = [kb.sb([128, D], BF16, es=p4a) for _ in range(2)]
            sm4 = kb.sb([128, 4], F32, es=p4a)
            n4 = 0
            tl4 = []
            for q in seqs:
                if q.name not in SEQS_ON:
                    continue
                for (t, nt) in q.xt:
                    if t >= q.own0:
                        tl4.append((q, t, nt))

            def p4_head(i):
                q, t, nt = tl4[i]
                b = i % 2
                ot = t - q.own0
                c0 = ot * 128
                xr0 = (t - q.ncache) * 128
                kb.dma(oTt[b][:, :, :nt], q.oT_s[:, :, c0:c0 + nt].re("k p c -> p k c"), q='pool')
                kb.dma(xt4[b][:nt], q.x[xr0:xr0 + nt, :], q='pool')
                for dc in range(4):
                    pb = psA[dc]
                    for kt in range(16):
                        kb.mm(pb[:nt], oTt[b][:, kt, :nt], WOd[dc][:, kt, dc * 512:(dc + 1) * 512], kt == 0, kt == 15)
                    kb.tt(xn[b][:nt, dc * 512:(dc + 1) * 512], pb[:nt], xt4[b][:nt, dc * 512:(dc + 1) * 512], ALU.add)
                kb.dma(q.xn_s[c0:c0 + nt, :], xn[b][:nt])
                ss, tmp, rs = sm4[:nt, 0:1], sm4[:nt, 1:2], sm4[:nt, 2:3]
                kb.act(junk4[:nt], xn[b][:nt], AF.Square, accum=ss)
                kb.rstd(rs, ss, D, tmp)
                kb.stt(h2[b][:nt], xn[b][:nt], rs, g_ffn_bc[:nt], ALU.mult, ALU.mult)

            def p4_tail(i):
                q, t, nt = tl4[i]
                b = i % 2
                col = cols0[q.name] + (t - q.own0) * 128
                for half in range(2):
                    for k in range(8):
                        kt = half * 8 + k
                        kb.tr(psT[half][:, k * 128:k * 128 + nt], h2[b][:nt, kt * 128:(kt + 1) * 128], ident[:nt, :nt])
                    kb.cp(h2T[:, half * 8:half * 8 + 8, col:col + nt],
                          psT[half].re("p (k c) -> p k c", k=8)[:, :, :nt], eng='act' if half == 0 else 'dve')

            for i in range(len(tl4)):
                p4_head(i)
                if i > 0:
                    p4_tail(i - 1)
            p4_tail(len(tl4) - 1)
        with Phase(kb) as p4b:
            Wg = [kb.sb([128, 16, 512], BF16, es=p4b) for _ in range(2)]
            Wu = [kb.sb([128, 16, 512], BF16, es=p4b) for _ in range(2)]
            cw4 = [kb.sb([4, 512], F32, es=p4b) for _ in range(2)]
            taps = [kb.sb([128, 4], F32, es=p4b) for _ in range(2)]
            cf2 = {q.name: kb.sb([2, 512], F32, es=p4b) for q in seqs if not q.prompt}
            hist2 = kb.sb([128, 2], F32, es=p4b)
            graw = [kb.sb([128, 516], F32, es=p4b) for _ in range(2)]
            gcv = [kb.sb([128, 512], F32, es=p4b) for _ in range(2)]
            sg = [kb.sb([128, 512], F32, es=p4b) for _ in range(2)]
            hid = [kb.sb([128, 512], BF16, es=p4b) for _ in range(3)]
            fout = {q.name: kb.sb([128, 4, 2], F32, es=p4b) for q in seqs}
            fo2 = [kb.sb([2, 512], F32, es=p4b) for _ in range(2)]
            wgv = w_gate.re("(kt p) c -> p kt c", p=128)
            wuv = w_up.re("(kt p) c -> p kt c", p=128)
            n5 = 0
            stg5 = [kb.sb([128, 4096 if not CAST_DMA else 2], F32, es=p4b) for _ in range(2)]
            wl5 = WLoader(kb, stg5, ('pool',))
            def load_fc(fc):
                wb = fc % 2
                for k8 in range(2):
                    wl5.load(Wg[wb][:, 8 * k8:8 * k8 + 8, :], wgv[:, 8 * k8:8 * k8 + 8, fc * 512:(fc + 1) * 512])
                    wl5.load(Wu[wb][:, 8 * k8:8 * k8 + 8, :], wuv[:, 8 * k8:8 * k8 + 8, fc * 512:(fc + 1) * 512])
                kb.dma(cw4[wb][0:3, :], w_fconv[:, fc * 512:(fc + 1) * 512], q='pool')
                kb.dma(cw4[wb][3:4, :], b_fconv.re("(o c) -> o c", o=1)[:, fc * 512:(fc + 1) * 512], q='pool')

            load_fc(0)
            for fc in range(11):
                wb = fc % 2
                if fc + 1 < 11:
                    load_fc(fc + 1)
                for q in seqs:
                    if not q.prompt and q.name in SEQS_ON:
                        pass
                for ffi in range(4):
                    ff = fc * 4 + ffi
                    tp = taps[ff % 2]
                    kb.tr(psA[5][:, 0:4], cw4[wb][:4, ffi * 128:(ffi + 1) * 128], ident_f[:4, :4])
                    kb.cp(tp, psA[5][:, 0:4], eng='dve')
                    for q in seqs:
                        if q.name not in SEQS_ON:
                            continue
                        if q.prompt:
                            blocks = [(0, 128, True)] + [(128 + 512 * i, 512, False) for i in range((q.nown - 1) // 4)]
                            kb.memset(hist2, 0.0, eng='pool')
                        else:
                            blocks = [(0, 16, False)]
                            if ffi == 0:
                                kb.dma(cf2[q.name], q.c_fconv[:, fc * 512:(fc + 1) * 512])
                                q.cf2cur = cf2[q.name]
                            kb.tr(psA[5][:, 8:10], q.cf2cur[:2, ffi * 128:(ffi + 1) * 128], ident_f[:2, :2])
                            kb.cp(hist2, psA[5][:, 8:10], eng='dve')
                        for (cb, ntok, halo) in blocks:
                            col = cols0[q.name] + cb
                            n5 += 1
                            pg = psA[n5 % 2]
                            pu_ = psA[2 + n5 % 2]
                            for kt in range(16):
                                kb.mm(pg[:, :ntok], Wg[wb][:, kt, ffi * 128:(ffi + 1) * 128], h2T[:, kt, col:col + ntok], kt == 0, kt == 15)
                            if not halo:
                                for kt in range(16):
                                    kb.mm(pu_[:, :ntok], Wu[wb][:, kt, ffi * 128:(ffi + 1) * 128], h2T[:, kt, col:col + ntok], kt == 0, kt == 15)
                            gr = graw[n5 % 2]
                            kb.cp(gr[:, 0:2], hist2, eng='pool')
                            kb.cp(gr[:, 2:2 + ntok], pg[:, :ntok], eng='act')
                            kb.cp(hist2, gr[:, ntok:ntok + 2], eng='pool')
                            if halo:
                                continue
                            gv = gcv[n5 % 2]
                            kb.ts(gv[:, :ntok], gr[:, 0:ntok], tp[:, 0:1], None, ALU.mult)
                            kb.stt(gv[:, :ntok], gr[:, 1:1 + ntok], tp[:, 1:2], gv[:, :ntok], ALU.mult, ALU.add)
                            kb.stt(gv[:, :ntok], gr[:, 2:2 + ntok], tp[:, 2:3], gv[:, :ntok], ALU.mult, ALU.add)
                            sg_ = sg[n5 % 2]
                            kb.act(sg_[:, :ntok], gv[:, :ntok], AF.Silu, bias=tp[:, 3:4])
                            hd = hid[n5 % 3]
                            kb.tt(hd[:, :ntok], sg_[:, :ntok], pu_[:, :ntok], ALU.mult)
                            kb.dma(hid_s[ff, :, col:col + ntok], hd[:, :ntok])
                        kb.cp(fout[q.name][:, ffi, :], hist2, eng='pool')
                for q in seqs:
                    if q.name not in SEQS_ON:
                        continue
                    n5 += 1
                    pb = psA[4]
                    for ffi in range(4):
                        kb.tr(pb[:2, ffi * 128:(ffi + 1) * 128], fout[q.name][:, ffi, :], ident_f)
                    kb.cp(fo2[n5 % 2], pb[:2, :], eng='dve')
                    kb.dma(q.o_fconv[:, fc * 512:(fc + 1) * 512], fo2[n5 % 2])
      with Phase(kb) as p4c:
        Wd = [kb.sb([128, 44, 512], BF16, es=p4c) for _ in range(2)]
        hidt = [kb.sb([128, 44, 512], BF16, es=p4c) for _ in range(2)]
        n7 = [0]
        xnt = [kb.sb([128, 512], F32, es=p4c) for _ in range(2)]
        yt = [kb.sb([128, 512], F32, es=p4c) for _ in range(2)]
        wdv = w_down.re("(kt p) c -> p kt c", p=128)
        n6 = 0
        stg6 = [kb.sb([128, 4096 if not CAST_DMA else 2], F32, es=p4c) for _ in range(2)]
        wl6 = WLoader(kb, stg6, ('pool',))
        def load_dc(dc):
            wb = dc % 2
            for k8 in range(0, 44, 8):
                k9 = min(44, k8 + 8)
                wl6.load(Wd[wb][:, k8:k9, :], wdv[:, k8:k9, dc * 512:(dc + 1) * 512])

        load_dc(0)
        for dc in range(4):
            wb = dc % 2
            if dc + 1 < 4:
                load_dc(dc + 1)
            for q in seqs:
                if q.name not in SEQS_ON:
                    continue
                tl = [(t, nt) for (t, nt) in q.xt if t >= q.own0 and not (q.halo and t == q.own0)]
                for bi in range(0, len(tl), 4):
                    blk = tl[bi:bi + 4]
                    nb_ = sum(nt for _, nt in blk)
                    n6 += 1
                    hb6 = hidt[n6 % 2]
                    colb = cols0[q.name] + (blk[0][0] - q.own0) * 128
                    kb.dma(hb6[:, :, :nb_], hid_s[:, :, colb:colb + nb_].re("k p c -> p k c"), q='pool')
                    for j, (t, nt) in enumerate(blk):
                        n7[0] += 1
                        b = n7[0] % 2
                        ot = t - q.own0
                        c0 = ot * 128
                        yr0 = (ot - 1) * 128 if q.halo else 0
                        kb.dma(xnt[b][:nt], q.xn_s[c0:c0 + nt, dc * 512:(dc + 1) * 512], q='pool')
                        pb = psA[n7[0] % 4]
                        for kt in range(44):
                            kb.mm(pb[:nt], hb6[:, kt, j * 128:j * 128 + nt], Wd[wb][:, kt, :], kt == 0, kt == 43)
                        kb.tt(yt[b][:nt], pb[:nt], xnt[b][:nt], ALU.add)
                        kb.dma(q.o_y[yr0:yr0 + nt, dc * 512:(dc + 1) * 512], yt[b][:nt])

    return nc, kb, es, seqs, I, locals()


SEQS_ON = ('p', 's0', 's1')


def rope_tables(pos):
    half = 32
    inv = (1.0 / (10000.0 ** (np.arange(half, dtype=np.float32) / np.float32(half)))).astype(np.float32)
    ang = pos.astype(np.float32)[:, None] * inv[None, :]
    c = np.cos(ang).astype(np.float32)
    s = np.sin(ang).astype(np.float32)
    return np.concatenate([c, c], 1), np.concatenate([-s, s], 1)


def prep_inputs(inp):
    maps = []
    L = NT * 128
    idx = np.arange(128)
    tri = np.stack([
        (idx[:, None] <= idx[None, :]),
        (idx[:, None] > idx[None, :]),
        (idx[:, None] >= idx[None, :]),
        (idx[:, None] > idx[None, :]),
    ]).astype(np.float32)
    wnames = ['g_attn_norm', 'w_in', 'g_q_lat', 'g_kv_lat', 'w_q_up', 'w_kv_up', 'g_q_nope', 'g_q_rope', 'g_k_nope',
              'g_k_rope', 'w_gdn_conv', 'a_log', 'dt_bias', 'g_gdn_out', 'w_out', 'g_ffn_norm', 'w_ffn_gate',
              'w_ffn_up', 'w_ffn_conv', 'b_ffn_conv', 'w_ffn_down']
    W = {k: np.ascontiguousarray(inp[k][0]) for k in wnames}
    W['w_kv_up'] = W['w_kv_up'].reshape(512, 8 * 256)
    cs_s, sn_s = rope_tables(PAST + np.arange(16))
    for c in range(8):
        b, q = c // 4, c % 4
        nreal = 2048 * (q + 1)
        pad = L - nreal
        m = dict(W)
        xcv = np.zeros((L, D), np.float32)
        xcv[pad:] = inp['x_prompt'][b, :nreal]
        m['p_x'] = xcv
        pos = np.maximum(np.arange(L) - pad, 0)
        cs, sn = rope_tables(pos)
        m['p_cs'] = cs
        m['p_sn'] = sn
        km = np.where(np.arange(L) >= pad, 0.0, NEG).astype(np.float32)
        m['p_kmask'] = np.ascontiguousarray(km.reshape(NT, 128).T)
        m['ident'] = np.eye(128, dtype=np.float32)
        m['tri'] = tri
        for j in range(2):
            sb_ = 2 * c + j
            nm = 's%d' % j
            m[nm + '_x'] = np.ascontiguousarray(inp['x_sample'][sb_])
            m[nm + '_cs'] = cs_s
            m[nm + '_sn'] = sn_s
            m[nm + '_clat'] = np.ascontiguousarray(inp['cache_mla_latent'][0, sb_])
            m[nm + '_ckr'] = np.ascontiguousarray(inp['cache_mla_krope'][0, sb_])
            m[nm + '_cgconv'] = np.ascontiguousarray(inp['state_gdn_conv'][0, sb_])
            m[nm + '_cS'] = np.ascontiguousarray(inp['state_gdn_S'][0, sb_])
            m[nm + '_cfconv'] = np.ascontiguousarray(inp['state_ffn_conv'][0, sb_])
        maps.append(m)
    return maps


_CACHE = {}


def kernel(**inputs):
    inp = {k: np.asarray(v) for k, v in inputs.items()}
    if 'prog' not in _CACHE:
        r_ = build_program(stages=(1, 2, 3, 4))
        nc, kb, es, seqs, I = r_[:5]
        kb.S.finish()
        es.close()
        _CACHE['prog'] = (nc, I)
    nc, I = _CACHE['prog']
    maps = prep_inputs(inp)
    maps = [{k: v for k, v in m.items() if k in I} for m in maps]
    res = run_bass_kernel_spmd(nc, maps, core_ids=list(range(8)))
    r = res.results
    f32 = np.float32
    y_p = np.zeros((2, 8192, D), f32)
    y_s = np.zeros((16, 16, D), f32)
    p_lat = np.zeros((1, 2, 8192, 512), f32)
    p_kr = np.zeros((1, 2, 8192, 64), f32)
    p_gc = np.zeros((1, 2, 3, 3072), f32)
    p_S = np.zeros((1, 2, 8, 128, 128), f32)
    p_fc = np.zeros((1, 2, 2, DFF), f32)
    s_lat = np.zeros((1, 16, 16, 512), f32)
    s_kr = np.zeros((1, 16, 16, 64), f32)
    s_gc = np.zeros((1, 16, 3, 3072), f32)
    s_S = np.zeros((1, 16, 8, 128, 128), f32)
    s_fc = np.zeros((1, 16, 2, DFF), f32)
    own_r0 = (NT - 16) * 128
    for c in range(8):
        b, q = c // 4, c % 4
        rc = r[c]
        sl = slice(q * 2048, (q + 1) * 2048)
        y_p[b, sl] = np.asarray(rc['p_oy'])
        p_lat[0, b, sl] = np.asarray(rc['p_olat'])[own_r0:]
        p_kr[0, b, sl] = np.asarray(rc['p_okr'])[own_r0:]
        if q == 3:
            p_gc[0, b] = np.asarray(rc['p_ogconv'])
            p_S[0, b] = np.asarray(rc['p_oS'])
            p_fc[0, b] = np.asarray(rc['p_ofconv'])
        for j in range(2):
            sb_ = 2 * c + j
            nm = 's%d' % j
            y_s[sb_] = np.asarray(rc[nm + '_oy'])
            s_lat[0, sb_] = np.asarray(rc[nm + '_olat'])
            s_kr[0, sb_] = np.asarray(rc[nm + '_okr'])
            s_gc[0, sb_] = np.asarray(rc[nm + '_ogconv'])
            s_S[0, sb_] = np.asarray(rc[nm + '_oS'])
            s_fc[0, sb_] = np.asarray(rc[nm + '_ofconv'])
    return (y_p, y_s, p_lat, p_kr, p_gc, p_S, p_fc, s_lat, s_kr, s_gc, s_S, s_fc)
```

```python
import numpy as np
from contextlib import ExitStack
import concourse.bass as bass
import concourse.mybir as mybir
from concourse.bass_utils import run_bass_kernel_spmd

F32 = mybir.dt.float32
F32R = mybir.dt.float32r
BF16 = mybir.dt.bfloat16
AF = mybir.ActivationFunctionType
ALU = mybir.AluOpType
AX = mybir.AxisListType

D = 2048
NT = 64
OWN0 = 47
NOWN = NT - OWN0
EPS = 1e-6
H = 8
DIN = 5200
DFF = 5632
PAST = 4096
NEG = -30000.0


class V:
    def __init__(self, ap, h):
        self.ap = ap
        self.h = h

    def __getitem__(self, idx):
        return V(self.ap[idx], self.h)

    def sub(self, key):
        return V(self.ap, (self.h, key))

    def re(self, s, **kw):
        return V(self.ap.rearrange(s, **kw), self.h)

    def bc(self, shape):
        return V(self.ap.to_broadcast(list(shape)), self.h)

    def unsq(self, d):
        return V(self.ap.unsqueeze(d), self.h)

    def bitc(self, dt):
        return V(self.ap.bitcast(dt), self.h)

    def all2(self):
        return V(self.ap, ('__multi__', ((self.h, 0), (self.h, 1))))


class Sched:
    EPOCH = 16000
    R = 6

    def __init__(self, nc, es):
        self.nc = nc
        self.es = es
        self.engs = {'pe': nc.tensor, 'act': nc.scalar, 'dve': nc.vector, 'pool': nc.gpsimd, 'sp': nc.sync}
        self.cnt = {e: 0 for e in self.engs}
        self.sems = {}
        self.known = {e: {} for e in self.engs}
        self.lastw = {}
        self.readers = {}
        self.dma_cnt = {'sp': 0, 'pool': 0, 'act': 0}
        self.ninst = 0
        self.nwait = 0
        self.uid = 0
        self.rec = None

    def sem(self, name):
        if name not in self.sems:
            self.sems[name] = self.es.enter_context(self.nc.semaphore(name))
        return self.sems[name]

    def _wait(self, e, ev):
        name, val, src = ev
        if src == 'pe' and e == 'pe':
            return
        k = self.known[e]
        if k.get(name, 0) >= val:
            return
        self.engs[e].wait_ge(self.sem(name), val)
        self.nwait += 1
        k[name] = val

    def _deps(self, reads, writes):
        evs = []
        for h in reads:
            w = self.lastw.get(h)
            if w is not None:
                evs.append(w)
        for h in writes:
            w = self.lastw.get(h)
            if w is not None:
                evs.append(w)
            evs.extend(self.readers.get(h, {}).values())
        return evs

    def _record(self, ev, reads, writes):
        for h in writes:
            self.lastw[h] = ev
            self.readers[h] = {}
        for h in reads:
            if h in writes:
                continue
            self.readers.setdefault(h, {})[ev[2] + ev[0]] = ev

    @staticmethod
    def _flat(lst):
        out = []
        for r in lst:
            h = r.h if isinstance(r, V) else r
            if isinstance(h, tuple) and len(h) == 2 and h[0] == '__multi__':
                out.extend(h[1])
            else:
                out.append(h)
        return out

    def op(self, e, fn, reads, writes):
        if self.rec is not None:
            self.rec.append(('op', (e, fn, reads, writes)))
            return
        reads = self._flat(reads)
        writes = self._flat(writes)
        for ev in self._deps(reads, writes):
            self._wait(e, ev)
        ins = fn(self.engs[e])
        n = self.cnt[e]
        name = "%s_%d" % (e, n // self.EPOCH)
        val = n % self.EPOCH + 1
        ins.then_inc(self.sem(name), 1)
        self.cnt[e] = n + 1
        self.ninst += 1
        self._record((name, val, e), reads, writes)

    def record(self):
        self.rec = []

    def stop_record(self):
        r, self.rec = self.rec, None
        return r

    def replay(self, items):
        for kind, args in items:
            if kind == 'op':
                self.op(*args)
            else:
                q, out, in_, kw = args
                self.dma(q, out, in_, **kw)

    def dma(self, q, out, in_, **kw):
        if self.rec is not None:
            self.rec.append(('dma', (q, out, in_, kw)))
            return
        i = self.dma_cnt[q]
        name = "dq_%s_%d" % (q, i % self.R)
        prev = 16 * (i // self.R)
        if prev > 0:
            self._wait(q, (name, prev, 'dma'))
        reads = self._flat([in_])
        writes = self._flat([out])
        for ev in self._deps(reads, writes):
            self._wait(q, ev)
        self.engs[q].dma_start(out=out.ap, in_=in_.ap, **kw).then_inc(self.sem(name), 16)
        self.dma_cnt[q] = i + 1
        self.ninst += 1
        self._record((name, prev + 16, 'dma'), reads, writes)

    def barrier(self):
        evs = []
        for e, n in self.cnt.items():
            if n > 0:
                m = n - 1
                evs.append(("%s_%d" % (e, m // self.EPOCH), m % self.EPOCH + 1, 'x'))
        for q, i in self.dma_cnt.items():
            for r in range(self.R):
                cntr = (i - r + self.R - 1) // self.R
                if cntr > 0:
                    evs.append(("dq_%s_%d" % (q, r), 16 * cntr, 'dma'))
        for e in self.engs:
            for ev in evs:
                self._wait(e, ev)

    def finish(self):
        for q, i in self.dma_cnt.items():
            for r in range(min(i, self.R)):
                last = (i - 1 - r) // self.R + 1 if i - 1 >= r else 0
                cntr = (i - r + self.R - 1) // self.R
                if cntr > 0:
                    self._wait('sp', ("dq_%s_%d" % (q, r), 16 * cntr, 'dma'))


class KB:
    def __init__(self, nc, es):
        self.nc = nc
        self.es = es
        self.S = Sched(nc, es)
        self.n = 0

    def name(self, p):
        self.n += 1
        return "%s%d" % (p, self.n)

    def sb(self, shape, dt, es=None, name=None):
        nm = name or self.name("sb")
        t = (es or self.es).enter_context(self.nc.sbuf_tensor(nm, list(shape), dt))
        return V(t[:], nm)

    def ps(self, shape, dt, es=None, name=None):
        nm = name or self.name("ps")
        t = (es or self.es).enter_context(self.nc.psum_tensor(nm, list(shape), dt))
        return V(t[:], nm)

    def dram(self, name, shape, dt, kind="Internal"):
        t = self.nc.dram_tensor(name, list(shape), dt, kind=kind)
        return V(t.ap(), name)

    def mm(self, out, lhsT, rhs, start, stop):
        self.S.op('pe', lambda e: e.matmul(out.ap, lhsT.ap, rhs.ap, start=start, stop=stop),
                  [lhsT, rhs] + ([] if start else [out]), [out])

    def tr(self, out, in_, ident):
        self.S.op('pe', lambda e: e.transpose(out.ap, in_.ap, ident.ap), [in_, ident], [out])

    def act(self, out, in_, func, bias=None, scale=None, accum=None, eng='act'):
        kw = {}
        rd = [in_]
        if bias is not None:
            kw['bias'] = bias.ap if isinstance(bias, V) else bias
            if isinstance(bias, V):
                rd.append(bias)
        if scale is not None:
            kw['scale'] = scale.ap if isinstance(scale, V) else scale
            if isinstance(scale, V):
                rd.append(scale)
        wr = [out]
        if accum is not None:
            kw['accum_out'] = accum.ap
            wr.append(accum)
        self.S.op('act', lambda e: e.activation(out.ap, in_.ap, func, **kw), rd, wr)

    def tt(self, out, a, b, op, eng='dve'):
        self.S.op(eng, lambda e: e.tensor_tensor(out.ap, a.ap, b.ap, op), [a, b], [out])

    def ts(self, out, a, s1, s2, op0, op1=None, eng='dve'):
        rd = [a]
        if isinstance(s1, V):
            rd.append(s1)
        if isinstance(s2, V):
            rd.append(s2)
        v1 = s1.ap if isinstance(s1, V) else s1
        v2 = s2.ap if isinstance(s2, V) else s2
        if op1 is None:
            self.S.op(eng, lambda e: e.tensor_scalar(out.ap, a.ap, v1, None, op0), rd, [out])
        else:
            self.S.op(eng, lambda e: e.tensor_scalar(out.ap, a.ap, v1, v2, op0, op1), rd, [out])

    def stt(self, out, a, s, b, op0, op1):
        rd = [a, b]
        if isinstance(s, V):
            rd.append(s)
        sv = s.ap if isinstance(s, V) else s
        self.S.op('dve', lambda e: e.scalar_tensor_tensor(out.ap, a.ap, sv, b.ap, op0, op1), rd, [out])

    def cp(self, out, in_, eng='dve'):
        if eng == 'act':
            self.S.op('act', lambda e: e.activation(out.ap, in_.ap, AF.Copy), [in_], [out])
        else:
            self.S.op(eng, lambda e: e.tensor_copy(out.ap, in_.ap), [in_], [out])

    def recip(self, out, in_):
        self.S.op('dve', lambda e: e.reciprocal(out.ap, in_.ap), [in_], [out])

    def red(self, out, in_, op=ALU.add, axis=AX.X):
        self.S.op('dve', lambda e: e.tensor_reduce(out.ap, in_.ap, axis, op), [in_], [out])

    def memset(self, out, val, eng='dve'):
        self.S.op(eng, lambda e: e.memset(out.ap, val), [], [out])

    def dma(self, out, in_, q='sp', **kw):
        self.S.dma(q, out, in_, **kw)

    def rstd(self, out, ss, n, tmp):
        self.act(tmp, ss, AF.Ln, bias=EPS, scale=1.0 / n)
        self.act(out, tmp, AF.Exp, scale=-0.5)


class Seq:
    pass


class Phase(ExitStack):
    def __init__(self, kb):
        super().__init__()
        self.kb = kb

    def __exit__(self, *a):
        self.kb.S.barrier()
        return super().__exit__(*a)


def load_bc(kb, dst, src, q='sp'):
    parts = dst.ap.shape[0]
    kb.dma(dst, V(src.ap.partition_broadcast(parts), src.h), q=q)


WQ_ = 'pool'
CAST_DMA = True


class WLoader:
    def __init__(self, kb, stg, engs=('pool',)):
        self.kb, self.stg, self.engs, self.n = kb, stg, engs, 0

    def load(self, dst, src):
        a, c = dst.ap.shape[1], dst.ap.shape[2]
        if CAST_DMA:
            self.kb.dma(dst, src, q='pool')
            self.n += 1
            return
        st = self.stg[self.n % len(self.stg)]
        sv = st[:, 0:a * c].re("p (a c) -> p a c", a=a)
        self.kb.dma(sv, src, q=WQ_)
        self.kb.cp(dst, sv, eng=self.engs[self.n % len(self.engs)])
        self.n += 1


def build_program(stages=(1, 2, 3, 4), debug=False):
    nc = bass.Bass("TRN2", target_bir_lowering=False)
    es = ExitStack()
    kb = KB(nc, es)
    I = {}

    def inp(name, shape, dt=F32):
        I[name] = kb.dram(name, shape, dt, kind="ExternalInput")
        return I[name]

    def outp(name, shape, dt=F32):
        I[name] = kb.dram(name, shape, dt, kind="ExternalOutput")
        return I[name]

    def scr(name, shape, dt):
        if debug:
            return outp(name, shape, dt)
        return kb.dram(name, shape, dt)

    ident_in = inp("ident", [128, 128])
    tri_in = inp("tri", [4, 128, 128])
    g_attn = inp("g_attn_norm", [D])
    w_in = inp("w_in", [D, DIN])
    g_q_lat = inp("g_q_lat", [512])
    g_kv_lat = inp("g_kv_lat", [512])
    w_q_up = inp("w_q_up", [512, 1536])
    w_kv_up = inp("w_kv_up", [512, 8 * 256])
    g_q_nope = inp("g_q_nope", [128])
    g_q_rope = inp("g_q_rope", [64])
    g_k_nope = inp("g_k_nope", [128])
    g_k_rope = inp("g_k_rope", [64])
    w_gdn_conv = inp("w_gdn_conv", [4, 3072])
    a_log = inp("a_log", [8])
    dt_bias = inp("dt_bias", [8])
    g_gdn_out = inp("g_gdn_out", [128])
    w_out = inp("w_out", [D, D])
    g_ffn = inp("g_ffn_norm", [D])
    w_gate = inp("w_ffn_gate", [D, DFF])
    w_up = inp("w_ffn_up", [D, DFF])
    w_fconv = inp("w_ffn_conv", [3, DFF])
    b_fconv = inp("b_ffn_conv", [DFF])
    w_down = inp("w_ffn_down", [DFF, D])

    seqs = []
    for si, nm in enumerate(['p', 's0', 's1']):
        q = Seq()
        q.name = nm
        q.prompt = (si == 0)
        if q.prompt:
            q.ntc, q.ncache, q.own0 = NT, 0, OWN0
            q.xt = [(t, 128) for t in range(NT)]
            q.ntok = NT * 128
            q.halo = True
        else:
            q.ntc, q.ncache, q.own0 = 33, 32, 32
            q.xt = [(32, 16)]
            q.ntok = 16
            q.halo = False
        q.nown = q.ntc - q.own0
        q.nkeys = q.ncache * 128 + q.ntok
        q.x = inp(nm + "_x", [q.ntok, D])
        q.cs = inp(nm + "_cs", [q.ntok, 64])
        q.sn = inp(nm + "_sn", [q.ntok, 64])
        if q.prompt:
            q.kmask = inp(nm + "_kmask", [128, NT])
        else:
            q.c_lat = inp(nm + "_clat", [PAST, 512])
            q.c_kr = inp(nm + "_ckr", [PAST, 64])
            q.c_gconv = inp(nm + "_cgconv", [3, 3072])
            q.c_S = inp(nm + "_cS", [8, 128, 128])
            q.c_fconv = inp(nm + "_cfconv", [2, DFF])
        q.o_lat = outp(nm + "_olat", [q.ntok, 512])
        q.o_kr = outp(nm + "_okr", [q.ntok, 64])
        q.o_gconv = outp(nm + "_ogconv", [3, 3072])
        q.o_S = outp(nm + "_oS", [8, 128, 128])
        q.o_fconv = outp(nm + "_ofconv", [2, DFF])
        q.ny = (q.nown - 1) * 128 if q.prompt else 16
        q.o_y = outp(nm + "_oy", [q.ny, D])
        nx = len(q.xt)
        q.hT_s = scr(nm + "_hT", [nx, 128, 16 * 128], BF16)
        q.ckvT_s = scr(nm + "_ckvT", [128, 4, q.ntc * 128], BF16)
        q.krT_s = scr(nm + "_krT", [64, q.ntc * 128], BF16)
        q.gb_s = scr(nm + "_gb", [nx * 128, 16], F32)
        q.qnT_s = scr(nm + "_qnT", [128, 8, q.nown * 128], BF16)
        q.qrT_s = scr(nm + "_qrT", [64, 8, q.nown * 128], BF16)
        q.z_s = scr(nm + "_z", [q.nown * 128, 1024], F32)
        q.oT_s = scr(nm + "_oT", [16, 128, q.nown * 128], BF16)
        q.xn_s = scr(nm + "_xn", [q.nown * 128, D], F32)
        seqs.append(q)

    ident_f = kb.sb([128, 128], F32)
    ident = kb.sb([128, 128], BF16)
    kb.dma(ident_f, ident_in)
    kb.cp(ident, ident_f)
    ones_f = kb.sb([128, 128], F32)
    kb.memset(ones_f, 1.0)
    ones_b = kb.sb([128, 128], BF16)
    kb.memset(ones_b, 1.0)

    psA = [kb.ps([128, 512], F32) for _ in range(6)]
    psT = [kb.ps([128, 1024], BF16) for _ in range(2)]
    w_in_v = w_in.re("(kt p) c -> p kt c", p=128)
    rr = [0]

    def evac_eng():
        rr[0] += 1
        return 'act' if rr[0] % 2 else 'dve'

    if 1 in stages:
      with Phase(kb) as p1:
        WA = kb.sb([128, 16, 2128], BF16, es=p1)
        WQ = kb.sb([128, 4, 1536], BF16, es=p1)
        stg1 = [kb.sb([128, 2048 if not CAST_DMA else 2], F32, es=p1) for _ in range(2)]
        wl = WLoader(kb, stg1, ('pool', 'dve', 'act'))
        WAkv, WAown = WA.sub('kv'), WA.sub('own')
        for k2 in range(8):
            wl.load(WAkv[:, 2 * k2:2 * k2 + 2, 512:1088], w_in_v[:, 2 * k2:2 * k2 + 2, 512:1088])
        for k8 in range(2):
            wl.load(WAkv[:, 8 * k8:8 * k8 + 8, 2112:2128], w_in_v[:, 8 * k8:8 * k8 + 8, 5184:5200])
        wl.engs = ('pool',)
        for k4 in range(4):
            wl.load(WAown[:, 4 * k4:4 * k4 + 4, 0:512], w_in_v[:, 4 * k4:4 * k4 + 4, 0:512])
        for k2 in range(8):
            wl.load(WAown[:, 2 * k2:2 * k2 + 2, 1088:2112], w_in_v[:, 2 * k2:2 * k2 + 2, 4160:5184])
        wqv = w_q_up.re("(kt p) c -> p kt c", p=128)
        for k1 in range(4):
            wl.load(WQ[:, k1:k1 + 1, :], wqv[:, k1:k1 + 1, :])
        g_attn_bc = kb.sb([128, D], F32, es=p1)
        load_bc(kb, g_attn_bc, g_attn)
        g_kv_bc = kb.sb([128, 512], F32, es=p1)
        load_bc(kb, g_kv_bc, g_kv_lat)
        g_ql_bc = kb.sb([128, 512], F32, es=p1)
        load_bc(kb, g_ql_bc, g_q_lat)
        g_kr_bc = kb.sb([128, 64], F32, es=p1)
        load_bc(kb, g_kr_bc, g_k_rope)
        g_qn_bc = kb.sb([128, 128], F32, es=p1)
        load_bc(kb, g_qn_bc, g_q_nope)
        g_qr_bc = kb.sb([128, 64], F32, es=p1)
        load_bc(kb, g_qr_bc, g_q_rope)
        alog_bc = kb.sb([128, 8], F32, es=p1)
        load_bc(kb, alog_bc, a_log)
        dtb_bc = kb.sb([128, 8], F32, es=p1)
        load_bc(kb, dtb_bc, dt_bias)
        negA = kb.sb([128, 8], F32, es=p1)
        kb.act(negA, alog_bc, AF.Exp)
        kb.ts(negA, negA, -1.0, None, ALU.mult)

        xt = [kb.sb([128, D], F32, es=p1) for _ in range(2)]
        junk_2 = [kb.sb([128, D], BF16, es=p1) for _ in range(2)]
        hb = [kb.sb([128, D], BF16, es=p1) for _ in range(2)]
        hT = [kb.sb([128, 16, 128], BF16, es=p1) for _ in range(2)]
        sm_2 = [[kb.sb([128, 8], F32, es=p1) for _ in range(4)] for _ in range(2)]
        ckv = [kb.sb([128, 512], F32, es=p1) for _ in range(2)]
        ckvb_2 = [kb.sb([128, 512], BF16, es=p1) for _ in range(2)]
        ckvT = [kb.sb([128, 4, 128], BF16, es=p1) for _ in range(2)]
        krn_2 = [kb.sb([128, 64], F32, es=p1) for _ in range(2)]
        kro = [kb.sb([128, 64], F32, es=p1) for _ in range(2)]
        krt_2 = [kb.sb([128, 64], F32, es=p1) for _ in range(2)]
        krb_2 = [kb.sb([128, 64], BF16, es=p1) for _ in range(2)]
        krT = [kb.sb([64, 128], BF16, es=p1) for _ in range(2)]
        gbt = [kb.sb([128, 16], F32, es=p1) for _ in range(2)]
        gtmp_2 = [kb.sb([128, 8], F32, es=p1) for _ in range(2)]
        cst = [kb.sb([128, 64], F32, es=p1) for _ in range(2)]
        snt = [kb.sb([128, 64], F32, es=p1) for _ in range(2)]
        qan_2 = [kb.sb([128, 512], BF16, es=p1) for _ in range(2)]
        qanT_2 = [kb.sb([128, 4, 128], BF16, es=p1) for _ in range(2)]
        qf_2 = [kb.sb([128, 8, 192], F32, es=p1) for _ in range(2)]
        qsq_2 = [kb.sb([128, 8, 192], F32, es=p1) for _ in range(2)]
        qst_2 = [[kb.sb([128, 8], F32, es=p1) for _ in range(6)] for _ in range(2)]
        qnf_2 = [v[:, :, 0:128] for v in qsq_2]
        qnb_2 = [kb.sb([128, 8, 128], BF16, es=p1) for _ in range(2)]
        qrf_2 = [kb.sb([128, 8, 64], F32, es=p1) for _ in range(2)]
        qrt_2 = [kb.sb([128, 8, 64], F32, es=p1) for _ in range(2)]
        qro_2 = [kb.sb([128, 8, 64], F32, es=p1) for _ in range(2)]
        qrb_2 = [kb.sb([128, 8, 64], BF16, es=p1) for _ in range(2)]
        qnT = [kb.sb([128, 8, 128], BF16, es=p1) for _ in range(2)]
        qrT = [kb.sb([64, 8, 128], BF16, es=p1) for _ in range(2)]
        zs = [kb.sb([128, 1024], F32, es=p1) for _ in range(2)]

        def stageA(q, ti):
            t, nt = q.xt[ti]
            b = ti % 2
            junk, ckvb, krn, krt, krb, gtmp, qan, qanT, qf, qsq, qnf, qnb, qrf, qrt, qro, qrb, sm, qst = [v[b] for v in (junk_2, ckvb_2, krn_2, krt_2, krb_2, gtmp_2, qan_2, qanT_2, qf_2, qsq_2, qnf_2, qnb_2, qrf_2, qrt_2, qro_2, qrb_2, sm_2, qst_2)]
            r0 = ti * 128
            kb.dma(xt[b][:nt], q.x[r0:r0 + nt, :])
            kb.dma(cst[ti % 2][:nt], q.cs[r0:r0 + nt, :])
            kb.dma(snt[ti % 2][:nt], q.sn[r0:r0 + nt, :])
            ss, tmp, rs = sm[0][:nt, 0:1], sm[0][:nt, 1:2], sm[0][:nt, 2:3]
            kb.act(junk[:nt], xt[b][:nt], AF.Square, accum=ss)
            kb.rstd(rs, ss, D, tmp)
            kb.stt(hb[b][:nt], xt[b][:nt], rs, g_attn_bc[:nt], ALU.mult, ALU.mult)

        def stageB(q, ti):
            t, nt = q.xt[ti]
            b = ti % 2
            junk, ckvb, krn, krt, krb, gtmp, qan, qanT, qf, qsq, qnf, qnb, qrf, qrt, qro, qrb, sm, qst = [v[b] for v in (junk_2, ckvb_2, krn_2, krt_2, krb_2, gtmp_2, qan_2, qanT_2, qf_2, qsq_2, qnf_2, qnb_2, qrf_2, qrt_2, qro_2, qrb_2, sm_2, qst_2)]
            r0 = ti * 128
            for half in range(2):
                for k in range(8):
                    kt = half * 8 + k
                    kb.tr(psT[b][:, k * 128:k * 128 + nt], hb[b][:nt, kt * 128:(kt + 1) * 128], ident[:nt, :nt])
                kb.cp(hT[b][:, half * 8:half * 8 + 8, :nt], psT[b].re("p (k c) -> p k c", k=8)[:, :, :nt],
                      eng='act' if half == 0 else 'dve')
            kb.dma(q.hT_s[ti].re("p (k c) -> p k c", k=16)[:, :, :nt], hT[b][:, :, :nt])
            pk0, pk1 = (psA[0], psA[1]) if ti % 2 == 0 else (psA[2], psA[3])
            for kt in range(16):
                kb.mm(pk0[:nt], hT[b][:, kt, :nt], WAkv[:, kt, 512:1024], kt == 0, kt == 15)
            for kt in range(16):
                kb.mm(pk1[:nt, 0:64], hT[b][:, kt, :nt], WAkv[:, kt, 1024:1088], kt == 0, kt == 15)
            for kt in range(16):
                kb.mm(pk1[:nt, 64:80], hT[b][:, kt, :nt], WAkv[:, kt, 2112:2128], kt == 0, kt == 15)

        def stageB2(q, ti):
            t, nt = q.xt[ti]
            b = ti % 2
            junk, ckvb, krn, krt, krb, gtmp, qan, qanT, qf, qsq, qnf, qnb, qrf, qrt, qro, qrb, sm, qst = [v[b] for v in (junk_2, ckvb_2, krn_2, krt_2, krb_2, gtmp_2, qan_2, qanT_2, qf_2, qsq_2, qnf_2, qnb_2, qrf_2, qrt_2, qro_2, qrb_2, sm_2, qst_2)]
            b3 = ti % 2
            r0 = ti * 128
            pk0, pk1 = (psA[0], psA[1]) if ti % 2 == 0 else (psA[2], psA[3])
            ss, tmp, rs = sm[1][:nt, 0:1], sm[1][:nt, 1:2], sm[1][:nt, 2:3]
            kb.act(junk[:nt, 0:512], pk0[:nt], AF.Square, accum=ss)
            kb.rstd(rs, ss, 512, tmp)
            kb.stt(ckv[b][:nt], pk0[:nt], rs, g_kv_bc[:nt], ALU.mult, ALU.mult)
            kb.dma(q.o_lat[r0:r0 + nt, :], ckv[b][:nt])
            kb.cp(ckvb[:nt], ckv[b][:nt], eng='act')
            for k in range(4):
                kb.tr(psT[b][:, k * 128:k * 128 + nt], ckvb[:nt, k * 128:(k + 1) * 128], ident[:nt, :nt])
            kb.cp(ckvT[b][:, :, :nt], psT[b][:, 0:512].re("p (k c) -> p k c", k=4)[:, :, :nt], eng='act')
            kb.dma(q.ckvT_s[:, :, t * 128:t * 128 + nt], ckvT[b][:, :, :nt])
            ss, tmp, rs = sm[2][:nt, 0:1], sm[2][:nt, 1:2], sm[2][:nt, 2:3]
            kb.act(junk[:nt, 512:576], pk1[:nt, 0:64], AF.Square, accum=ss)
            kb.rstd(rs, ss, 64, tmp)
            kb.stt(krn[:nt], pk1[:nt, 0:64], rs, g_kr_bc[:nt], ALU.mult, ALU.mult)
            kb.tt(kro[b][:nt], krn[:nt], cst[b3][:nt], ALU.mult)
            kb.tt(krt[:nt, 0:32], krn[:nt, 32:64], snt[b3][:nt, 0:32], ALU.mult)
            kb.tt(krt[:nt, 32:64], krn[:nt, 0:32], snt[b3][:nt, 32:64], ALU.mult)
            kb.tt(kro[b][:nt], kro[b][:nt], krt[:nt], ALU.add)
            kb.dma(q.o_kr[r0:r0 + nt, :], kro[b][:nt])
            kb.cp(krb[:nt], kro[b][:nt], eng='act')
            kb.tr(psT[b][0:64, 0:nt], krb[:nt], ident[:nt, :nt])
            kb.cp(krT[b][:, :nt], psT[b][0:64, 0:nt], eng='act')
            kb.dma(q.krT_s[:, t * 128:t * 128 + nt], krT[b][:, :nt])
            kb.tt(gtmp[:nt], pk1[:nt, 64:72], dtb_bc[:nt], ALU.add)
            kb.act(gtmp[:nt], gtmp[:nt], AF.Exp)
            kb.act(gtmp[:nt], gtmp[:nt], AF.Ln, bias=1.0)
            kb.tt(gbt[b][:nt, 0:8], gtmp[:nt], negA[:nt], ALU.mult)
            kb.act(gbt[b][:nt, 8:16], pk1[:nt, 72:80], AF.Sigmoid)
            kb.dma(q.gb_s[r0:r0 + nt, :], gbt[b][:nt])
            if t < q.own0:
                return
            ot = t - q.own0
            c0 = ot * 128
            pown = psA[4 + b]
            for hh in range(2):
                for kt in range(16):
                    kb.mm(pown[:nt], hT[b][:, kt, :nt], WAown[:, kt, 1088 + hh * 512:1088 + (hh + 1) * 512], kt == 0, kt == 15)
                kb.act(zs[b][:nt, hh * 512:(hh + 1) * 512], pown[:nt], AF.Silu)
            kb.dma(q.z_s[c0:c0 + nt, :], zs[b][:nt])
            pq = pown
            for kt in range(16):
                kb.mm(pq[:nt], hT[b][:, kt, :nt], WAown[:, kt, 0:512], kt == 0, kt == 15)
            ss, tmp, rs = sm[3][:nt, 0:1], sm[3][:nt, 1:2], sm[3][:nt, 2:3]
            kb.act(junk[:nt, 0:512], pq[:nt], AF.Square, accum=ss)
            kb.rstd(rs, ss, 512, tmp)
            kb.stt(qan[:nt], pq[:nt], rs, g_ql_bc[:nt], ALU.mult, ALU.mult)
            for k in range(4):
                kb.tr(psT[b][:, k * 128:k * 128 + nt], qan[:nt, k * 128:(k + 1) * 128], ident[:nt, :nt])
            kb.cp(qanT[:, :, :nt], psT[b][:, 0:512].re("p (k c) -> p k c", k=4)[:, :, :nt], eng='dve')
            qfl = qf.re("p h c -> p (h c)")
            for j, pb in enumerate((pown, pown, pown)):
                for kt in range(4):
                    kb.mm(pb[:nt], qanT[:, kt, :nt], WQ[:, kt, j * 512:(j + 1) * 512], kt == 0, kt == 3)
                kb.cp(qfl[:nt, j * 512:(j + 1) * 512], pb[:nt], eng='act' if j % 2 == 0 else 'dve')
            kb.tt(qsq[:nt], qf[:nt], qf[:nt], ALU.mult, eng='pool')
            kb.red(qst[0][:nt], qsq[:nt, :, 0:128])
            kb.red(qst[1][:nt], qsq[:nt, :, 128:192])
            kb.rstd(qst[2][:nt], qst[0][:nt], 128, qst[4][:nt])
            kb.rstd(qst[3][:nt], qst[1][:nt], 64, qst[5][:nt])
            kb.tt(qnf[:nt], qf[:nt, :, 0:128], qst[2][:nt].unsq(2).bc([nt, 8, 128]), ALU.mult)
            kb.tt(qnb[:nt], qnf[:nt], g_qn_bc[:nt].unsq(1).bc([nt, 8, 128]), ALU.mult)
            kb.tt(qrf[:nt], qf[:nt, :, 128:192], qst[3][:nt].unsq(2).bc([nt, 8, 64]), ALU.mult)
            kb.tt(qrf[:nt], qrf[:nt], g_qr_bc[:nt].unsq(1).bc([nt, 8, 64]), ALU.mult)
            kb.tt(qro[:nt], qrf[:nt], cst[b3][:nt].unsq(1).bc([nt, 8, 64]), ALU.mult)
            kb.tt(qrt[:nt, :, 0:32], qrf[:nt, :, 32:64], snt[b3][:nt, 0:32].unsq(1).bc([nt, 8, 32]), ALU.mult)
            kb.tt(qrt[:nt, :, 32:64], qrf[:nt, :, 0:32], snt[b3][:nt, 32:64].unsq(1).bc([nt, 8, 32]), ALU.mult)
            kb.tt(qrb[:nt], qro[:nt], qrt[:nt], ALU.add)
            for hh in range(8):
                kb.tr(psT[b][:, hh * 128:hh * 128 + nt], qnb[:nt, hh, :], ident[:nt, :nt])
            kb.cp(qnT[b][:, :, :nt], psT[b].re("p (k c) -> p k c", k=8)[:, :, :nt], eng='act')
            kb.dma(q.qnT_s[:, :, c0:c0 + nt], qnT[b][:, :, :nt])
            for hh in range(8):
                kb.tr(psT[b][0:64, hh * 128:hh * 128 + nt], qrb[:nt, hh, :], ident[:nt, :nt])
            kb.cp(qrT[b][:, :, :nt], psT[b][0:64, :].re("p (k c) -> p k c", k=8)[:, :, :nt], eng='dve')
            kb.dma(q.qrT_s[:, :, c0:c0 + nt], qrT[b][:, :, :nt])

        ck4 = [v.bitc(BF16)[:, 0:2048].re("p (t c) -> p t c", t=4) for v in xt]
        ckT4 = [v.re("p (t c) -> p t c", t=4) for v in hb]
        kr4 = [v[:, 0:256].re("p (t c) -> p t c", t=4) for v in junk_2]
        krT4 = [v[0:64, 512:1024] for v in junk_2]
        psK = psA[5].bitc(BF16)

        def cached_group(q, gi):
            b = gi % 2
            t0 = gi * 4
            r0, r1 = t0 * 128, (t0 + 4) * 128
            kb.dma(ck4[b], q.c_lat[r0:r1, :].re("(t p) c -> p t c", p=128), q='pool')
            kb.dma(kr4[b], q.c_kr[r0:r1, :].re("(t p) c -> p t c", p=128), q='pool')
            for half in range(2):
                for kk in range(2):
                    k = half * 2 + kk
                    for j in range(4):
                        kb.tr(psT[half][:, kk * 512 + j * 128:kk * 512 + (j + 1) * 128], ck4[b][:, j, k * 128:(k + 1) * 128], ident)
                kb.cp(ckT4[b][:, half * 2:half * 2 + 2, :], psT[half].re("p (k c) -> p k c", k=2), eng='act' if half == 0 else 'dve')
            kb.dma(q.ckvT_s[:, :, r0:r1], ckT4[b])
            for j in range(4):
                kb.tr(psK[0:64, j * 128:(j + 1) * 128], kr4[b][:, j, :], ident)
            kb.cp(krT4[b], psK[0:64, 0:512], eng='dve')
            kb.dma(q.krT_s[:, r0:r1], krT4[b])

        for q in seqs:
            if q.name not in SEQS_ON:
                continue
            for gi in range(q.ncache // 4):
                cached_group(q, gi)
            n = len(q.xt)
            streams = [[], []]
            for ti in range(n):
                kb.S.record()
                stageA(q, ti)
                stageB(q, ti)
                stageB2(q, ti)
                streams[ti % 2].extend(kb.S.stop_record())
            s0, s1 = streams
            off = min(len(s1), 40)
            merged = list(s0[:off])
            i0, i1 = off, 0
            while i0 < len(s0) or i1 < len(s1):
                if i1 < len(s1):
                    merged.append(s1[i1])
                    i1 += 1
                if i0 < len(s0):
                    merged.append(s0[i0])
                    i0 += 1
            kb.S.replay(merged)

    if 2 in stages:
      with Phase(kb) as p2:
        WGkv = kb.sb([128, 16, 2048], BF16, es=p2)
        for k2 in range(8):
            kb.dma(WGkv[:, 2 * k2:2 * k2 + 2, :], w_in_v[:, 2 * k2:2 * k2 + 2, 2112:4160], q='pool')
        tri = kb.sb([128, 4, 128], F32, es=p2)
        kb.dma(tri, tri_in.re("k p c -> p k c"))
        LinclT, Umat, Mincl, Mstrict = tri[:, 0, :], tri[:, 1, :], tri[:, 2, :], tri[:, 3, :]
        g_go_bc = kb.sb([128, 128], F32, es=p2)
        load_bc(kb, g_go_bc, g_gdn_out)
        h3o = kb.sb([4, 3072], F32, es=p2)
        wc4 = h3o
        kb.dma(wc4, w_gdn_conv)
        wconv = kb.sb([128, 24, 4], F32, es=p2)
        for ct in range(24):
            kb.tr(psA[ct % 2][:, 0:4], wc4[:, ct * 128:(ct + 1) * 128], ident_f[:4, :4])
            kb.cp(wconv[:, ct, :], psA[ct % 2][:, 0:4], eng=evac_eng())
        hist3 = kb.sb([128, 24, 3], F32, es=p2)
        Sm = kb.sb([128, 8, 128], F32, es=p2)
        Sb = kb.sb([128, 8, 128], BF16, es=p2)
        ktm = kb.sb([128, 8, 128], BF16, es=p2)
        vtm = kb.sb([128, 8, 128], BF16, es=p2)
        qtm = kb.sb([128, 8, 128], BF16, es=p2)
        st = [kb.sb([128, 8], F32, es=p2) for _ in range(8)]
        kn = kb.sb([128, 8, 128], BF16, es=p2)
        qn = kb.sb([128, 8, 128], BF16, es=p2)
        qg = qtm
        knT = kb.sb([128, 8, 128], BF16, es=p2)
        qnT2 = kb.sb([128, 8, 128], BF16, es=p2)
        qgT = kb.sb([128, 8, 128], BF16, es=p2)
        gbt2 = [kb.sb([128, 16], F32, es=p2) for _ in range(2)]
        gs = kb.sb([128, 8, 8], F32, es=p2)
        rhsg = kb.sb([128, 8, 128], F32, es=p2)
        sq = rhsg
        Dm = kb.sb([128, 8, 128], F32, es=p2)
        NB = rhsg
        NbR = [kb.sb([128, 8, 128], F32R, es=p2) for _ in range(2)]
        YbR = [kb.sb([128, 8, 128], F32R, es=p2) for _ in range(2)]
        Nb16 = [kb.sb([128, 8, 16], BF16, es=p2) for _ in range(2)]
        Yb16 = [kb.sb([128, 8, 16], BF16, es=p2) for _ in range(2)]
        ident_r = kb.sb([128, 128], F32R, es=p2)
        kb.cp(ident_r, ident_f)
        intra = kb.sb([128, 8, 128], BF16, es=p2)
        intraT = kb.sb([128, 8, 128], BF16, es=p2)
        DmA = Dm.all2()
        DmH = [Dm[:, 0:4, :].sub(0), Dm[:, 4:8, :].sub(1)]
        TT = Dm
        onf = DmA
        TTbR = kb.sb([128, 8, 128], F32R, es=p2)
        RuR = kb.sb([128, 8, 128], F32R, es=p2)
        RwR = kb.sb([128, 8, 128], F32R, es=p2)
        TTb16 = kb.sb([128, 8, 16], BF16, es=p2)
        Ru16 = kb.sb([128, 8, 128], BF16, es=p2)
        Rw16 = kb.sb([128, 8, 128], BF16, es=p2)
        ub = rhsg
        wT = Rw16
        vn = ktm
        kd = Ru16
        zt = [ub.re('p h c -> p (h c)')] * 2
        ob = intra
        obT = [qnT2] * 2

        def ps3(bank, nt_, n=4):
            return bank.re("p (h j) -> p h j", h=n)

        def headmm(banks, fn):
            for hh in range(8):
                fn(hh, banks[hh // 4], (hh % 4) * 128)

        def evac2(dst, banks, nt_, cols, eng0=None):
            for half in range(2):
                kb.cp(dst[:nt_, half * 4:half * 4 + 4, :cols], ps3(banks[half], nt_)[:nt_, :, :cols],
                      eng=('act' if half == 0 else 'dve') if eng0 is None else eng0)

        def trans8(dst, src, nt_in, nparts_out, bank):
            for hh in range(8):
                kb.tr(bank[:nparts_out, hh * 128:hh * 128 + nt_in], src[:nt_in, hh, :nparts_out], ident[:nt_in, :nt_in])
            kb.cp(dst[:nparts_out, :, :nt_in], bank.re("p (h c) -> p h c", h=8)[:nparts_out, :, :nt_in], eng=evac_eng())

        def gdn_tile(q, t, nt, j, use_q, qk):
            ti = t - q.ncache
            hp = (nt == 128)
            Nb, Yb = (NbR, YbR) if hp else (Nb16, Yb16)
            TTb, Ru, Rw = (TTbR, RuR, RwR) if hp else (TTb16, Ru16, Rw16)
            rd = (lambda v: v.bitc(F32)) if hp else (lambda v: v)
            gb = gbt2[ti % 2]
            kb.dma(gb[:nt], q.gb_s[ti * 128:ti * 128 + nt, :])
            g = gb[:, 0:8]
            beta = gb[:, 8:16]
            qkvT, kof, vof, qof = qk
            groups = [(kof, ktm, psT[0]), (vof, vtm, psT[1])]
            if use_q:
                groups.append((qof, qtm, psT[0]))
            for c0_, dst, bank in groups:
                for hh in range(8):
                    kb.tr(bank[:nt, hh * 128:(hh + 1) * 128], qkvT[:, c0_ + hh, j * 128:j * 128 + nt], ident)
                kb.cp(dst[:nt], bank.re("p (h c) -> p h c", h=8)[:nt], eng=evac_eng())
            kb.tt(sq[:nt], ktm[:nt], ktm[:nt], ALU.mult, eng='pool')
            kb.red(st[0][:nt], sq[:nt])
            kb.rstd(st[1][:nt], st[0][:nt], 1.0, st[2][:nt])
            kb.tt(kn[:nt], ktm[:nt], st[1][:nt].unsq(2).bc([nt, 8, 128]), ALU.mult)
            if use_q:
                kb.tt(sq[:nt], qtm[:nt], qtm[:nt], ALU.mult, eng='pool')
                kb.red(st[3][:nt], sq[:nt])
                kb.rstd(st[4][:nt], st[3][:nt], 1.0, st[5][:nt])
                kb.ts(st[4][:nt], st[4][:nt], 128.0 ** -0.5, None, ALU.mult)
                kb.tt(qn[:nt], qtm[:nt], st[4][:nt].unsq(2).bc([nt, 8, 128]), ALU.mult)
            pg = psA[0]
            kb.mm(pg[:nt, 0:8], LinclT[:nt, :nt], g[:nt], True, True)
            kb.mm(pg[:, 8:16], ones_f[:nt, :], g[:nt], True, True)
            gc, egc, gtot, elast, kdec, nbeta, bexp, gtmp2 = [gs[:, k_, :] for k_ in range(8)]
            kb.cp(gc[:nt], pg[:nt, 0:8], eng='dve')
            kb.cp(gtot, pg[:, 8:16], eng='dve')
            kb.act(egc[:nt], gc[:nt], AF.Exp)
            kb.act(elast, gtot, AF.Exp)
            kb.tt(gtmp2[:nt], gtot[:nt], gc[:nt], ALU.subtract)
            kb.act(kdec[:nt], gtmp2[:nt], AF.Exp)
            kb.ts(nbeta[:nt], beta[:nt], -1.0, None, ALU.mult)
            kb.tt(bexp[:nt], beta[:nt], egc[:nt], ALU.mult)
            kb.tt(rhsg[:nt, :, :nt], Umat[:nt, :nt].unsq(1).bc([nt, 8, nt]), g[:nt].unsq(2).bc([nt, 8, nt]), ALU.mult,
                  eng='pool')
            pd = (psA[2], psA[3])
            for half in range(2):
                kb.mm(ps3(pd[half], nt)[:nt, :, :nt], LinclT[:nt, :nt], rhsg[:nt, half * 4:half * 4 + 4, :nt], True, True)
                kb.act(DmH[half][:nt, :, :nt], ps3(pd[half], nt)[:nt, :, :nt], AF.Exp)
            kb.tt(DmA[:nt, :, :nt], DmA[:nt, :, :nt], Mincl[:nt, :nt].unsq(1).bc([nt, 8, nt]), ALU.mult, eng='pool')
            kb.tt(NB[:nt, :, :nt], DmA[:nt, :, :nt], Mstrict[:nt, :nt].unsq(1).bc([nt, 8, nt]), ALU.mult, eng='pool')
            kb.tt(NB[:nt, :, :nt], NB[:nt, :, :nt], nbeta[:nt].unsq(2).bc([nt, 8, nt]), ALU.mult, eng='pool')
            trans8(knT, kn, nt, 128, psT[1])
            pkk = (psA[4], psA[5])
            headmm(pkk, lambda hh, bank, c: kb.mm(bank[:nt, c:c + nt], knT[:, hh, :nt], knT[:, hh, :nt], True, True))
            hv = lambda v: [v[:, 0:4, :].sub(0), v[:, 4:8, :].sub(1)]
            NbH = [hv(Nb[0]), hv(Nb[1])]
            YbH = [hv(Yb[0]), hv(Yb[1])]
            TTH = hv(TT)
            TTbH = hv(TTb)
            for half in range(2):
                kb.tt(NbH[0][half][:nt, :, :nt], ps3(pkk[half], nt)[:nt, :, :nt],
                      NB[:nt, half * 4:half * 4 + 4, :nt], ALU.mult)
            if hp:
                for hh in range(8):
                    kb.mm(psA[hh // 4][:nt, (hh % 4) * 128:(hh % 4) * 128 + nt], NbH[0][hh // 4][:nt, hh % 4, :nt],
                          ident_r[:nt, :nt], True, True)
                for half in range(2):
                    kb.cp(YbH[0][half][:nt, :, :nt], ps3(psA[half], nt)[:nt, :, :nt], eng='act' if half == 0 else 'dve')
            else:
                for hh in range(8):
                    kb.tr(psT[0][:nt, hh * 128:hh * 128 + nt], NbH[0][hh // 4][:nt, hh % 4, :nt], ident[:nt, :nt])
                for half in range(2):
                    kb.cp(YbH[0][half][:nt, :, :nt], psT[0].re("p (h c) -> p h c", h=8)[:nt, half * 4:half * 4 + 4, :nt],
                          eng='act' if half == 0 else 'dve')
            if use_q:
                trans8(qnT2, qn, nt, 128, psT[1])
                pqk = (psA[2], psA[3])
                headmm(pqk, lambda hh, bank, c: kb.mm(bank[:nt, c:c + nt], qnT2[:, hh, :nt], knT[:, hh, :nt], True, True))
                for half in range(2):
                    kb.tt(intra[:nt, half * 4:half * 4 + 4, :nt], ps3(pqk[half], nt)[:nt, :, :nt],
                          DmH[half][:nt, :, :nt], ALU.mult)
                trans8(intraT, intra, nt, nt, psT[0])
                kb.tt(qg[:nt], qn[:nt], egc[:nt].unsq(2).bc([nt, 8, 128]), ALU.mult, eng='pool')
                trans8(qgT, qg, nt, 128, psT[1])
            for half in range(2):
                kb.tt(TTH[half][:nt, :, :nt], rd(YbH[0][half])[:nt, :, :nt],
                      ident_f[:nt, :nt].unsq(1).bc([nt, 4, nt]), ALU.add, eng='dve' if half == 0 else 'pool')
                kb.cp(TTbH[half][:nt, :, :nt], TTH[half][:nt, :, :nt], eng='act')
            nlev = 0
            while (1 << (nlev + 1)) < nt:
                nlev += 1
            cur = 0
            for lev in range(1, nlev + 1):
                nxt = 1 - cur
                last = (lev == nlev)
                pN = (psA[0], psA[1])
                pY = (psA[2], psA[3])
                pT = (psA[4], psA[5])
                for half in range(2):
                    for j in range(4):
                        kb.mm(pN[half][:nt, j * 128:j * 128 + nt], YbH[cur][half][:nt, j, :nt], NbH[cur][half][:nt, j, :nt], True, True)
                    if not last:
                        for j in range(4):
                            kb.mm(pY[half][:nt, j * 128:j * 128 + nt], NbH[cur][half][:nt, j, :nt], YbH[cur][half][:nt, j, :nt], True, True)
                for half in range(2):
                    kb.cp(NbH[nxt][half][:nt, :, :nt], ps3(pN[half], nt)[:nt, :, :nt], eng='act')
                    if not last:
                        kb.cp(YbH[nxt][half][:nt, :, :nt], ps3(pY[half], nt)[:nt, :, :nt], eng='dve')
                for half in range(2):
                    for j in range(4):
                        kb.mm(pT[half][:nt, j * 128:j * 128 + nt], NbH[nxt][half][:nt, j, :nt], TTbH[half][:nt, j, :nt], True, True)
                for half in range(2):
                    kb.tt(TTH[half][:nt, :, :nt], TTH[half][:nt, :, :nt], ps3(pT[half], nt)[:nt, :, :nt], ALU.add)
                    kb.cp(TTbH[half][:nt, :, :nt], TTH[half][:nt, :, :nt], eng='act')
                cur = nxt
            kb.tt(Ru[:nt], vtm[:nt], beta[:nt].unsq(2).bc([nt, 8, 128]), ALU.mult, eng='dve' if hp else 'pool')
            kb.tt(Rw[:nt], kn[:nt], bexp[:nt].unsq(2).bc([nt, 8, 128]), ALU.mult, eng='dve' if hp else 'pool')
            pu = (psA[0], psA[1])
            pw = (psA[2], psA[3])
            headmm(pu, lambda hh, bank, c: kb.mm(bank[:nt, c:c + 128], TTbH[hh // 4][:nt, hh % 4, :nt], Ru[:nt, hh, :], True, True))
            headmm(pw, lambda hh, bank, c: kb.mm(bank[:, c:c + nt], Rw[:nt, hh, :], TTbH[hh // 4][:nt, hh % 4, :nt], True, True))
            evac2(ub, pu, nt, 128)
            evac2(wT, pw, 128, nt)
            pws = (psA[4], psA[5])
            headmm(pws, lambda hh, bank, c: kb.mm(bank[:nt, c:c + 128], wT[:, hh, :nt], Sb[:, hh, :], True, True))
            for half in range(2):
                kb.tt(vn[:nt, half * 4:half * 4 + 4, :], ub[:nt, half * 4:half * 4 + 4, :], ps3(pws[half], nt)[:nt], ALU.subtract)
            if use_q:
                po = (psA[0], psA[1])

                def omm(hh, bank, c):
                    kb.mm(bank[:nt, c:c + 128], intraT[:nt, hh, :nt], vn[:nt, hh, :], True, False)
                    kb.mm(bank[:nt, c:c + 128], qgT[:, hh, :nt], Sb[:, hh, :], False, True)
                headmm(po, omm)
            kb.tt(kd[:nt], kn[:nt], kdec[:nt].unsq(2).bc([nt, 8, 128]), ALU.mult, eng='pool')
            pds = (psA[2], psA[3])
            headmm(pds, lambda hh, bank, c: kb.mm(bank[:, c:c + 128], kd[:nt, hh, :], vn[:nt, hh, :], True, True))
            kb.tt(Sm, Sm, elast.unsq(2).bc([128, 8, 128]), ALU.mult, eng='pool')
            for half in range(2):
                kb.tt(Sm[:, half * 4:half * 4 + 4, :], Sm[:, half * 4:half * 4 + 4, :], ps3(pds[half], 128), ALU.add)
            kb.cp(Sb, Sm, eng='act')
            if use_q:
                ot = t - q.own0
                c0 = ot * 128
                z = zt[ot % 2]
                kb.dma(z[:nt], q.z_s[c0:c0 + nt, :])
                for half in range(2):
                    kb.act(onf[:nt, half * 4:half * 4 + 4, :], ps3(po[half], nt)[:nt], AF.Square)
                kb.red(st[6][:nt], onf[:nt])
                kb.rstd(st[7][:nt], st[6][:nt], 128.0, st[5][:nt])
                for half in range(2):
                    kb.tt(onf[:nt, half * 4:half * 4 + 4, :], ps3(po[half], nt)[:nt],
                          st[7][:nt, half * 4:half * 4 + 4].unsq(2).bc([nt, 4, 128]), ALU.mult)
                kb.tt(onf[:nt], onf[:nt], g_go_bc[:nt].unsq(1).bc([nt, 8, 128]), ALU.mult, eng='pool')
                kb.tt(ob[:nt], onf[:nt], z[:nt].re("p (h c) -> p h c", h=8), ALU.mult, eng='pool')
                oT_ = obT[ot % 2]
                trans8(oT_, ob, nt, 128, psT[0])
                kb.dma(q.oT_s[8:16].re("h p c -> p h c")[:, :, c0:c0 + nt], oT_[:, :, :nt])

        for q in seqs:
            if q.name not in SEQS_ON:
                continue
            if q.prompt:
                kb.memset(hist3, 0.0)
                kb.memset(Sm, 0.0)
                kb.memset(Sb, 0.0)
            else:
                kb.dma(h3o[:3], q.c_gconv)
                for ct in range(24):
                    kb.tr(psA[ct % 2][:, 0:3], h3o[:3, ct * 128:(ct + 1) * 128], ident_f[:3, :3])
                    kb.cp(hist3[:, ct, :], psA[ct % 2][:, 0:3], eng=evac_eng())
                kb.dma(Sm, q.c_S.re("h k v -> k h v"))
                kb.cp(Sb, Sm, eng='act')
            def run_blocks(tiles, bs, hTb_, raw_, acc_, qkvT_, with_q, WGq_):
                nct = 24 if with_q else 16
                for bi in range(0, len(tiles), bs):
                    bl = tiles[bi:bi + bs]
                    ntok = sum(nt for _, nt in bl)
                    for j, (t, nt) in enumerate(bl):
                        kb.dma(hTb_[:, :, j * 128:j * 128 + nt],
                               q.hT_s[t - q.ncache].re("p (k c) -> p k c", k=16)[:, :, :nt])
                    cts = list(range(8, 24)) + (list(range(8)) if with_q else [])
                    for n_, ct in enumerate(cts):
                        pb = psA[n_ % 2]
                        for kt in range(16):
                            wsl = WGq_[:, kt, ct * 128:(ct + 1) * 128] if ct < 8 else WGkv[:, kt, (ct - 8) * 128:(ct - 7) * 128]
                            kb.mm(pb[:, :ntok], wsl, hTb_[:, kt, :ntok], kt == 0, kt == 15)
                        rw = raw_[n_ % 2]
                        ac = acc_[n_ % 2]
                        slot = ct if with_q else ct - 8
                        kb.cp(rw[:, 0:3], hist3[:, ct, :], eng='pool')
                        kb.cp(rw[:, 3:3 + ntok], pb[:, :ntok], eng='act')
                        kb.cp(hist3[:, ct, :], rw[:, ntok:ntok + 3], eng='pool')
                        kb.ts(ac[:, :ntok], rw[:, 0:ntok], wconv[:, ct, 0:1], None, ALU.mult)
                        for k_ in range(1, 4):
                            kb.stt(ac[:, :ntok], rw[:, k_:k_ + ntok], wconv[:, ct, k_:k_ + 1], ac[:, :ntok], ALU.mult, ALU.add)
                        kb.act(qkvT_[:, slot, :ntok], ac[:, :ntok], AF.Silu)
                    qk = (qkvT_, 8, 16, 0) if with_q else (qkvT_, 0, 8, None)
                    for j, (t, nt) in enumerate(bl):
                        gdn_tile(q, t, nt, j, with_q and t >= q.own0, qk)

            nA = ((q.own0 - q.ncache - 3) // 4) * 4 if q.prompt else 0
            nA = max(nA, 0)
            tilesA, tilesB = q.xt[:nA], q.xt[nA:]
            if tilesA:
                with Phase(kb) as p2a:
                    hTb4 = kb.sb([128, 16, 512], BF16, es=p2a)
                    raw4 = [kb.sb([128, 516], F32, es=p2a) for _ in range(2)]
                    acc4 = [kb.sb([128, 512], F32, es=p2a) for _ in range(2)]
                    qkvT4 = kb.sb([128, 16, 512], BF16, es=p2a)
                    run_blocks(tilesA, 4, hTb4, raw4, acc4, qkvT4, False, None)
            with Phase(kb) as p2b:
                WGq = kb.sb([128, 16, 1024], BF16, es=p2b)
                for k4 in range(4):
                    kb.dma(WGq[:, 4 * k4:4 * k4 + 4, :], w_in_v[:, 4 * k4:4 * k4 + 4, 1088:2112], q='pool')
                hTb2 = kb.sb([128, 16, 256], BF16, es=p2b)
                raw2 = [kb.sb([128, 260], F32, es=p2b) for _ in range(2)]
                acc2 = [kb.sb([128, 256], F32, es=p2b) for _ in range(2)]
                qkvT2 = kb.sb([128, 24, 256], BF16, es=p2b)
                run_blocks(tilesB, 2, hTb2, raw2, acc2, qkvT2, True, WGq)
            kb.dma(q.o_S.re("h k v -> k h v"), Sm)
            for ct in range(24):
                pb = psA[(ct // 4) % 2]
                kb.tr(pb[:3, (ct % 4) * 128:(ct % 4 + 1) * 128], hist3[:, ct, :], ident_f)
                if ct % 4 == 3:
                    kb.cp(h3o[:3, (ct - 3) * 128:(ct + 1) * 128], pb[:3, :], eng=evac_eng())
            kb.dma(q.o_gconv, h3o[:3])

    SCALE = 192.0 ** -0.5
    if 3 in stages:
      with Phase(kb) as p3:
        WUK = kb.sb([128, 4, 2048], BF16, es=p3)
        stg3 = [kb.sb([128, 2048 if not CAST_DMA else 2], F32, es=p3) for _ in range(2)]
        wl3 = WLoader(kb, stg3, ('pool', 'dve'))
        wkv_v = w_kv_up.re("(kt p) c -> p kt c", p=128)
        for kt in range(4):
            wl3.load(WUK[:, kt:kt + 1, :], wkv_v[:, kt:kt + 1, :])
        gk_col = kb.sb([128, 1], F32, es=p3)
        kb.dma(gk_col, g_k_nope.re("(p o) -> p o", o=1))
        ckvT3 = kb.sb([128, 4, NT * 128], BF16, es=p3)
        krT3 = kb.sb([64, NT * 128], BF16, es=p3)
        KT = kb.sb([128, NT * 128], BF16, es=p3)
        Vh = kb.sb([128, NT, 128], BF16, es=p3)
        qn_h = kb.sb([128, NOWN * 128], BF16, es=p3)
        qr_h = kb.sb([64, NOWN * 128], BF16, es=p3)
        kmask_sb = kb.sb([128, NT], F32, es=p3)
        sqb = [kb.sb([128, 512], BF16, es=p3) for _ in range(2)]
        rst = [kb.sb([128, 512], F32, es=p3) for _ in range(2)]
        PT = [kb.sb([128, 512], BF16, es=p3) for _ in range(3)]
        oT3 = [kb.sb([128, 512], BF16, es=p3) for _ in range(2)]
        den = kb.sb([128, 512], F32, es=p3)
        dacc = kb.sb([128, 512], F32, es=p3)
        cnt3 = [0, 0, 0]
        for q in seqs:
            if q.name not in SEQS_ON:
                continue
            nkeys = q.nkeys
            for c in range(4):
                kb.dma(ckvT3[:, c, :nkeys], q.ckvT_s[:, c, :nkeys])
            kb.dma(krT3[:, :nkeys], q.krT_s[:, :nkeys])
            if q.prompt:
                kb.dma(kmask_sb, q.kmask)
            else:
                kb.memset(kmask_sb, 0.0)
            ktiles = [(t, min(128, nkeys - t * 128)) for t in range(q.ntc)]
            if q.prompt:
                groups = [[(q.own0, 128)]] + [[(t, 128) for t in range(a, a + 4)] for a in range(q.own0 + 1, q.ntc, 4)]
            else:
                groups = [[(q.own0, 16)]]
            nq_all = sum(nt for g_ in groups for _, nt in g_)
            for hh in range(8):
                for bi, k0 in enumerate(range(0, nkeys, 512)):
                    kw = min(512, nkeys - k0)
                    pk = psA[bi % 2]
                    for c in range(4):
                        kb.mm(pk[:, :kw], WUK[:, c, hh * 256:hh * 256 + 128], ckvT3[:, c, k0:k0 + kw], c == 0, c == 3)
                    sq_ = sqb[bi % 2]
                    kb.act(sq_[:, :kw], pk[:, :kw], AF.Square)
                    pss = psA[2 + bi % 2]
                    kb.mm(pss[:, :kw], ones_b, sq_[:, :kw], True, True)
                    rs_ = rst[bi % 2]
                    kb.ts(rs_[:, :kw], pss[:, :kw], 1.0 / 128, EPS, ALU.mult, ALU.add)
                    kb.act(rs_[:, :kw], rs_[:, :kw], AF.Sqrt)
                    kb.recip(rs_[:, :kw], rs_[:, :kw])
                    kb.stt(KT[:, k0:k0 + kw], pk[:, :kw], gk_col, rs_[:, :kw], ALU.mult, ALU.mult)
                full = [t for t, nk in ktiles if nk == 128]
                for gi, a in enumerate(range(0, len(full), 4)):
                    grp = full[a:a + 4]
                    pv = psA[4 + gi % 2]
                    for sl, t in enumerate(grp):
                        for c in range(4):
                            kb.mm(pv[:, sl * 128:(sl + 1) * 128], ckvT3[:, c, t * 128:(t + 1) * 128],
                                  WUK[:, c, hh * 256 + 128:hh * 256 + 256], c == 0, c == 3)
                    kb.cp(Vh[:, grp[0]:grp[0] + len(grp), :].re("p t c -> p (t c)"), pv[:, :len(grp) * 128], eng=evac_eng())
                for t, nk in ktiles:
                    if nk < 128:
                        pv = psA[4]
                        for c in range(4):
                            kb.mm(pv[:nk, 0:128], ckvT3[:, c, t * 128:t * 128 + nk],
                                  WUK[:, c, hh * 256 + 128:hh * 256 + 256], c == 0, c == 3)
                        kb.cp(Vh[:nk, t, :], pv[:nk, 0:128], eng=evac_eng())
                kb.dma(qn_h[:, :nq_all], q.qnT_s[:, hh, :nq_all])
                kb.dma(qr_h[:, :nq_all], q.qrT_s[:, hh, :nq_all])
                for G in groups:
                    ntq = sum(nt for _, nt in G)
                    c0 = (G[0][0] - q.own0) * 128
                    vis = [kt for kt in ktiles if (kt[0] <= G[-1][0] or not q.prompt)]
                    pnum, pden = psA[2], psA[3]

                    def geom(vi):
                        kt, nk = vis[vi]
                        off = (max(G[0][0], kt) - G[0][0]) * 128 if q.prompt else 0
                        return kt, nk, off, ntq - off

                    def emit_qk(vi):
                        kt, nk, off, nq = geom(vi)
                        pst = psA[vi % 2]
                        kb.mm(pst[:nk, :nq], KT[:, kt * 128:kt * 128 + nk], qn_h[:, c0 + off:c0 + off + nq], True, False)
                        kb.mm(pst[:nk, :nq], krT3[:, kt * 128:kt * 128 + nk], qr_h[:, c0 + off:c0 + off + nq], False, True)

                    emit_qk(0)
                    for vi in range(len(vis)):
                        kt, nk, off, nq = geom(vi)
                        if vi + 1 < len(vis):
                            emit_qk(vi + 1)
                        pst = psA[vi % 2]
                        pt = PT[cnt3[1] % 3]
                        cnt3[1] += 1
                        kb.act(pt[:nk, :nq], pst[:nk, :nq], AF.Exp, bias=kmask_sb[:nk, kt:kt + 1], scale=SCALE)
                        if q.prompt and kt >= G[0][0]:
                            kb.memset(pt[64:128, 0:64], 0.0, eng='pool')
                        if vi == 0:
                            kb.cp(dacc[:, :ntq], pt[:, :ntq], eng='dve')
                        else:
                            kb.tt(dacc[:nk, off:off + nq], dacc[:nk, off:off + nq], pt[:nk, :nq], ALU.add)
                        kb.mm(pnum[:, off:off + nq], Vh[:nk, kt, :], pt[:nk, :nq], vi == 0, vi == len(vis) - 1)
                    kb.mm(pden[:, :ntq], ones_f, dacc[:, :ntq], True, True)
                    kb.ts(den[:, :ntq], pden[:, :ntq], 1e-30, None, ALU.max)
                    kb.recip(den[:, :ntq], den[:, :ntq])
                    o_ = oT3[cnt3[2] % 2]
                    cnt3[2] += 1
                    kb.tt(o_[:, :ntq], pnum[:, :ntq], den[:, :ntq], ALU.mult)
                    kb.dma(q.oT_s[hh, :, c0:c0 + ntq], o_[:, :ntq])

    if 4 in stages:
      cols0 = {}
      tot = 0
      for q in seqs:
          cols0[q.name] = tot
          tot += sum(nt for t, nt in q.xt if t >= q.own0)
      TOT = tot
      hid_s = scr("hid_s", [44, 128, TOT], BF16)
      with Phase(kb) as p4:
        h2T = kb.sb([128, 16, TOT], BF16, es=p4)
        with Phase(kb) as p4a:
            WO = kb.sb([128, 16, 2048], BF16, es=p4a)
            stg4 = [kb.sb([128, 2048 if not CAST_DMA else 2], F32, es=p4a) for _ in range(2)]
            wl4 = WLoader(kb, stg4, ('pool', 'dve', 'act'))
            wo_v = w_out.re("(kt p) c -> p kt c", p=128)
            WOd = [WO.sub(dc) for dc in range(4)]
            for dc in range(4):
                for k4 in range(4):
                    wl4.load(WOd[dc][:, 4 * k4:4 * k4 + 4, dc * 512:(dc + 1) * 512], wo_v[:, 4 * k4:4 * k4 + 4, dc * 512:(dc + 1) * 512])
                wl4.engs = ('pool',)
            g_ffn_bc = kb.sb([128, D], F32, es=p4a)
            load_bc(kb, g_ffn_bc, g_ffn)
            oTt = [kb.sb([128, 16, 128], BF16, es=p4a) for _ in range(2)]
            xt4 = [kb.sb([128, D], F32, es=p4a) for _ in range(2)]
            xn = [kb.sb([128, D], F32, es=p4a) for _ in range(2)]
            junk4 = kb.sb([128, D], BF16, es=p4a)
            h2 = [kb.sb([128, D], BF16, es=p4a) for _ in range(2)]
            sm4 = kb.sb([128, 4], F32, es=p4a)
            n4 = 0
            tl4 = []
            for q in seqs:
                if q.name not in SEQS_ON:
                    continue
                for (t, nt) in q.xt:
                    if t >= q.own0:
                        tl4.append((q, t, nt))

            def p4_head(i):
                q, t, nt = tl4[i]
                b = i % 2
                ot = t - q.own0
                c0 = ot * 128
                xr0 = (t - q.ncache) * 128
                kb.dma(oTt[b][:, :, :nt], q.oT_s[:, :, c0:c0 + nt].re("k p c -> p k c"), q='pool')
                kb.dma(xt4[b][:nt], q.x[xr0:xr0 + nt, :], q='pool')
                for dc in range(4):
                    pb = psA[dc]
                    for kt in range(16):
                        kb.mm(pb[:nt], oTt[b][:, kt, :nt], WOd[dc][:, kt, dc * 512:(dc + 1) * 512], kt == 0, kt == 15)
                    kb.tt(xn[b][:nt, dc * 512:(dc + 1) * 512], pb[:nt], xt4[b][:nt, dc * 512:(dc + 1) * 512], ALU.add)
                kb.dma(q.xn_s[c0:c0 + nt, :], xn[b][:nt])
                ss, tmp, rs = sm4[:nt, 0:1], sm4[:nt, 1:2], sm4[:nt, 2:3]
                kb.act(junk4[:nt], xn[b][:nt], AF.Square, accum=ss)
                kb.rstd(rs, ss, D, tmp)
                kb.stt(h2[b][:nt], xn[b][:nt], rs, g_ffn_bc[:nt], ALU.mult, ALU.mult)

            def p4_tail(i):
                q, t, nt = tl4[i]
                b = i % 2
                col = cols0[q.name] + (t - q.own0) * 128
                for half in range(2):
                    for k in range(8):
                        kt = half * 8 + k
                        kb.tr(psT[half][:, k * 128:k * 128 + nt], h2[b][:nt, kt * 128:(kt + 1) * 128], ident[:nt, :nt])
                    kb.cp(h2T[:, half * 8:half * 8 + 8, col:col + nt],
                          psT[half].re("p (k c) -> p k c", k=8)[:, :, :nt], eng='act' if half == 0 else 'dve')

            for i in range(len(tl4)):
                p4_head(i)
                if i > 0:
                    p4_tail(i - 1)
            p4_tail(len(tl4) - 1)
        with Phase(kb) as p4b:
            Wg = [kb.sb([128, 16, 512], BF16, es=p4b) for _ in range(2)]
            Wu = [kb.sb([128, 16, 512], BF16, es=p4b) for _ in range(2)]
            cw4 = [kb.sb([4, 512], F32, es=p4b) for _ in range(2)]
            taps = [kb.sb([128, 4], F32, es=p4b) for _ in range(2)]
            cf2 = {q.name: kb.sb([2, 512], F32, es=p4b) for q in seqs if not q.prompt}
            hist2 = kb.sb([128, 2], F32, es=p4b)
            graw = [kb.sb([128, 516], F32, es=p4b) for _ in range(2)]
            gcv = [kb.sb([128, 512], F32, es=p4b) for _ in range(2)]
            sg = [kb.sb([128, 512], F32, es=p4b) for _ in range(2)]
            hid = [kb.sb([128, 512], BF16, es=p4b) for _ in range(3)]
            fout = {q.name: kb.sb([128, 4, 2], F32, es=p4b) for q in seqs}
            fo2 = [kb.sb([2, 512], F32, es=p4b) for _ in range(2)]
            wgv = w_gate.re("(kt p) c -> p kt c", p=128)
            wuv = w_up.re("(kt p) c -> p kt c", p=128)
            n5 = 0
            stg5 = [kb.sb([128, 4096 if not CAST_DMA else 2], F32, es=p4b) for _ in range(2)]
            wl5 = WLoader(kb, stg5, ('pool',))
            def load_fc(fc):
                wb = fc % 2
                for k8 in range(2):
                    wl5.load(Wg[wb][:, 8 * k8:8 * k8 + 8, :], wgv[:, 8 * k8:8 * k8 + 8, fc * 512:(fc + 1) * 512])
                    wl5.load(Wu[wb][:, 8 * k8:8 * k8 + 8, :], wuv[:, 8 * k8:8 * k8 + 8, fc * 512:(fc + 1) * 512])
                kb.dma(cw4[wb][0:3, :], w_fconv[:, fc * 512:(fc + 1) * 512], q='pool')
                kb.dma(cw4[wb][3:4, :], b_fconv.re("(o c) -> o c", o=1)[:, fc * 512:(fc + 1) * 512], q='pool')

            load_fc(0)
            for fc in range(11):
                wb = fc % 2
                if fc + 1 < 11:
                    load_fc(fc + 1)
                for q in seqs:
                    if not q.prompt and q.name in SEQS_ON:
                        pass
                for ffi in range(4):
                    ff = fc * 4 + ffi
                    tp = taps[ff % 2]
                    kb.tr(psA[5][:, 0:4], cw4[wb][:4, ffi * 128:(ffi + 1) * 128], ident_f[:4, :4])
                    kb.cp(tp, psA[5][:, 0:4], eng='dve')
                    for q in seqs:
                        if q.name not in SEQS_ON:
                            continue
                        if q.prompt:
                            blocks = [(0, 128, True)] + [(128 + 512 * i, 512, False) for i in range((q.nown - 1) // 4)]
                            kb.memset(hist2, 0.0, eng='pool')
                        else:
                            blocks = [(0, 16, False)]
                            if ffi == 0:
                                kb.dma(cf2[q.name], q.c_fconv[:, fc * 512:(fc + 1) * 512])
                                q.cf2cur = cf2[q.name]
                            kb.tr(psA[5][:, 8:10], q.cf2cur[:2, ffi * 128:(ffi + 1) * 128], ident_f[:2, :2])
                            kb.cp(hist2, psA[5][:, 8:10], eng='dve')
                        for (cb, ntok, halo) in blocks:
                            col = cols0[q.name] + cb
                            n5 += 1
                            pg = psA[n5 % 2]
                            pu_ = psA[2 + n5 % 2]
                            for kt in range(16):
                                kb.mm(pg[:, :ntok], Wg[wb][:, kt, ffi * 128:(ffi + 1) * 128], h2T[:, kt, col:col + ntok], kt == 0, kt == 15)
                            if not halo:
                                for kt in range(16):
                                    kb.mm(pu_[:, :ntok], Wu[wb][:, kt, ffi * 128:(ffi + 1) * 128], h2T[:, kt, col:col + ntok], kt == 0, kt == 15)
                            gr = graw[n5 % 2]
                            kb.cp(gr[:, 0:2], hist2, eng='pool')
                            kb.cp(gr[:, 2:2 + ntok], pg[:, :ntok], eng='act')
                            kb.cp(hist2, gr[:, ntok:ntok + 2], eng='pool')
                            if halo:
                                continue
                            gv = gcv[n5 % 2]
                            kb.ts(gv[:, :ntok], gr[:, 0:ntok], tp[:, 0:1], None, ALU.mult)
                            kb.stt(gv[:, :ntok], gr[:, 1:1 + ntok], tp[:, 1:2], gv[:, :ntok], ALU.mult, ALU.add)
                            kb.stt(gv[:, :ntok], gr[:, 2:2 + ntok], tp[:, 2:3], gv[:, :ntok], ALU.mult, ALU.add)
                            sg_ = sg[n5 % 2]
                            kb.act(sg_[:, :ntok], gv[:, :ntok], AF.Silu, bias=tp[:, 3:4])
                            hd = hid[n5 % 3]
                            kb.tt(hd[:, :ntok], sg_[:, :ntok], pu_[:, :ntok], ALU.mult)
                            kb.dma(hid_s[ff, :, col:col + ntok], hd[:, :ntok])
                        kb.cp(fout[q.name][:, ffi, :], hist2, eng='pool')
                for q in seqs:
                    if q.name not in SEQS_ON:
                        continue
                    n5 += 1
                    pb = psA[4]
                    for ffi in range(4):
                        kb.tr(pb[:2, ffi * 128:(ffi + 1) * 128], fout[q.name][:, ffi, :], ident_f)
                    kb.cp(fo2[n5 % 2], pb[:2, :], eng='dve')
                    kb.dma(q.o_fconv[:, fc * 512:(fc + 1) * 512], fo2[n5 % 2])
      with Phase(kb) as p4c:
        Wd = [kb.sb([128, 44, 512], BF16, es=p4c) for _ in range(2)]
        hidt = [kb.sb([128, 44, 512], BF16, es=p4c) for _ in range(2)]
        n7 = [0]
        xnt = [kb.sb([128, 512], F32, es=p4c) for _ in range(2)]
        yt = [kb.sb([128, 512], F32, es=p4c) for _ in range(2)]
        wdv = w_down.re("(kt p) c -> p kt c", p=128)
        n6 = 0
        stg6 = [kb.sb([128, 4096 if not CAST_DMA else 2], F32, es=p4c) for _ in range(2)]
        wl6 = WLoader(kb, stg6, ('pool',))
        def load_dc(dc):
            wb = dc % 2
            for k8 in range(0, 44, 8):
                k9 = min(44, k8 + 8)
                wl6.load(Wd[wb][:, k8:k9, :], wdv[:, k8:k9, dc * 512:(dc + 1) * 512])

        load_dc(0)
        for dc in range(4):
            wb = dc % 2
            if dc + 1 < 4:
                load_dc(dc + 1)
            for q in seqs:
                if q.name not in SEQS_ON:
                    continue
                tl = [(t, nt) for (t, nt) in q.xt if t >= q.own0 and not (q.halo and t == q.own0)]
                for bi in range(0, len(tl), 4):
                    blk = tl[bi:bi + 4]
                    nb_ = sum(nt for _, nt in blk)
                    n6 += 1
                    hb6 = hidt[n6 % 2]
                    colb = cols0[q.name] + (blk[0][0] - q.own0) * 128
                    kb.dma(hb6[:, :, :nb_], hid_s[:, :, colb:colb + nb_].re("k p c -> p k c"), q='pool')
                    for j, (t, nt) in enumerate(blk):
                        n7[0] += 1
                        b = n7[0] % 2
                        ot = t - q.own0
                        c0 = ot * 128
                        yr0 = (ot - 1) * 128 if q.halo else 0
                        kb.dma(xnt[b][:nt], q.xn_s[c0:c0 + nt, dc * 512:(dc + 1) * 512], q='pool')
                        pb = psA[n7[0] % 4]
                        for kt in range(44):
                            kb.mm(pb[:nt], hb6[:, kt, j * 128:j * 128 + nt], Wd[wb][:, kt, :], kt == 0, kt == 43)
                        kb.tt(yt[b][:nt], pb[:nt], xnt[b][:nt], ALU.add)
                        kb.dma(q.o_y[yr0:yr0 + nt, dc * 512:(dc + 1) * 512], yt[b][:nt])

    return nc, kb, es, seqs, I, locals()


SEQS_ON = ('p', 's0', 's1')


def rope_tables(pos):
    half = 32
    inv = (1.0 / (10000.0 ** (np.arange(half, dtype=np.float32) / np.float32(half)))).astype(np.float32)
    ang = pos.astype(np.float32)[:, None] * inv[None, :]
    c = np.cos(ang).astype(np.float32)
    s = np.sin(ang).astype(np.float32)
    return np.concatenate([c, c], 1), np.concatenate([-s, s], 1)


def prep_inputs(inp):
    maps = []
    L = NT * 128
    idx = np.arange(128)
    tri = np.stack([
        (idx[:, None] <= idx[None, :]),
        (idx[:, None] > idx[None, :]),
        (idx[:, None] >= idx[None, :]),
        (idx[:, None] > idx[None, :]),
    ]).astype(np.float32)
    wnames = ['g_attn_norm', 'w_in', 'g_q_lat', 'g_kv_lat', 'w_q_up', 'w_kv_up', 'g_q_nope', 'g_q_rope', 'g_k_nope',
              'g_k_rope', 'w_gdn_conv', 'a_log', 'dt_bias', 'g_gdn_out', 'w_out', 'g_ffn_norm', 'w_ffn_gate',
              'w_ffn_up', 'w_ffn_conv', 'b_ffn_conv', 'w_ffn_down']
    W = {k: np.ascontiguousarray(inp[k][0]) for k in wnames}
    W['w_kv_up'] = W['w_kv_up'].reshape(512, 8 * 256)
    cs_s, sn_s = rope_tables(PAST + np.arange(16))
    for c in range(8):
        b, q = c // 4, c % 4
        nreal = 2048 * (q + 1)
        pad = L - nreal
        m = dict(W)
        xcv = np.zeros((L, D), np.float32)
        xcv[pad:] = inp['x_prompt'][b, :nreal]
        m['p_x'] = xcv
        pos = np.maximum(np.arange(L) - pad, 0)
        cs, sn = rope_tables(pos)
        m['p_cs'] = cs
        m['p_sn'] = sn
        km = np.where(np.arange(L) >= pad, 0.0, NEG).astype(np.float32)
        m['p_kmask'] = np.ascontiguousarray(km.reshape(NT, 128).T)
        m['ident'] = np.eye(128, dtype=np.float32)
        m['tri'] = tri
        for j in range(2):
            sb_ = 2 * c + j
            nm = 's%d' % j
            m[nm + '_x'] = np.ascontiguousarray(inp['x_sample'][sb_])
            m[nm + '_cs'] = cs_s
            m[nm + '_sn'] = sn_s
            m[nm + '_clat'] = np.ascontiguousarray(inp['cache_mla_latent'][0, sb_])
            m[nm + '_ckr'] = np.ascontiguousarray(inp['cache_mla_krope'][0, sb_])
            m[nm + '_cgconv'] = np.ascontiguousarray(inp['state_gdn_conv'][0, sb_])
            m[nm + '_cS'] = np.ascontiguousarray(inp['state_gdn_S'][0, sb_])
            m[nm + '_cfconv'] = np.ascontiguousarray(inp['state_ffn_conv'][0, sb_])
        maps.append(m)
    return maps


_CACHE = {}


def kernel(**inputs):
    inp = {k: np.asarray(v) for k, v in inputs.items()}
    if 'prog' not in _CACHE:
        r_ = build_program(stages=(1, 2, 3, 4))
        nc, kb, es, seqs, I = r_[:5]
        kb.S.finish()
        es.close()
        _CACHE['prog'] = (nc, I)
    nc, I = _CACHE['prog']
    maps = prep_inputs(inp)
    maps = [{k: v for k, v in m.items() if k in I} for m in maps]
    res = run_bass_kernel_spmd(nc, maps, core_ids=list(range(8)))
    r = res.results
    f32 = np.float32
    y_p = np.zeros((2, 8192, D), f32)
    y_s = np.zeros((16, 16, D), f32)
    p_lat = np.zeros((1, 2, 8192, 512), f32)
    p_kr = np.zeros((1, 2, 8192, 64), f32)
    p_gc = np.zeros((1, 2, 3, 3072), f32)
    p_S = np.zeros((1, 2, 8, 128, 128), f32)
    p_fc = np.zeros((1, 2, 2, DFF), f32)
    s_lat = np.zeros((1, 16, 16, 512), f32)
    s_kr = np.zeros((1, 16, 16, 64), f32)
    s_gc = np.zeros((1, 16, 3, 3072), f32)
    s_S = np.zeros((1, 16, 8, 128, 128), f32)
    s_fc = np.zeros((1, 16, 2, DFF), f32)
    own_r0 = (NT - 16) * 128
    for c in range(8):
        b, q = c // 4, c % 4
        rc = r[c]
        sl = slice(q * 2048, (q + 1) * 2048)
        y_p[b, sl] = np.asarray(rc['p_oy'])
        p_lat[0, b, sl] = np.asarray(rc['p_olat'])[own_r0:]
        p_kr[0, b, sl] = np.asarray(rc['p_okr'])[own_r0:]
        if q == 3:
            p_gc[0, b] = np.asarray(rc['p_ogconv'])
            p_S[0, b] = np.asarray(rc['p_oS'])
            p_fc[0, b] = np.asarray(rc['p_ofconv'])
        for j in range(2):
            sb_ = 2 * c + j
            nm = 's%d' % j
            y_s[sb_] = np.asarray(rc[nm + '_oy'])
            s_lat[0, sb_] = np.asarray(rc[nm + '_olat'])
            s_kr[0, sb_] = np.asarray(rc[nm + '_okr'])
            s_gc[0, sb_] = np.asarray(rc[nm + '_ogconv'])
            s_S[0, sb_] = np.asarray(rc[nm + '_oS'])
            s_fc[0, sb_] = np.asarray(rc[nm + '_ofconv'])
    return (y_p, y_s, p_lat, p_kr, p_gc, p_S, p_fc, s_lat, s_kr, s_gc, s_S, s_fc)
```

```python
import numpy as np
from contextlib import ExitStack
import concourse.bass as bass
import concourse.mybir as mybir
from concourse.bass_utils import run_bass_kernel_spmd

F32 = mybir.dt.float32
F32R = mybir.dt.float32r
BF16 = mybir.dt.bfloat16
AF = mybir.ActivationFunctionType
ALU = mybir.AluOpType
AX = mybir.AxisListType

D = 2048
NT = 64
OWN0 = 47
NOWN = NT - OWN0
EPS = 1e-6
H = 8
DIN = 5200
DFF = 5632
PAST = 4096
NEG = -30000.0


class V:
    def __init__(self, ap, h):
        self.ap = ap
        self.h = h

    def __getitem__(self, idx):
        return V(self.ap[idx], self.h)

    def sub(self, key):
        return V(self.ap, (self.h, key))

    def re(self, s, **kw):
        return V(self.ap.rearrange(s, **kw), self.h)

    def bc(self, shape):
        return V(self.ap.to_broadcast(list(shape)), self.h)

    def unsq(self, d):
        return V(self.ap.unsqueeze(d), self.h)

    def bitc(self, dt):
        return V(self.ap.bitcast(dt), self.h)

    def all2(self):
        return V(self.ap, ('__multi__', ((self.h, 0), (self.h, 1))))


class Sched:
    EPOCH = 16000
    R = 6

    def __init__(self, nc, es):
        self.nc = nc
        self.es = es
        self.engs = {'pe': nc.tensor, 'act': nc.scalar, 'dve': nc.vector, 'pool': nc.gpsimd, 'sp': nc.sync}
        self.cnt = {e: 0 for e in self.engs}
        self.sems = {}
        self.known = {e: {} for e in self.engs}
        self.lastw = {}
        self.readers = {}
        self.dma_cnt = {'sp': 0, 'pool': 0, 'act': 0}
        self.ninst = 0
        self.nwait = 0
        self.uid = 0
        self.rec = None

    def sem(self, name):
        if name not in self.sems:
            self.sems[name] = self.es.enter_context(self.nc.semaphore(name))
        return self.sems[name]

    def _wait(self, e, ev):
        name, val, src = ev
        if src == 'pe' and e == 'pe':
            return
        k = self.known[e]
        if k.get(name, 0) >= val:
            return
        self.engs[e].wait_ge(self.sem(name), val)
        self.nwait += 1
        k[name] = val

    def _deps(self, reads, writes):
        evs = []
        for h in reads:
            w = self.lastw.get(h)
            if w is not None:
                evs.append(w)
        for h in writes:
            w = self.lastw.get(h)
            if w is not None:
                evs.append(w)
            evs.extend(self.readers.get(h, {}).values())
        return evs

    def _record(self, ev, reads, writes):
        for h in writes:
            self.lastw[h] = ev
            self.readers[h] = {}
        for h in reads:
            if h in writes:
                continue
            self.readers.setdefault(h, {})[ev[2] + ev[0]] = ev

    @staticmethod
    def _flat(lst):
        out = []
        for r in lst:
            h = r.h if isinstance(r, V) else r
            if isinstance(h, tuple) and len(h) == 2 and h[0] == '__multi__':
                out.extend(h[1])
            else:
                out.append(h)
        return out

    def op(self, e, fn, reads, writes):
        if self.rec is not None:
            self.rec.append(('op', (e, fn, reads, writes)))
            return
        reads = self._flat(reads)
        writes = self._flat(writes)
        for ev in self._deps(reads, writes):
            self._wait(e, ev)
        ins = fn(self.engs[e])
        n = self.cnt[e]
        name = "%s_%d" % (e, n // self.EPOCH)
        val = n % self.EPOCH + 1
        ins.then_inc(self.sem(name), 1)
        self.cnt[e] = n + 1
        self.ninst += 1
        self._record((name, val, e), reads, writes)

    def record(self):
        self.rec = []

    def stop_record(self):
        r, self.rec = self.rec, None
        return r

    def replay(self, items):
        for kind, args in items:
            if kind == 'op':
                self.op(*args)
            else:
                q, out, in_, kw = args
                self.dma(q, out, in_, **kw)

    def dma(self, q, out, in_, **kw):
        if self.rec is not None:
            self.rec.append(('dma', (q, out, in_, kw)))
            return
        i = self.dma_cnt[q]
        name = "dq_%s_%d" % (q, i % self.R)
        prev = 16 * (i // self.R)
        if prev > 0:
            self._wait(q, (name, prev, 'dma'))
        reads = self._flat([in_])
        writes = self._flat([out])
        for ev in self._deps(reads, writes):
            self._wait(q, ev)
        self.engs[q].dma_start(out=out.ap, in_=in_.ap, **kw).then_inc(self.sem(name), 16)
        self.dma_cnt[q] = i + 1
        self.ninst += 1
        self._record((name, prev + 16, 'dma'), reads, writes)

    def barrier(self):
        evs = []
        for e, n in self.cnt.items():
            if n > 0:
                m = n - 1
                evs.append(("%s_%d" % (e, m // self.EPOCH), m % self.EPOCH + 1, 'x'))
        for q, i in self.dma_cnt.items():
            for r in range(self.R):
                cntr = (i - r + self.R - 1) // self.R
                if cntr > 0:
                    evs.append(("dq_%s_%d" % (q, r), 16 * cntr, 'dma'))
        for e in self.engs:
            for ev in evs:
                self._wait(e, ev)

    def finish(self):
        for q, i in self.dma_cnt.items():
            for r in range(min(i, self.R)):
                last = (i - 1 - r) // self.R + 1 if i - 1 >= r else 0
                cntr = (i - r + self.R - 1) // self.R
                if cntr > 0:
                    self._wait('sp', ("dq_%s_%d" % (q, r), 16 * cntr, 'dma'))


class KB:
    def __init__(self, nc, es):
        self.nc = nc
        self.es = es
        self.S = Sched(nc, es)
        self.n = 0

    def name(self, p):
        self.n += 1
        return "%s%d" % (p, self.n)

    def sb(self, shape, dt, es=None, name=None):
        nm = name or self.name("sb")
        t = (es or self.es).enter_context(self.nc.sbuf_tensor(nm, list(shape), dt))
        return V(t[:], nm)

    def ps(self, shape, dt, es=None, name=None):
        nm = name or self.name("ps")
        t = (es or self.es).enter_context(self.nc.psum_tensor(nm, list(shape), dt))
        return V(t[:], nm)

    def dram(self, name, shape, dt, kind="Internal"):
        t = self.nc.dram_tensor(name, list(shape), dt, kind=kind)
        return V(t.ap(), name)

    def mm(self, out, lhsT, rhs, start, stop):
        self.S.op('pe', lambda e: e.matmul(out.ap, lhsT.ap, rhs.ap, start=start, stop=stop),
                  [lhsT, rhs] + ([] if start else [out]), [out])

    def tr(self, out, in_, ident):
        self.S.op('pe', lambda e: e.transpose(out.ap, in_.ap, ident.ap), [in_, ident], [out])

    def act(self, out, in_, func, bias=None, scale=None, accum=None, eng='act'):
        kw = {}
        rd = [in_]
        if bias is not None:
            kw['bias'] = bias.ap if isinstance(bias, V) else bias
            if isinstance(bias, V):
                rd.append(bias)
        if scale is not None:
            kw['scale'] = scale.ap if isinstance(scale, V) else scale
            if isinstance(scale, V):
                rd.append(scale)
        wr = [out]
        if accum is not None:
            kw['accum_out'] = accum.ap
            wr.append(accum)
        self.S.op('act', lambda e: e.activation(out.ap, in_.ap, func, **kw), rd, wr)

    def tt(self, out, a, b, op, eng='dve'):
        self.S.op(eng, lambda e: e.tensor_tensor(out.ap, a.ap, b.ap, op), [a, b], [out])

    def ts(self, out, a, s1, s2, op0, op1=None, eng='dve'):
        rd = [a]
        if isinstance(s1, V):
            rd.append(s1)
        if isinstance(s2, V):
            rd.append(s2)
        v1 = s1.ap if isinstance(s1, V) else s1
        v2 = s2.ap if isinstance(s2, V) else s2
        if op1 is None:
            self.S.op(eng, lambda e: e.tensor_scalar(out.ap, a.ap, v1, None, op0), rd, [out])
        else:
            self.S.op(eng, lambda e: e.tensor_scalar(out.ap, a.ap, v1, v2, op0, op1), rd, [out])

    def stt(self, out, a, s, b, op0, op1):
        rd = [a, b]
        if isinstance(s, V):
            rd.append(s)
        sv = s.ap if isinstance(s, V) else s
        self.S.op('dve', lambda e: e.scalar_tensor_tensor(out.ap, a.ap, sv, b.ap, op0, op1), rd, [out])

    def cp(self, out, in_, eng='dve'):
        if eng == 'act':
            self.S.op('act', lambda e: e.activation(out.ap, in_.ap, AF.Copy), [in_], [out])
        else:
            self.S.op(eng, lambda e: e.tensor_copy(out.ap, in_.ap), [in_], [out])

    def recip(self, out, in_):
        self.S.op('dve', lambda e: e.reciprocal(out.ap, in_.ap), [in_], [out])

    def red(self, out, in_, op=ALU.add, axis=AX.X):
        self.S.op('dve', lambda e: e.tensor_reduce(out.ap, in_.ap, axis, op), [in_], [out])

    def memset(self, out, val, eng='dve'):
        self.S.op(eng, lambda e: e.memset(out.ap, val), [], [out])

    def dma(self, out, in_, q='sp', **kw):
        self.S.dma(q, out, in_, **kw)

    def rstd(self, out, ss, n, tmp):
        self.act(tmp, ss, AF.Ln, bias=EPS, scale=1.0 / n)
        self.act(out, tmp, AF.Exp, scale=-0.5)


class Seq:
    pass


class Phase(ExitStack):
    def __init__(self, kb):
        super().__init__()
        self.kb = kb

    def __exit__(self, *a):
        self.kb.S.barrier()
        return super().__exit__(*a)


def load_bc(kb, dst, src, q='sp'):
    parts = dst.ap.shape[0]
    kb.dma(dst, V(src.ap.partition_broadcast(parts), src.h), q=q)


WQ_ = 'pool'
CAST_DMA = True


class WLoader:
    def __init__(self, kb, stg, engs=('pool',)):
        self.kb, self.stg, self.engs, self.n = kb, stg, engs, 0

    def load(self, dst, src):
        a, c = dst.ap.shape[1], dst.ap.shape[2]
        if CAST_DMA:
            self.kb.dma(dst, src, q='pool')
            self.n += 1
            return
        st = self.stg[self.n % len(self.stg)]
        sv = st[:, 0:a * c].re("p (a c) -> p a c", a=a)
        self.kb.dma(sv, src, q=WQ_)
        self.kb.cp(dst, sv, eng=self.engs[self.n % len(self.engs)])
        self.n += 1


def build_program(stages=(1, 2, 3, 4), debug=False):
    nc = bass.Bass("TRN2", target_bir_lowering=False)
    es = ExitStack()
    kb = KB(nc, es)
    I = {}

    def inp(name, shape, dt=F32):
        I[name] = kb.dram(name, shape, dt, kind="ExternalInput")
        return I[name]

    def outp(name, shape, dt=F32):
        I[name] = kb.dram(name, shape, dt, kind="ExternalOutput")
        return I[name]

    def scr(name, shape, dt):
        if debug:
            return outp(name, shape, dt)
        return kb.dram(name, shape, dt)

    ident_in = inp("ident", [128, 128])
    tri_in = inp("tri", [4, 128, 128])
    g_attn = inp("g_attn_norm", [D])
    w_in = inp("w_in", [D, DIN])
    g_q_lat = inp("g_q_lat", [512])
    g_kv_lat = inp("g_kv_lat", [512])
    w_q_up = inp("w_q_up", [512, 1536])
    w_kv_up = inp("w_kv_up", [512, 8 * 256])
    g_q_nope = inp("g_q_nope", [128])
    g_q_rope = inp("g_q_rope", [64])
    g_k_nope = inp("g_k_nope", [128])
    g_k_rope = inp("g_k_rope", [64])
    w_gdn_conv = inp("w_gdn_conv", [4, 3072])
    a_log = inp("a_log", [8])
    dt_bias = inp("dt_bias", [8])
    g_gdn_out = inp("g_gdn_out", [128])
    w_out = inp("w_out", [D, D])
    g_ffn = inp("g_ffn_norm", [D])
    w_gate = inp("w_ffn_gate", [D, DFF])
    w_up = inp("w_ffn_up", [D, DFF])
    w_fconv = inp("w_ffn_conv", [3, DFF])
    b_fconv = inp("b_ffn_conv", [DFF])
    w_down = inp("w_ffn_down", [DFF, D])

    seqs = []
    for si, nm in enumerate(['p', 's0', 's1']):
        q = Seq()
        q.name = nm
        q.prompt = (si == 0)
        if q.prompt:
            q.ntc, q.ncache, q.own0 = NT, 0, OWN0
            q.xt = [(t, 128) for t in range(NT)]
            q.ntok = NT * 128
            q.halo = True
        else:
            q.ntc, q.ncache, q.own0 = 33, 32, 32
            q.xt = [(32, 16)]
            q.ntok = 16
            q.halo = False
        q.nown = q.ntc - q.own0
        q.nkeys = q.ncache * 128 + q.ntok
        q.x = inp(nm + "_x", [q.ntok, D])
        q.cs = inp(nm + "_cs", [q.ntok, 64])
        q.sn = inp(nm + "_sn", [q.ntok, 64])
        if q.prompt:
            q.kmask = inp(nm + "_kmask", [128, NT])
        else:
            q.c_lat = inp(nm + "_clat", [PAST, 512])
            q.c_kr = inp(nm + "_ckr", [PAST, 64])
            q.c_gconv = inp(nm + "_cgconv", [3, 3072])
            q.c_S = inp(nm + "_cS", [8, 128, 128])
            q.c_fconv = inp(nm + "_cfconv", [2, DFF])
        q.o_lat = outp(nm + "_olat", [q.ntok, 512])
        q.o_kr = outp(nm + "_okr", [q.ntok, 64])
        q.o_gconv = outp(nm + "_ogconv", [3, 3072])
        q.o_S = outp(nm + "_oS", [8, 128, 128])
        q.o_fconv = outp(nm + "_ofconv", [2, DFF])
        q.ny = (q.nown - 1) * 128 if q.prompt else 16
        q.o_y = outp(nm + "_oy", [q.ny, D])
        nx = len(q.xt)
        q.hT_s = scr(nm + "_hT", [nx, 128, 16 * 128], BF16)
        q.ckvT_s = scr(nm + "_ckvT", [128, 4, q.ntc * 128], BF16)
        q.krT_s = scr(nm + "_krT", [64, q.ntc * 128], BF16)
        q.gb_s = scr(nm + "_gb", [nx * 128, 16], F32)
        q.qnT_s = scr(nm + "_qnT", [128, 8, q.nown * 128], BF16)
        q.qrT_s = scr(nm + "_qrT", [64, 8, q.nown * 128], BF16)
        q.z_s = scr(nm + "_z", [q.nown * 128, 1024], F32)
        q.oT_s = scr(nm + "_oT", [16, 128, q.nown * 128], BF16)
        q.xn_s = scr(nm + "_xn", [q.nown * 128, D], F32)
        seqs.append(q)

    ident_f = kb.sb([128, 128], F32)
    ident = kb.sb([128, 128], BF16)
    kb.dma(ident_f, ident_in)
    kb.cp(ident, ident_f)
    ones_f = kb.sb([128, 128], F32)
    kb.memset(ones_f, 1.0)
    ones_b = kb.sb([128, 128], BF16)
    kb.memset(ones_b, 1.0)

    psA = [kb.ps([128, 512], F32) for _ in range(6)]
    psT = [kb.ps([128, 1024], BF16) for _ in range(2)]
    w_in_v = w_in.re("(kt p) c -> p kt c", p=128)
    rr = [0]

    def evac_eng():
        rr[0] += 1
        return 'act' if rr[0] % 2 else 'dve'

    if 1 in stages:
      with Phase(kb) as p1:
        WA = kb.sb([128, 16, 2128], BF16, es=p1)
        WQ = kb.sb([128, 4, 1536], BF16, es=p1)
        stg1 = [kb.sb([128, 2048 if not CAST_DMA else 2], F32, es=p1) for _ in range(2)]
        wl = WLoader(kb, stg1, ('pool', 'dve', 'act'))
        WAkv, WAown = WA.sub('kv'), WA.sub('own')
        for k2 in range(8):
            wl.load(WAkv[:, 2 * k2:2 * k2 + 2, 512:1088], w_in_v[:, 2 * k2:2 * k2 + 2, 512:1088])
        for k8 in range(2):
            wl.load(WAkv[:, 8 * k8:8 * k8 + 8, 2112:2128], w_in_v[:, 8 * k8:8 * k8 + 8, 5184:5200])
        wl.engs = ('pool',)
        for k4 in range(4):
            wl.load(WAown[:, 4 * k4:4 * k4 + 4, 0:512], w_in_v[:, 4 * k4:4 * k4 + 4, 0:512])
        for k2 in range(8):
            wl.load(WAown[:, 2 * k2:2 * k2 + 2, 1088:2112], w_in_v[:, 2 * k2:2 * k2 + 2, 4160:5184])
        wqv = w_q_up.re("(kt p) c -> p kt c", p=128)
        for k1 in range(4):
            wl.load(WQ[:, k1:k1 + 1, :], wqv[:, k1:k1 + 1, :])
        g_attn_bc = kb.sb([128, D], F32, es=p1)
        load_bc(kb, g_attn_bc, g_attn)
        g_kv_bc = kb.sb([128, 512], F32, es=p1)
        load_bc(kb, g_kv_bc, g_kv_lat)
        g_ql_bc = kb.sb([128, 512], F32, es=p1)
        load_bc(kb, g_ql_bc, g_q_lat)
        g_kr_bc = kb.sb([128, 64], F32, es=p1)
        load_bc(kb, g_kr_bc, g_k_rope)
        g_qn_bc = kb.sb([128, 128], F32, es=p1)
        load_bc(kb, g_qn_bc, g_q_nope)
        g_qr_bc = kb.sb([128, 64], F32, es=p1)
        load_bc(kb, g_qr_bc, g_q_rope)
        alog_bc = kb.sb([128, 8], F32, es=p1)
        load_bc(kb, alog_bc, a_log)
        dtb_bc = kb.sb([128, 8], F32, es=p1)
        load_bc(kb, dtb_bc, dt_bias)
        negA = kb.sb([128, 8], F32, es=p1)
        kb.act(negA, alog_bc, AF.Exp)
        kb.ts(negA, negA, -1.0, None, ALU.mult)

        xt = [kb.sb([128, D], F32, es=p1) for _ in range(2)]
        junk_2 = [kb.sb([128, D], BF16, es=p1) for _ in range(2)]
        hb = [kb.sb([128, D], BF16, es=p1) for _ in range(2)]
        hT = [kb.sb([128, 16, 128], BF16, es=p1) for _ in range(2)]
        sm_2 = [[kb.sb([128, 8], F32, es=p1) for _ in range(4)] for _ in range(2)]
        ckv = [kb.sb([128, 512], F32, es=p1) for _ in range(2)]
        ckvb_2 = [kb.sb([128, 512], BF16, es=p1) for _ in range(2)]
        ckvT = [kb.sb([128, 4, 128], BF16, es=p1) for _ in range(2)]
        krn_2 = [kb.sb([128, 64], F32, es=p1) for _ in range(2)]
        kro = [kb.sb([128, 64], F32, es=p1) for _ in range(2)]
        krt_2 = [kb.sb([128, 64], F32, es=p1) for _ in range(2)]
        krb_2 = [kb.sb([128, 64], BF16, es=p1) for _ in range(2)]
        krT = [kb.sb([64, 128], BF16, es=p1) for _ in range(2)]
        gbt = [kb.sb([128, 16], F32, es=p1) for _ in range(2)]
        gtmp_2 = [kb.sb([128, 8], F32, es=p1) for _ in range(2)]
        cst = [kb.sb([128, 64], F32, es=p1) for _ in range(2)]
        snt = [kb.sb([128, 64], F32, es=p1) for _ in range(2)]
        qan_2 = [kb.sb([128, 512], BF16, es=p1) for _ in range(2)]
        qanT_2 = [kb.sb([128, 4, 128], BF16, es=p1) for _ in range(2)]
        qf_2 = [kb.sb([128, 8, 192], F32, es=p1) for _ in range(2)]
        qsq_2 = [kb.sb([128, 8, 192], F32, es=p1) for _ in range(2)]
        qst_2 = [[kb.sb([128, 8], F32, es=p1) for _ in range(6)] for _ in range(2)]
        qnf_2 = [v[:, :, 0:128] for v in qsq_2]
        qnb_2 = [kb.sb([128, 8, 128], BF16, es=p1) for _ in range(2)]
        qrf_2 = [kb.sb([128, 8, 64], F32, es=p1) for _ in range(2)]
        qrt_2 = [kb.sb([128, 8, 64], F32, es=p1) for _ in range(2)]
        qro_2 = [kb.sb([128, 8, 64], F32, es=p1) for _ in range(2)]
        qrb_2 = [kb.sb([128, 8, 64], BF16, es=p1) for _ in range(2)]
        qnT = [kb.sb([128, 8, 128], BF16, es=p1) for _ in range(2)]
        qrT = [kb.sb([64, 8, 128], BF16, es=p1) for _ in range(2)]
        zs = [kb.sb([128, 1024], F32, es=p1) for _ in range(2)]

        def stageA(q, ti):
            t, nt = q.xt[ti]
            b = ti % 2
            junk, ckvb, krn, krt, krb, gtmp, qan, qanT, qf, qsq, qnf, qnb, qrf, qrt, qro, qrb, sm, qst = [v[b] for v in (junk_2, ckvb_2, krn_2, krt_2, krb_2, gtmp_2, qan_2, qanT_2, qf_2, qsq_2, qnf_2, qnb_2, qrf_2, qrt_2, qro_2, qrb_2, sm_2, qst_2)]
            r0 = ti * 128
            kb.dma(xt[b][:nt], q.x[r0:r0 + nt, :])
            kb.dma(cst[ti % 2][:nt], q.cs[r0:r0 + nt, :])
            kb.dma(snt[ti % 2][:nt], q.sn[r0:r0 + nt, :])
            ss, tmp, rs = sm[0][:nt, 0:1], sm[0][:nt, 1:2], sm[0][:nt, 2:3]
            kb.act(junk[:nt], xt[b][:nt], AF.Square, accum=ss)
            kb.rstd(rs, ss, D, tmp)
            kb.stt(hb[b][:nt], xt[b][:nt], rs, g_attn_bc[:nt], ALU.mult, ALU.mult)

        def stageB(q, ti):
            t, nt = q.xt[ti]
            b = ti % 2
            junk, ckvb, krn, krt, krb, gtmp, qan, qanT, qf, qsq, qnf, qnb, qrf, qrt, qro, qrb, sm, qst = [v[b] for v in (junk_2, ckvb_2, krn_2, krt_2, krb_2, gtmp_2, qan_2, qanT_2, qf_2, qsq_2, qnf_2, qnb_2, qrf_2, qrt_2, qro_2, qrb_2, sm_2, qst_2)]
            r0 = ti * 128
            for half in range(2):
                for k in range(8):
                    kt = half * 8 + k
                    kb.tr(psT[b][:, k * 128:k * 128 + nt], hb[b][:nt, kt * 128:(kt + 1) * 128], ident[:nt, :nt])
                kb.cp(hT[b][:, half * 8:half * 8 + 8, :nt], psT[b].re("p (k c) -> p k c", k=8)[:, :, :nt],
                      eng='act' if half == 0 else 'dve')
            kb.dma(q.hT_s[ti].re("p (k c) -> p k c", k=16)[:, :, :nt], hT[b][:, :, :nt])
            pk0, pk1 = (psA[0], psA[1]) if ti % 2 == 0 else (psA[2], psA[3])
            for kt in range(16):
                kb.mm(pk0[:nt], hT[b][:, kt, :nt], WAkv[:, kt, 512:1024], kt == 0, kt == 15)
            for kt in range(16):
                kb.mm(pk1[:nt, 0:64], hT[b][:, kt, :nt], WAkv[:, kt, 1024:1088], kt == 0, kt == 15)
            for kt in range(16):
                kb.mm(pk1[:nt, 64:80], hT[b][:, kt, :nt], WAkv[:, kt, 2112:2128], kt == 0, kt == 15)

        def stageB2(q, ti):
            t, nt = q.xt[ti]
            b = ti % 2
            junk, ckvb, krn, krt, krb, gtmp, qan, qanT, qf, qsq, qnf, qnb, qrf, qrt, qro, qrb, sm, qst = [v[b] for v in (junk_2, ckvb_2, krn_2, krt_2, krb_2, gtmp_2, qan_2, qanT_2, qf_2, qsq_2, qnf_2, qnb_2, qrf_2, qrt_2, qro_2, qrb_2, sm_2, qst_2)]
            b3 = ti % 2
            r0 = ti * 128
            pk0, pk1 = (psA[0], psA[1]) if ti % 2 == 0 else (psA[2], psA[3])
            ss, tmp, rs = sm[1][:nt, 0:1], sm[1][:nt, 1:2], sm[1][:nt, 2:3]
            kb.act(junk[:nt, 0:512], pk0[:nt], AF.Square, accum=ss)
            kb.rstd(rs, ss, 512, tmp)
            kb.stt(ckv[b][:nt], pk0[:nt], rs, g_kv_bc[:nt], ALU.mult, ALU.mult)
            kb.dma(q.o_lat[r0:r0 + nt, :], ckv[b][:nt])
            kb.cp(ckvb[:nt], ckv[b][:nt], eng='act')
            for k in range(4):
                kb.tr(psT[b][:, k * 128:k * 128 + nt], ckvb[:nt, k * 128:(k + 1) * 128], ident[:nt, :nt])
            kb.cp(ckvT[b][:, :, :nt], psT[b][:, 0:512].re("p (k c) -> p k c", k=4)[:, :, :nt], eng='act')
            kb.dma(q.ckvT_s[:, :, t * 128:t * 128 + nt], ckvT[b][:, :, :nt])
            ss, tmp, rs = sm[2][:nt, 0:1], sm[2][:nt, 1:2], sm[2][:nt, 2:3]
            kb.act(junk[:nt, 512:576], pk1[:nt, 0:64], AF.Square, accum=ss)
            kb.rstd(rs, ss, 64, tmp)
            kb.stt(krn[:nt], pk1[:nt, 0:64], rs, g_kr_bc[:nt], ALU.mult, ALU.mult)
            kb.tt(kro[b][:nt], krn[:nt], cst[b3][:nt], ALU.mult)
            kb.tt(krt[:nt, 0:32], krn[:nt, 32:64], snt[b3][:nt, 0:32], ALU.mult)
            kb.tt(krt[:nt, 32:64], krn[:nt, 0:32], snt[b3][:nt, 32:64], ALU.mult)
            kb.tt(kro[b][:nt], kro[b][:nt], krt[:nt], ALU.add)
            kb.dma(q.o_kr[r0:r0 + nt, :], kro[b][:nt])
            kb.cp(krb[:nt], kro[b][:nt], eng='act')
            kb.tr(psT[b][0:64, 0:nt], krb[:nt], ident[:nt, :nt])
            kb.cp(krT[b][:, :nt], psT[b][0:64, 0:nt], eng='act')
            kb.dma(q.krT_s[:, t * 128:t * 128 + nt], krT[b][:, :nt])
            kb.tt(gtmp[:nt], pk1[:nt, 64:72], dtb_bc[:nt], ALU.add)
            kb.act(gtmp[:nt], gtmp[:nt], AF.Exp)
            kb.act(gtmp[:nt], gtmp[:nt], AF.Ln, bias=1.0)
            kb.tt(gbt[b][:nt, 0:8], gtmp[:nt], negA[:nt], ALU.mult)
            kb.act(gbt[b][:nt, 8:16], pk1[:nt, 72:80], AF.Sigmoid)
            kb.dma(q.gb_s[r0:r0 + nt, :], gbt[b][:nt])
            if t < q.own0:
                return
            ot = t - q.own0
            c0 = ot * 128
            pown = psA[4 + b]
            for hh in range(2):
                for kt in range(16):
                    kb.mm(pown[:nt], hT[b][:, kt, :nt], WAown[:, kt, 1088 + hh * 512:1088 + (hh + 1) * 512], kt == 0, kt == 15)
                kb.act(zs[b][:nt, hh * 512:(hh + 1) * 512], pown[:nt], AF.Silu)
            kb.dma(q.z_s[c0:c0 + nt, :], zs[b][:nt])
            pq = pown
            for kt in range(16):
                kb.mm(pq[:nt], hT[b][:, kt, :nt], WAown[:, kt, 0:512], kt == 0, kt == 15)
            ss, tmp, rs = sm[3][:nt, 0:1], sm[3][:nt, 1:2], sm[3][:nt, 2:3]
            kb.act(junk[:nt, 0:512], pq[:nt], AF.Square, accum=ss)
            kb.rstd(rs, ss, 512, tmp)
            kb.stt(qan[:nt], pq[:nt], rs, g_ql_bc[:nt], ALU.mult, ALU.mult)
            for k in range(4):
                kb.tr(psT[b][:, k * 128:k * 128 + nt], qan[:nt, k * 128:(k + 1) * 128], ident[:nt, :nt])
            kb.cp(qanT[:, :, :nt], psT[b][:, 0:512].re("p (k c) -> p k c", k=4)[:, :, :nt], eng='dve')
            qfl = qf.re("p h c -> p (h c)")
            for j, pb in enumerate((pown, pown, pown)):
                for kt in range(4):
                    kb.mm(pb[:nt], qanT[:, kt, :nt], WQ[:, kt, j * 512:(j + 1) * 512], kt == 0, kt == 3)
                kb.cp(qfl[:nt, j * 512:(j + 1) * 512], pb[:nt], eng='act' if j % 2 == 0 else 'dve')
            kb.tt(qsq[:nt], qf[:nt], qf[:nt], ALU.mult, eng='pool')
            kb.red(qst[0][:nt], qsq[:nt, :, 0:128])
            kb.red(qst[1][:nt], qsq[:nt, :, 128:192])
            kb.rstd(qst[2][:nt], qst[0][:nt], 128, qst[4][:nt])
            kb.rstd(qst[3][:nt], qst[1][:nt], 64, qst[5][:nt])
            kb.tt(qnf[:nt], qf[:nt, :, 0:128], qst[2][:nt].unsq(2).bc([nt, 8, 128]), ALU.mult)
            kb.tt(qnb[:nt], qnf[:nt], g_qn_bc[:nt].unsq(1).bc([nt, 8, 128]), ALU.mult)
            kb.tt(qrf[:nt], qf[:nt, :, 128:192], qst[3][:nt].unsq(2).bc([nt, 8, 64]), ALU.mult)
            kb.tt(qrf[:nt], qrf[:nt], g_qr_bc[:nt].unsq(1).bc([nt, 8, 64]), ALU.mult)
            kb.tt(qro[:nt], qrf[:nt], cst[b3][:nt].unsq(1).bc([nt, 8, 64]), ALU.mult)
            kb.tt(qrt[:nt, :, 0:32], qrf[:nt, :, 32:64], snt[b3][:nt, 0:32].unsq(1).bc([nt, 8, 32]), ALU.mult)
            kb.tt(qrt[:nt, :, 32:64], qrf[:nt, :, 0:32], snt[b3][:nt, 32:64].unsq(1).bc([nt, 8, 32]), ALU.mult)
            kb.tt(qrb[:nt], qro[:nt], qrt[:nt], ALU.add)
            for hh in range(8):
                kb.tr(psT[b][:, hh * 128:hh * 128 + nt], qnb[:nt, hh, :], ident[:nt, :nt])
            kb.cp(qnT[b][:, :, :nt], psT[b].re("p (k c) -> p k c", k=8)[:, :, :nt], eng='act')
            kb.dma(q.qnT_s[:, :, c0:c0 + nt], qnT[b][:, :, :nt])
            for hh in range(8):
                kb.tr(psT[b][0:64, hh * 128:hh * 128 + nt], qrb[:nt, hh, :], ident[:nt, :nt])
            kb.cp(qrT[b][:, :, :nt], psT[b][0:64, :].re("p (k c) -> p k c", k=8)[:, :, :nt], eng='dve')
            kb.dma(q.qrT_s[:, :, c0:c0 + nt], qrT[b][:, :, :nt])

        ck4 = [v.bitc(BF16)[:, 0:2048].re("p (t c) -> p t c", t=4) for v in xt]
        ckT4 = [v.re("p (t c) -> p t c", t=4) for v in hb]
        kr4 = [v[:, 0:256].re("p (t c) -> p t c", t=4) for v in junk_2]
        krT4 = [v[0:64, 512:1024] for v in junk_2]
        psK = psA[5].bitc(BF16)

        def cached_group(q, gi):
            b = gi % 2
            t0 = gi * 4
            r0, r1 = t0 * 128, (t0 + 4) * 128
            kb.dma(ck4[b], q.c_lat[r0:r1, :].re("(t p) c -> p t c", p=128), q='pool')
            kb.dma(kr4[b], q.c_kr[r0:r1, :].re("(t p) c -> p t c", p=128), q='pool')
            for half in range(2):
                for kk in range(2):
                    k = half * 2 + kk
                    for j in range(4):
                        kb.tr(psT[half][:, kk * 512 + j * 128:kk * 512 + (j + 1) * 128], ck4[b][:, j, k * 128:(k + 1) * 128], ident)
                kb.cp(ckT4[b][:, half * 2:half * 2 + 2, :], psT[half].re("p (k c) -> p k c", k=2), eng='act' if half == 0 else 'dve')
            kb.dma(q.ckvT_s[:, :, r0:r1], ckT4[b])
            for j in range(4):
                kb.tr(psK[0:64, j * 128:(j + 1) * 128], kr4[b][:, j, :], ident)
            kb.cp(krT4[b], psK[0:64, 0:512], eng='dve')
            kb.dma(q.krT_s[:, r0:r1], krT4[b])

        for q in seqs:
            if q.name not in SEQS_ON:
                continue
            for gi in range(q.ncache // 4):
                cached_group(q, gi)
            n = len(q.xt)
            streams = [[], []]
            for ti in range(n):
                kb.S.record()
                stageA(q, ti)
                stageB(q, ti)
                stageB2(q, ti)
                streams[ti % 2].extend(kb.S.stop_record())
            s0, s1 = streams
            off = min(len(s1), 40)
            merged = list(s0[:off])
            i0, i1 = off, 0
            while i0 < len(s0) or i1 < len(s1):
                if i1 < len(s1):
                    merged.append(s1[i1])
                    i1 += 1
                if i0 < len(s0):
                    merged.append(s0[i0])
                    i0 += 1
            kb.S.replay(merged)

    if 2 in stages:
      with Phase(kb) as p2:
        WGkv = kb.sb([128, 16, 2048], BF16, es=p2)
        for k2 in range(8):
            kb.dma(WGkv[:, 2 * k2:2 * k2 + 2, :], w_in_v[:, 2 * k2:2 * k2 + 2, 2112:4160], q='pool')
        tri = kb.sb([128, 4, 128], F32, es=p2)
        kb.dma(tri, tri_in.re("k p c -> p k c"))
        LinclT, Umat, Mincl, Mstrict = tri[:, 0, :], tri[:, 1, :], tri[:, 2, :], tri[:, 3, :]
        g_go_bc = kb.sb([128, 128], F32, es=p2)
        load_bc(kb, g_go_bc, g_gdn_out)
        h3o = kb.sb([4, 3072], F32, es=p2)
        wc4 = h3o
        kb.dma(wc4, w_gdn_conv)
        wconv = kb.sb([128, 24, 4], F32, es=p2)
        for ct in range(24):
            kb.tr(psA[ct % 2][:, 0:4], wc4[:, ct * 128:(ct + 1) * 128], ident_f[:4, :4])
            kb.cp(wconv[:, ct, :], psA[ct % 2][:, 0:4], eng=evac_eng())
        hist3 = kb.sb([128, 24, 3], F32, es=p2)
        Sm = kb.sb([128, 8, 128], F32, es=p2)
        Sb = kb.sb([128, 8, 128], BF16, es=p2)
        ktm = kb.sb([128, 8, 128], BF16, es=p2)
        vtm = kb.sb([128, 8, 128], BF16, es=p2)
        qtm = kb.sb([128, 8, 128], BF16, es=p2)
        st = [kb.sb([128, 8], F32, es=p2) for _ in range(8)]
        kn = kb.sb([128, 8, 128], BF16, es=p2)
        qn = kb.sb([128, 8, 128], BF16, es=p2)
        qg = qtm
        knT = kb.sb([128, 8, 128], BF16, es=p2)
        qnT2 = kb.sb([128, 8, 128], BF16, es=p2)
        qgT = kb.sb([128, 8, 128], BF16, es=p2)
        gbt2 = [kb.sb([128, 16], F32, es=p2) for _ in range(2)]
        gs = kb.sb([128, 8, 8], F32, es=p2)
        rhsg = kb.sb([128, 8, 128], F32, es=p2)
        sq = rhsg
        Dm = kb.sb([128, 8, 128], F32, es=p2)
        NB = rhsg
        NbR = [kb.sb([128, 8, 128], F32R, es=p2) for _ in range(2)]
        YbR = [kb.sb([128, 8, 128], F32R, es=p2) for _ in range(2)]
        Nb16 = [kb.sb([128, 8, 16], BF16, es=p2) for _ in range(2)]
        Yb16 = [kb.sb([128, 8, 16], BF16, es=p2) for _ in range(2)]
        ident_r = kb.sb([128, 128], F32R, es=p2)
        kb.cp(ident_r, ident_f)
        intra = kb.sb([128, 8, 128], BF16, es=p2)
        intraT = kb.sb([128, 8, 128], BF16, es=p2)
        DmA = Dm.all2()
        DmH = [Dm[:, 0:4, :].sub(0), Dm[:, 4:8, :].sub(1)]
        TT = Dm
        onf = DmA
        TTbR = kb.sb([128, 8, 128], F32R, es=p2)
        RuR = kb.sb([128, 8, 128], F32R, es=p2)
        RwR = kb.sb([128, 8, 128], F32R, es=p2)
        TTb16 = kb.sb([128, 8, 16], BF16, es=p2)
        Ru16 = kb.sb([128, 8, 128], BF16, es=p2)
        Rw16 = kb.sb([128, 8, 128], BF16, es=p2)
        ub = rhsg
        wT = Rw16
        vn = ktm
        kd = Ru16
        zt = [ub.re('p h c -> p (h c)')] * 2
        ob = intra
        obT = [qnT2] * 2

        def ps3(bank, nt_, n=4):
            return bank.re("p (h j) -> p h j", h=n)

        def headmm(banks, fn):
            for hh in range(8):
                fn(hh, banks[hh // 4], (hh % 4) * 128)

        def evac2(dst, banks, nt_, cols, eng0=None):
            for half in range(2):
                kb.cp(dst[:nt_, half * 4:half * 4 + 4, :cols], ps3(banks[half], nt_)[:nt_, :, :cols],
                      eng=('act' if half == 0 else 'dve') if eng0 is None else eng0)

        def trans8(dst, src, nt_in, nparts_out, bank):
            for hh in range(8):
                kb.tr(bank[:nparts_out, hh * 128:hh * 128 + nt_in], src[:nt_in, hh, :nparts_out], ident[:nt_in, :nt_in])
            kb.cp(dst[:nparts_out, :, :nt_in], bank.re("p (h c) -> p h c", h=8)[:nparts_out, :, :nt_in], eng=evac_eng())

        def gdn_tile(q, t, nt, j, use_q, qk):
            ti = t - q.ncache
            hp = (nt == 128)
            Nb, Yb = (NbR, YbR) if hp else (Nb16, Yb16)
            TTb, Ru, Rw = (TTbR, RuR, RwR) if hp else (TTb16, Ru16, Rw16)
            rd = (lambda v: v.bitc(F32)) if hp else (lambda v: v)
            gb = gbt2[ti % 2]
            kb.dma(gb[:nt], q.gb_s[ti * 128:ti * 128 + nt, :])
            g = gb[:, 0:8]
            beta = gb[:, 8:16]
            qkvT, kof, vof, qof = qk
            groups = [(kof, ktm, psT[0]), (vof, vtm, psT[1])]
            if use_q:
                groups.append((qof, qtm, psT[0]))
            for c0_, dst, bank in groups:
                for hh in range(8):
                    kb.tr(bank[:nt, hh * 128:(hh + 1) * 128], qkvT[:, c0_ + hh, j * 128:j * 128 + nt], ident)
                kb.cp(dst[:nt], bank.re("p (h c) -> p h c", h=8)[:nt], eng=evac_eng())
            kb.tt(sq[:nt], ktm[:nt], ktm[:nt], ALU.mult, eng='pool')
            kb.red(st[0][:nt], sq[:nt])
            kb.rstd(st[1][:nt], st[0][:nt], 1.0, st[2][:nt])
            kb.tt(kn[:nt], ktm[:nt], st[1][:nt].unsq(2).bc([nt, 8, 128]), ALU.mult)
            if use_q:
                kb.tt(sq[:nt], qtm[:nt], qtm[:nt], ALU.mult, eng='pool')
                kb.red(st[3][:nt], sq[:nt])
                kb.rstd(st[4][:nt], st[3][:nt], 1.0, st[5][:nt])
                kb.ts(st[4][:nt], st[4][:nt], 128.0 ** -0.5, None, ALU.mult)
                kb.tt(qn[:nt], qtm[:nt], st[4][:nt].unsq(2).bc([nt, 8, 128]), ALU.mult)
            pg = psA[0]
            kb.mm(pg[:nt, 0:8], LinclT[:nt, :nt], g[:nt], True, True)
            kb.mm(pg[:, 8:16], ones_f[:nt, :], g[:nt], True, True)
            gc, egc, gtot, elast, kdec, nbeta, bexp, gtmp2 = [gs[:, k_, :] for k_ in range(8)]
            kb.cp(gc[:nt], pg[:nt, 0:8], eng='dve')
            kb.cp(gtot, pg[:, 8:16], eng='dve')
            kb.act(egc[:nt], gc[:nt], AF.Exp)
            kb.act(elast, gtot, AF.Exp)
            kb.tt(gtmp2[:nt], gtot[:nt], gc[:nt], ALU.subtract)
            kb.act(kdec[:nt], gtmp2[:nt], AF.Exp)
            kb.ts(nbeta[:nt], beta[:nt], -1.0, None, ALU.mult)
            kb.tt(bexp[:nt], beta[:nt], egc[:nt], ALU.mult)
            kb.tt(rhsg[:nt, :, :nt], Umat[:nt, :nt].unsq(1).bc([nt, 8, nt]), g[:nt].unsq(2).bc([nt, 8, nt]), ALU.mult,
                  eng='pool')
            pd = (psA[2], psA[3])
            for half in range(2):
                kb.mm(ps3(pd[half], nt)[:nt, :, :nt], LinclT[:nt, :nt], rhsg[:nt, half * 4:half * 4 + 4, :nt], True, True)
                kb.act(DmH[half][:nt, :, :nt], ps3(pd[half], nt)[:nt, :, :nt], AF.Exp)
            kb.tt(DmA[:nt, :, :nt], DmA[:nt, :, :nt], Mincl[:nt, :nt].unsq(1).bc([nt, 8, nt]), ALU.mult, eng='pool')
            kb.tt(NB[:nt, :, :nt], DmA[:nt, :, :nt], Mstrict[:nt, :nt].unsq(1).bc([nt, 8, nt]), ALU.mult, eng='pool')
            kb.tt(NB[:nt, :, :nt], NB[:nt, :, :nt], nbeta[:nt].unsq(2).bc([nt, 8, nt]), ALU.mult, eng='pool')
            trans8(knT, kn, nt, 128, psT[1])
            pkk = (psA[4], psA[5])
            headmm(pkk, lambda hh, bank, c: kb.mm(bank[:nt, c:c + nt], knT[:, hh, :nt], knT[:, hh, :nt], True, True))
            hv = lambda v: [v[:, 0:4, :].sub(0), v[:, 4:8, :].sub(1)]
            NbH = [hv(Nb[0]), hv(Nb[1])]
            YbH = [hv(Yb[0]), hv(Yb[1])]
            TTH = hv(TT)
            TTbH = hv(TTb)
            for half in range(2):
                kb.tt(NbH[0][half][:nt, :, :nt], ps3(pkk[half], nt)[:nt, :, :nt],
                      NB[:nt, half * 4:half * 4 + 4, :nt], ALU.mult)
            if hp:
                for hh in range(8):
                    kb.mm(psA[hh // 4][:nt, (hh % 4) * 128:(hh % 4) * 128 + nt], NbH[0][hh // 4][:nt, hh % 4, :nt],
                          ident_r[:nt, :nt], True, True)
                for half in range(2):
                    kb.cp(YbH[0][half][:nt, :, :nt], ps3(psA[half], nt)[:nt, :, :nt], eng='act' if half == 0 else 'dve')
            else:
                for hh in range(8):
                    kb.tr(psT[0][:nt, hh * 128:hh * 128 + nt], NbH[0][hh // 4][:nt, hh % 4, :nt], ident[:nt, :nt])
                for half in range(2):
                    kb.cp(YbH[0][half][:nt, :, :nt], psT[0].re("p (h c) -> p h c", h=8)[:nt, half * 4:half * 4 + 4, :nt],
                          eng='act' if half == 0 else 'dve')
            if use_q:
                trans8(qnT2, qn, nt, 128, psT[1])
                pqk = (psA[2], psA[3])
                headmm(pqk, lambda hh, bank, c: kb.mm(bank[:nt, c:c + nt], qnT2[:, hh, :nt], knT[:, hh, :nt], True, True))
                for half in range(2):
                    kb.tt(intra[:nt, half * 4:half * 4 + 4, :nt], ps3(pqk[half], nt)[:nt, :, :nt],
                          DmH[half][:nt, :, :nt], ALU.mult)
                trans8(intraT, intra, nt, nt, psT[0])
                kb.tt(qg[:nt], qn[:nt], egc[:nt].unsq(2).bc([nt, 8, 128]), ALU.mult, eng='pool')
                trans8(qgT, qg, nt, 128, psT[1])
            for half in range(2):
                kb.tt(TTH[half][:nt, :, :nt], rd(YbH[0][half])[:nt, :, :nt],
                      ident_f[:nt, :nt].unsq(1).bc([nt, 4, nt]), ALU.add, eng='dve' if half == 0 else 'pool')
                kb.cp(TTbH[half][:nt, :, :nt], TTH[half][:nt, :, :nt], eng='act')
            nlev = 0
            while (1 << (nlev + 1)) < nt:
                nlev += 1
            cur = 0
            for lev in range(1, nlev + 1):
                nxt = 1 - cur
                last = (lev == nlev)
                pN = (psA[0], psA[1])
                pY = (psA[2], psA[3])
                pT = (psA[4], psA[5])
                for half in range(2):
                    for j in range(4):
                        kb.mm(pN[half][:nt, j * 128:j * 128 + nt], YbH[cur][half][:nt, j, :nt], NbH[cur][half][:nt, j, :nt], True, True)
                    if not last:
                        for j in range(4):
                            kb.mm(pY[half][:nt, j * 128:j * 128 + nt], NbH[cur][half][:nt, j, :nt], YbH[cur][half][:nt, j, :nt], True, True)
                for half in range(2):
                    kb.cp(NbH[nxt][half][:nt, :, :nt], ps3(pN[half], nt)[:nt, :, :nt], eng='act')
                    if not last:
                        kb.cp(YbH[nxt][half][:nt, :, :nt], ps3(pY[half], nt)[:nt, :, :nt], eng='dve')
                for half in range(2):
                    for j in range(4):
                        kb.mm(pT[half][:nt, j * 128:j * 128 + nt], NbH[nxt][half][:nt, j, :nt], TTbH[half][:nt, j, :nt], True, True)
                for half in range(2):
                    kb.tt(TTH[half][:nt, :, :nt], TTH[half][:nt, :, :nt], ps3(pT[half], nt)[:nt, :, :nt], ALU.add)
                    kb.cp(TTbH[half][:nt, :, :nt], TTH[half][:nt, :, :nt], eng='act')
                cur = nxt
            kb.tt(Ru[:nt], vtm[:nt], beta[:nt].unsq(2).bc([nt, 8, 128]), ALU.mult, eng='dve' if hp else 'pool')
            kb.tt(Rw[:nt], kn[:nt], bexp[:nt].unsq(2).bc([nt, 8, 128]), ALU.mult, eng='dve' if hp else 'pool')
            pu = (psA[0], psA[1])
            pw = (psA[2], psA[3])
            headmm(pu, lambda hh, bank, c: kb.mm(bank[:nt, c:c + 128], TTbH[hh // 4][:nt, hh % 4, :nt], Ru[:nt, hh, :], True, True))
            headmm(pw, lambda hh, bank, c: kb.mm(bank[:, c:c + nt], Rw[:nt, hh, :], TTbH[hh // 4][:nt, hh % 4, :nt], True, True))
            evac2(ub, pu, nt, 128)
            evac2(wT, pw, 128, nt)
            pws = (psA[4], psA[5])
            headmm(pws, lambda hh, bank, c: kb.mm(bank[:nt, c:c + 128], wT[:, hh, :nt], Sb[:, hh, :], True, True))
            for half in range(2):
                kb.tt(vn[:nt, half * 4:half * 4 + 4, :], ub[:nt, half * 4:half * 4 + 4, :], ps3(pws[half], nt)[:nt], ALU.subtract)
            if use_q:
                po = (psA[0], psA[1])

                def omm(hh, bank, c):
                    kb.mm(bank[:nt, c:c + 128], intraT[:nt, hh, :nt], vn[:nt, hh, :], True, False)
                    kb.mm(bank[:nt, c:c + 128], qgT[:, hh, :nt], Sb[:, hh, :], False, True)
                headmm(po, omm)
            kb.tt(kd[:nt], kn[:nt], kdec[:nt].unsq(2).bc([nt, 8, 128]), ALU.mult, eng='pool')
            pds = (psA[2], psA[3])
            headmm(pds, lambda hh, bank, c: kb.mm(bank[:, c:c + 128], kd[:nt, hh, :], vn[:nt, hh, :], True, True))
            kb.tt(Sm, Sm, elast.unsq(2).bc([128, 8, 128]), ALU.mult, eng='pool')
            for half in range(2):
                kb.tt(Sm[:, half * 4:half * 4 + 4, :], Sm[:, half * 4:half * 4 + 4, :], ps3(pds[half], 128), ALU.add)
            kb.cp(Sb, Sm, eng='act')
            if use_q:
                ot = t - q.own0
                c0 = ot * 128
                z = zt[ot % 2]
                kb.dma(z[:nt], q.z_s[c0:c0 + nt, :])
                for half in range(2):
                    kb.act(onf[:nt, half * 4:half * 4 + 4, :], ps3(po[half], nt)[:nt], AF.Square)
                kb.red(st[6][:nt], onf[:nt])
                kb.rstd(st[7][:nt], st[6][:nt], 128.0, st[5][:nt])
                for half in range(2):
                    kb.tt(onf[:nt, half * 4:half * 4 + 4, :], ps3(po[half], nt)[:nt],
                          st[7][:nt, half * 4:half * 4 + 4].unsq(2).bc([nt, 4, 128]), ALU.mult)
                kb.tt(onf[:nt], onf[:nt], g_go_bc[:nt].unsq(1).bc([nt, 8, 128]), ALU.mult, eng='pool')
                kb.tt(ob[:nt], onf[:nt], z[:nt].re("p (h c) -> p h c", h=8), ALU.mult, eng='pool')
                oT_ = obT[ot % 2]
                trans8(oT_, ob, nt, 128, psT[0])
                kb.dma(q.oT_s[8:16].re("h p c -> p h c")[:, :, c0:c0 + nt], oT_[:, :, :nt])

        for q in seqs:
            if q.name not in SEQS_ON:
                continue
            if q.prompt:
                kb.memset(hist3, 0.0)
                kb.memset(Sm, 0.0)
                kb.memset(Sb, 0.0)
            else:
                kb.dma(h3o[:3], q.c_gconv)
                for ct in range(24):
                    kb.tr(psA[ct % 2][:, 0:3], h3o[:3, ct * 128:(ct + 1) * 128], ident_f[:3, :3])
                    kb.cp(hist3[:, ct, :], psA[ct % 2][:, 0:3], eng=evac_eng())
                kb.dma(Sm, q.c_S.re("h k v -> k h v"))
                kb.cp(Sb, Sm, eng='act')
            def run_blocks(tiles, bs, hTb_, raw_, acc_, qkvT_, with_q, WGq_):
                nct = 24 if with_q else 16
                for bi in range(0, len(tiles), bs):
                    bl = tiles[bi:bi + bs]
                    ntok = sum(nt for _, nt in bl)
                    for j, (t, nt) in enumerate(bl):
                        kb.dma(hTb_[:, :, j * 128:j * 128 + nt],
                               q.hT_s[t - q.ncache].re("p (k c) -> p k c", k=16)[:, :, :nt])
                    cts = list(range(8, 24)) + (list(range(8)) if with_q else [])
                    for n_, ct in enumerate(cts):
                        pb = psA[n_ % 2]
                        for kt in range(16):
                            wsl = WGq_[:, kt, ct * 128:(ct + 1) * 128] if ct < 8 else WGkv[:, kt, (ct - 8) * 128:(ct - 7) * 128]
                            kb.mm(pb[:, :ntok], wsl, hTb_[:, kt, :ntok], kt == 0, kt == 15)
                        rw = raw_[n_ % 2]
                        ac = acc_[n_ % 2]
                        slot = ct if with_q else ct - 8
                        kb.cp(rw[:, 0:3], hist3[:, ct, :], eng='pool')
                        kb.cp(rw[:, 3:3 + ntok], pb[:, :ntok], eng='act')
                        kb.cp(hist3[:, ct, :], rw[:, ntok:ntok + 3], eng='pool')
                        kb.ts(ac[:, :ntok], rw[:, 0:ntok], wconv[:, ct, 0:1], None, ALU.mult)
                        for k_ in range(1, 4):
                            kb.stt(ac[:, :ntok], rw[:, k_:k_ + ntok], wconv[:, ct, k_:k_ + 1], ac[:, :ntok], ALU.mult, ALU.add)
                        kb.act(qkvT_[:, slot, :ntok], ac[:, :ntok], AF.Silu)
                    qk = (qkvT_, 8, 16, 0) if with_q else (qkvT_, 0, 8, None)
                    for j, (t, nt) in enumerate(bl):
                        gdn_tile(q, t, nt, j, with_q and t >= q.own0, qk)

            nA = ((q.own0 - q.ncache - 3) // 4) * 4 if q.prompt else 0
            nA = max(nA, 0)
            tilesA, tilesB = q.xt[:nA], q.xt[nA:]
            if tilesA:
                with Phase(kb) as p2a:
                    hTb4 = kb.sb([128, 16, 512], BF16, es=p2a)
                    raw4 = [kb.sb([128, 516], F32, es=p2a) for _ in range(2)]
                    acc4 = [kb.sb([128, 512], F32, es=p2a) for _ in range(2)]
                    qkvT4 = kb.sb([128, 16, 512], BF16, es=p2a)
                    run_blocks(tilesA, 4, hTb4, raw4, acc4, qkvT4, False, None)
            with Phase(kb) as p2b:
                WGq = kb.sb([128, 16, 1024], BF16, es=p2b)
                for k4 in range(4):
                    kb.dma(WGq[:, 4 * k4:4 * k4 + 4, :], w_in_v[:, 4 * k4:4 * k4 + 4, 1088:2112], q='pool')
                hTb2 = kb.sb([128, 16, 256], BF16, es=p2b)
                raw2 = [kb.sb([128, 260], F32, es=p2b) for _ in range(2)]
                acc2 = [kb.sb([128, 256], F32, es=p2b) for _ in range(2)]
                qkvT2 = kb.sb([128, 24, 256], BF16, es=p2b)
                run_blocks(tilesB, 2, hTb2, raw2, acc2, qkvT2, True, WGq)
            kb.dma(q.o_S.re("h k v -> k h v"), Sm)
            for ct in range(24):
                pb = psA[(ct // 4) % 2]
                kb.tr(pb[:3, (ct % 4) * 128:(ct % 4 + 1) * 128], hist3[:, ct, :], ident_f)
                if ct % 4 == 3:
                    kb.cp(h3o[:3, (ct - 3) * 128:(ct + 1) * 128], pb[:3, :], eng=evac_eng())
            kb.dma(q.o_gconv, h3o[:3])

    SCALE = 192.0 ** -0.5
    if 3 in stages:
      with Phase(kb) as p3:
        WUK = kb.sb([128, 4, 2048], BF16, es=p3)
        stg3 = [kb.sb([128, 2048 if not CAST_DMA else 2], F32, es=p3) for _ in range(2)]
        wl3 = WLoader(kb, stg3, ('pool', 'dve'))
        wkv_v = w_kv_up.re("(kt p) c -> p kt c", p=128)
        for kt in range(4):
            wl3.load(WUK[:, kt:kt + 1, :], wkv_v[:, kt:kt + 1, :])
        gk_col = kb.sb([128, 1], F32, es=p3)
        kb.dma(gk_col, g_k_nope.re("(p o) -> p o", o=1))
        ckvT3 = kb.sb([128, 4, NT * 128], BF16, es=p3)
        krT3 = kb.sb([64, NT * 128], BF16, es=p3)
        KT = kb.sb([128, NT * 128], BF16, es=p3)
        Vh = kb.sb([128, NT, 128], BF16, es=p3)
        qn_h = kb.sb([128, NOWN * 128], BF16, es=p3)
        qr_h = kb.sb([64, NOWN * 128], BF16, es=p3)
        kmask_sb = kb.sb([128, NT], F32, es=p3)
        sqb = [kb.sb([128, 512], BF16, es=p3) for _ in range(2)]
        rst = [kb.sb([128, 512], F32, es=p3) for _ in range(2)]
        PT = [kb.sb([128, 512], BF16, es=p3) for _ in range(3)]
        oT3 = [kb.sb([128, 512], BF16, es=p3) for _ in range(2)]
        den = kb.sb([128, 512], F32, es=p3)
        dacc = kb.sb([128, 512], F32, es=p3)
        cnt3 = [0, 0, 0]
        for q in seqs:
            if q.name not in SEQS_ON:
                continue
            nkeys = q.nkeys
            for c in range(4):
                kb.dma(ckvT3[:, c, :nkeys], q.ckvT_s[:, c, :nkeys])
            kb.dma(krT3[:, :nkeys], q.krT_s[:, :nkeys])
            if q.prompt:
                kb.dma(kmask_sb, q.kmask)
            else:
                kb.memset(kmask_sb, 0.0)
            ktiles = [(t, min(128, nkeys - t * 128)) for t in range(q.ntc)]
            if q.prompt:
                groups = [[(q.own0, 128)]] + [[(t, 128) for t in range(a, a + 4)] for a in range(q.own0 + 1, q.ntc, 4)]
            else:
                groups = [[(q.own0, 16)]]
            nq_all = sum(nt for g_ in groups for _, nt in g_)
            for hh in range(8):
                for bi, k0 in enumerate(range(0, nkeys, 512)):
                    kw = min(512, nkeys - k0)
                    pk = psA[bi % 2]
                    for c in range(4):
                        kb.mm(pk[:, :kw], WUK[:, c, hh * 256:hh * 256 + 128], ckvT3[:, c, k0:k0 + kw], c == 0, c == 3)
                    sq_ = sqb[bi % 2]
                    kb.act(sq_[:, :kw], pk[:, :kw], AF.Square)
                    pss = psA[2 + bi % 2]
                    kb.mm(pss[:, :kw], ones_b, sq_[:, :kw], True, True)
                    rs_ = rst[bi % 2]
                    kb.act(rs_[:, :kw], pss[:, :kw], AF.Ln, bias=EPS, scale=1.0 / 128)
                    kb.act(rs_[:, :kw], rs_[:, :kw], AF.Exp, scale=-0.5)
                    kb.stt(KT[:, k0:k0 + kw], pk[:, :kw], gk_col, rs_[:, :kw], ALU.mult, ALU.mult)
                full = [t for t, nk in ktiles if nk == 128]
                for gi, a in enumerate(range(0, len(full), 4)):
                    grp = full[a:a + 4]
                    pv = psA[4 + gi % 2]
                    for sl, t in enumerate(grp):
                        for c in range(4):
                            kb.mm(pv[:, sl * 128:(sl + 1) * 128], ckvT3[:, c, t * 128:(t + 1) * 128],
                                  WUK[:, c, hh * 256 + 128:hh * 256 + 256], c == 0, c == 3)
                    kb.cp(Vh[:, grp[0]:grp[0] + len(grp), :].re("p t c -> p (t c)"), pv[:, :len(grp) * 128], eng=evac_eng())
                for t, nk in ktiles:
                    if nk < 128:
                        pv = psA[4]
                        for c in range(4):
                            kb.mm(pv[:nk, 0:128], ckvT3[:, c, t * 128:t * 128 + nk],
                                  WUK[:, c, hh * 256 + 128:hh * 256 + 256], c == 0, c == 3)
                        kb.cp(Vh[:nk, t, :], pv[:nk, 0:128], eng=evac_eng())
                kb.dma(qn_h[:, :nq_all], q.qnT_s[:, hh, :nq_all])
                kb.dma(qr_h[:, :nq_all], q.qrT_s[:, hh, :nq_all])
                for G in groups:
                    ntq = sum(nt for _, nt in G)
                    c0 = (G[0][0] - q.own0) * 128
                    vis = [kt for kt in ktiles if (kt[0] <= G[-1][0] or not q.prompt)]
                    pnum, pden = psA[2], psA[3]

                    def geom(vi):
                        kt, nk = vis[vi]
                        off = (max(G[0][0], kt) - G[0][0]) * 128 if q.prompt else 0
                        return kt, nk, off, ntq - off

                    def emit_qk(vi):
                        kt, nk, off, nq = geom(vi)
                        pst = psA[vi % 2]
                        kb.mm(pst[:nk, :nq], KT[:, kt * 128:kt * 128 + nk], qn_h[:, c0 + off:c0 + off + nq], True, False)
                        kb.mm(pst[:nk, :nq], krT3[:, kt * 128:kt * 128 + nk], qr_h[:, c0 + off:c0 + off + nq], False, True)

                    emit_qk(0)
                    for vi in range(len(vis)):
                        kt, nk, off, nq = geom(vi)
                        if vi + 1 < len(vis):
                            emit_qk(vi + 1)
                        pst = psA[vi % 2]
                        pt = PT[cnt3[1] % 3]
                        cnt3[1] += 1
                        kb.act(pt[:nk, :nq], pst[:nk, :nq], AF.Exp, bias=kmask_sb[:nk, kt:kt + 1], scale=SCALE)
                        if q.prompt and kt >= G[0][0]:
                            kb.memset(pt[64:128, 0:64], 0.0, eng='pool')
                        if vi == 0:
                            kb.cp(dacc[:, :ntq], pt[:, :ntq], eng='dve')
                        else:
                            kb.tt(dacc[:nk, off:off + nq], dacc[:nk, off:off + nq], pt[:nk, :nq], ALU.add)
                        kb.mm(pnum[:, off:off + nq], Vh[:nk, kt, :], pt[:nk, :nq], vi == 0, vi == len(vis) - 1)
                    kb.mm(pden[:, :ntq], ones_f, dacc[:, :ntq], True, True)
                    kb.ts(den[:, :ntq], pden[:, :ntq], 1e-30, None, ALU.max)
                    kb.recip(den[:, :ntq], den[:, :ntq])
                    o_ = oT3[cnt3[2] % 2]
                    cnt3[2] += 1
                    kb.tt(o_[:, :ntq], pnum[:, :ntq], den[:, :ntq], ALU.mult)
                    kb.dma(q.oT_s[hh, :, c0:c0 + ntq], o_[:, :ntq])

    if 4 in stages:
      cols0 = {}
      tot = 0
      for q in seqs:
          cols0[q.name] = tot
          tot += sum(nt for t, nt in q.xt if t >= q.own0)
      TOT = tot
      hid_s = scr("hid_s", [44, 128, TOT], BF16)
      with Phase(kb) as p4:
        h2T = kb.sb([128, 16, TOT], BF16, es=p4)
        with Phase(kb) as p4a:
            WO = kb.sb([128, 16, 2048], BF16, es=p4a)
            stg4 = [kb.sb([128, 2048 if not CAST_DMA else 2], F32, es=p4a) for _ in range(2)]
            wl4 = WLoader(kb, stg4, ('pool', 'dve', 'act'))
            wo_v = w_out.re("(kt p) c -> p kt c", p=128)
            WOd = [WO.sub(dc) for dc in range(4)]
            for dc in range(4):
                for k4 in range(4):
                    wl4.load(WOd[dc][:, 4 * k4:4 * k4 + 4, dc * 512:(dc + 1) * 512], wo_v[:, 4 * k4:4 * k4 + 4, dc * 512:(dc + 1) * 512])
                wl4.engs = ('pool',)
            g_ffn_bc = kb.sb([128, D], F32, es=p4a)
            load_bc(kb, g_ffn_bc, g_ffn)
            oTt = [kb.sb([128, 16, 128], BF16, es=p4a) for _ in range(2)]
            xt4 = [kb.sb([128, D], F32, es=p4a) for _ in range(2)]
            xn = [kb.sb([128, D], F32, es=p4a) for _ in range(2)]
            junk4 = kb.sb([128, D], BF16, es=p4a)
            h2 = [kb.sb([128, D], BF16, es=p4a) for _ in range(2)]
            sm4 = kb.sb([128, 4], F32, es=p4a)
            n4 = 0
            tl4 = []
            for q in seqs:
                if q.name not in SEQS_ON:
                    continue
                for (t, nt) in q.xt:
                    if t >= q.own0:
                        tl4.append((q, t, nt))

            def p4_head(i):
                q, t, nt = tl4[i]
                b = i % 2
                ot = t - q.own0
                c0 = ot * 128
                xr0 = (t - q.ncache) * 128
                kb.dma(oTt[b][:, :, :nt], q.oT_s[:, :, c0:c0 + nt].re("k p c -> p k c"), q='pool')
                kb.dma(xt4[b][:nt], q.x[xr0:xr0 + nt, :], q='pool')
                for dc in range(4):
                    pb = psA[dc]
                    for kt in range(16):
                        kb.mm(pb[:nt], oTt[b][:, kt, :nt], WOd[dc][:, kt, dc * 512:(dc + 1) * 512], kt == 0, kt == 15)
                    kb.tt(xn[b][:nt, dc * 512:(dc + 1) * 512], pb[:nt], xt4[b][:nt, dc * 512:(dc + 1) * 512], ALU.add)
                kb.dma(q.xn_s[c0:c0 + nt, :], xn[b][:nt])
                ss, tmp, rs = sm4[:nt, 0:1], sm4[:nt, 1:2], sm4[:nt, 2:3]
                kb.act(junk4[:nt], xn[b][:nt], AF.Square, accum=ss)
                kb.rstd(rs, ss, D, tmp)
                kb.stt(h2[b][:nt], xn[b][:nt], rs, g_ffn_bc[:nt], ALU.mult, ALU.mult)

            def p4_tail(i):
                q, t, nt = tl4[i]
                b = i % 2
                col = cols0[q.name] + (t - q.own0) * 128
                for half in range(2):
                    for k in range(8):
                        kt = half * 8 + k
                        kb.tr(psT[half][:, k * 128:k * 128 + nt], h2[b][:nt, kt * 128:(kt + 1) * 128], ident[:nt, :nt])
                    kb.cp(h2T[:, half * 8:half * 8 + 8, col:col + nt],
                          psT[half].re("p (k c) -> p k c", k=8)[:, :, :nt], eng='act' if half == 0 else 'dve')

            for i in range(len(tl4)):
                p4_head(i)
                if i > 0:
                    p4_tail(i - 1)
            p4_tail(len(tl4) - 1)
        with Phase(kb) as p4b:
            Wg = [kb.sb([128, 16, 512], BF16, es=p4b) for _ in range(2)]
            Wu = [kb.sb([128, 16, 512], BF16, es=p4b) for _ in range(2)]
            cw4 = [kb.sb([4, 512], F32, es=p4b) for _ in range(2)]
            taps = [kb.sb([128, 4], F32, es=p4b) for _ in range(2)]
            cf2 = {q.name: kb.sb([2, 512], F32, es=p4b) for q in seqs if not q.prompt}
            hist2 = kb.sb([128, 2], F32, es=p4b)
            graw = [kb.sb([128, 516], F32, es=p4b) for _ in range(2)]
            gcv = [kb.sb([128, 512], F32, es=p4b) for _ in range(2)]
            sg = [kb.sb([128, 512], F32, es=p4b) for _ in range(2)]
            hid = [kb.sb([128, 512], BF16, es=p4b) for _ in range(3)]
            fout = {q.name: kb.sb([128, 4, 2], F32, es=p4b) for q in seqs}
            fo2 = [kb.sb([2, 512], F32, es=p4b) for _ in range(2)]
            wgv = w_gate.re("(kt p) c -> p kt c", p=128)
            wuv = w_up.re("(kt p) c -> p kt c", p=128)
            n5 = 0
            stg5 = [kb.sb([128, 4096 if not CAST_DMA else 2], F32, es=p4b) for _ in range(2)]
            wl5 = WLoader(kb, stg5, ('pool',))
            def load_fc(fc):
                wb = fc % 2
                for k8 in range(2):
                    wl5.load(Wg[wb][:, 8 * k8:8 * k8 + 8, :], wgv[:, 8 * k8:8 * k8 + 8, fc * 512:(fc + 1) * 512])
                    wl5.load(Wu[wb][:, 8 * k8:8 * k8 + 8, :], wuv[:, 8 * k8:8 * k8 + 8, fc * 512:(fc + 1) * 512])
                kb.dma(cw4[wb][0:3, :], w_fconv[:, fc * 512:(fc + 1) * 512], q='pool')
                kb.dma(cw4[wb][3:4, :], b_fconv.re("(o c) -> o c", o=1)[:, fc * 512:(fc + 1) * 512], q='pool')

            load_fc(0)
            for fc in range(11):
                wb = fc % 2
                if fc + 1 < 11:
                    load_fc(fc + 1)
                for q in seqs:
                    if not q.prompt and q.name in SEQS_ON:
                        pass
                for ffi in range(4):
                    ff = fc * 4 + ffi
                    tp = taps[ff % 2]
                    kb.tr(psA[5][:, 0:4], cw4[wb][:4, ffi * 128:(ffi + 1) * 128], ident_f[:4, :4])
                    kb.cp(tp, psA[5][:, 0:4], eng='dve')
                    for q in seqs:
                        if q.name not in SEQS_ON:
                            continue
                        if q.prompt:
                            blocks = [(0, 128, True)] + [(128 + 512 * i, 512, False) for i in range((q.nown - 1) // 4)]
                            kb.memset(hist2, 0.0, eng='pool')
                        else:
                            blocks = [(0, 16, False)]
                            if ffi == 0:
                                kb.dma(cf2[q.name], q.c_fconv[:, fc * 512:(fc + 1) * 512])
                                q.cf2cur = cf2[q.name]
                            kb.tr(psA[5][:, 8:10], q.cf2cur[:2, ffi * 128:(ffi + 1) * 128], ident_f[:2, :2])
                            kb.cp(hist2, psA[5][:, 8:10], eng='dve')
                        for (cb, ntok, halo) in blocks:
                            col = cols0[q.name] + cb
                            n5 += 1
                            pg = psA[n5 % 2]
                            pu_ = psA[2 + n5 % 2]
                            for kt in range(16):
                                kb.mm(pg[:, :ntok], Wg[wb][:, kt, ffi * 128:(ffi + 1) * 128], h2T[:, kt, col:col + ntok], kt == 0, kt == 15)
                            if not halo:
                                for kt in range(16):
                                    kb.mm(pu_[:, :ntok], Wu[wb][:, kt, ffi * 128:(ffi + 1) * 128], h2T[:, kt, col:col + ntok], kt == 0, kt == 15)
                            gr = graw[n5 % 2]
                            kb.cp(gr[:, 0:2], hist2, eng='pool')
                            kb.cp(gr[:, 2:2 + ntok], pg[:, :ntok], eng='act')
                            kb.cp(hist2, gr[:, ntok:ntok + 2], eng='pool')
                            if halo:
                                continue
                            gv = gcv[n5 % 2]
                            kb.ts(gv[:, :ntok], gr[:, 0:ntok], tp[:, 0:1], None, ALU.mult)
                            kb.stt(gv[:, :ntok], gr[:, 1:1 + ntok], tp[:, 1:2], gv[:, :ntok], ALU.mult, ALU.add)
                            kb.stt(gv[:, :ntok], gr[:, 2:2 + ntok], tp[:, 2:3], gv[:, :ntok], ALU.mult, ALU.add)
                            sg_ = sg[n5 % 2]
                            kb.act(sg_[:, :ntok], gv[:, :ntok], AF.Silu, bias=tp[:, 3:4])
                            hd = hid[n5 % 3]
                            kb.tt(hd[:, :ntok], sg_[:, :ntok], pu_[:, :ntok], ALU.mult)
                            kb.dma(hid_s[ff, :, col:col + ntok], hd[:, :ntok])
                        kb.cp(fout[q.name][:, ffi, :], hist2, eng='pool')
                for q in seqs:
                    if q.name not in SEQS_ON:
                        continue
                    n5 += 1
                    pb = psA[4]
                    for ffi in range(4):
                        kb.tr(pb[:2, ffi * 128:(ffi + 1) * 128], fout[q.name][:, ffi, :], ident_f)
                    kb.cp(fo2[n5 % 2], pb[:2, :], eng='dve')
                    kb.dma(q.o_fconv[:, fc * 512:(fc + 1) * 512], fo2[n5 % 2])
      with Phase(kb) as p4c:
        Wd = [kb.sb([128, 44, 512], BF16, es=p4c) for _ in range(2)]
        hidt = [kb.sb([128, 44, 512], BF16, es=p4c) for _ in range(2)]
        n7 = [0]
        xnt = [kb.sb([128, 512], F32, es=p4c) for _ in range(2)]
        yt = [kb.sb([128, 512], F32, es=p4c) for _ in range(2)]
        wdv = w_down.re("(kt p) c -> p kt c", p=128)
        n6 = 0
        stg6 = [kb.sb([128, 4096 if not CAST_DMA else 2], F32, es=p4c) for _ in range(2)]
        wl6 = WLoader(kb, stg6, ('pool',))
        def load_dc(dc):
            wb = dc % 2
            for k8 in range(0, 44, 8):
                k9 = min(44, k8 + 8)
                wl6.load(Wd[wb][:, k8:k9, :], wdv[:, k8:k9, dc * 512:(dc + 1) * 512])

        load_dc(0)
        for dc in range(4):
            wb = dc % 2
            if dc + 1 < 4:
                load_dc(dc + 1)
            for q in seqs:
                if q.name not in SEQS_ON:
                    continue
                tl = [(t, nt) for (t, nt) in q.xt if t >= q.own0 and not (q.halo and t == q.own0)]
                for bi in range(0, len(tl), 4):
                    blk = tl[bi:bi + 4]
                    nb_ = sum(nt for _, nt in blk)
                    n6 += 1
                    hb6 = hidt[n6 % 2]
                    colb = cols0[q.name] + (blk[0][0] - q.own0) * 128
                    kb.dma(hb6[:, :, :nb_], hid_s[:, :, colb:colb + nb_].re("k p c -> p k c"), q='pool')
                    for j, (t, nt) in enumerate(blk):
                        n7[0] += 1
                        b = n7[0] % 2
                        ot = t - q.own0
                        c0 = ot * 128
                        yr0 = (ot - 1) * 128 if q.halo else 0
                        kb.dma(xnt[b][:nt], q.xn_s[c0:c0 + nt, dc * 512:(dc + 1) * 512], q='pool')
                        pb = psA[n7[0] % 4]
                        for kt in range(44):
                            kb.mm(pb[:nt], hb6[:, kt, j * 128:j * 128 + nt], Wd[wb][:, kt, :], kt == 0, kt == 43)
                        kb.tt(yt[b][:nt], pb[:nt], xnt[b][:nt], ALU.add)
                        kb.dma(q.o_y[yr0:yr0 + nt, dc * 512:(dc + 1) * 512], yt[b][:nt])

    return nc, kb, es, seqs, I, locals()


SEQS_ON = ('p', 's0', 's1')


def rope_tables(pos):
    half = 32
    inv = (1.0 / (10000.0 ** (np.arange(half, dtype=np.float32) / np.float32(half)))).astype(np.float32)
    ang = pos.astype(np.float32)[:, None] * inv[None, :]
    c = np.cos(ang).astype(np.float32)
    s = np.sin(ang).astype(np.float32)
    return np.concatenate([c, c], 1), np.concatenate([-s, s], 1)


def prep_inputs(inp):
    maps = []
    L = NT * 128
    idx = np.arange(128)
    tri = np.stack([
        (idx[:, None] <= idx[None, :]),
        (idx[:, None] > idx[None, :]),
        (idx[:, None] >= idx[None, :]),
        (idx[:, None] > idx[None, :]),
    ]).astype(np.float32)
    wnames = ['g_attn_norm', 'w_in', 'g_q_lat', 'g_kv_lat', 'w_q_up', 'w_kv_up', 'g_q_nope', 'g_q_rope', 'g_k_nope',
              'g_k_rope', 'w_gdn_conv', 'a_log', 'dt_bias', 'g_gdn_out', 'w_out', 'g_ffn_norm', 'w_ffn_gate',
              'w_ffn_up', 'w_ffn_conv', 'b_ffn_conv', 'w_ffn_down']
    W = {k: np.ascontiguousarray(inp[k][0]) for k in wnames}
    W['w_kv_up'] = W['w_kv_up'].reshape(512, 8 * 256)
    cs_s, sn_s = rope_tables(PAST + np.arange(16))
    for c in range(8):
        b, q = c // 4, c % 4
        nreal = 2048 * (q + 1)
        pad = L - nreal
        m = dict(W)
        xcv = np.zeros((L, D), np.float32)
        xcv[pad:] = inp['x_prompt'][b, :nreal]
        m['p_x'] = xcv
        pos = np.maximum(np.arange(L) - pad, 0)
        cs, sn = rope_tables(pos)
        m['p_cs'] = cs
        m['p_sn'] = sn
        km = np.where(np.arange(L) >= pad, 0.0, NEG).astype(np.float32)
        m['p_kmask'] = np.ascontiguousarray(km.reshape(NT, 128).T)
        m['ident'] = np.eye(128, dtype=np.float32)
        m['tri'] = tri
        for j in range(2):
            sb_ = 2 * c + j
            nm = 's%d' % j
            m[nm + '_x'] = np.ascontiguousarray(inp['x_sample'][sb_])
            m[nm + '_cs'] = cs_s
            m[nm + '_sn'] = sn_s
            m[nm + '_clat'] = np.ascontiguousarray(inp['cache_mla_latent'][0, sb_])
            m[nm + '_ckr'] = np.ascontiguousarray(inp['cache_mla_krope'][0, sb_])
            m[nm + '_cgconv'] = np.ascontiguousarray(inp['state_gdn_conv'][0, sb_])
            m[nm + '_cS'] = np.ascontiguousarray(inp['state_gdn_S'][0, sb_])
            m[nm + '_cfconv'] = np.ascontiguousarray(inp['state_ffn_conv'][0, sb_])
        maps.append(m)
    return maps


_CACHE = {}


def kernel(**inputs):
    inp = {k: np.asarray(v) for k, v in inputs.items()}
    if 'prog' not in _CACHE:
        r_ = build_program(stages=(1, 2, 3, 4))
        nc, kb, es, seqs, I = r_[:5]
        kb.S.finish()
        es.close()
        _CACHE['prog'] = (nc, I)
    nc, I = _CACHE['prog']
    maps = prep_inputs(inp)
    maps = [{k: v for k, v in m.items() if k in I} for m in maps]
    res = run_bass_kernel_spmd(nc, maps, core_ids=list(range(8)))
    r = res.results
    f32 = np.float32
    y_p = np.zeros((2, 8192, D), f32)
    y_s = np.zeros((16, 16, D), f32)
    p_lat = np.zeros((1, 2, 8192, 512), f32)
    p_kr = np.zeros((1, 2, 8192, 64), f32)
    p_gc = np.zeros((1, 2, 3, 3072), f32)
    p_S = np.zeros((1, 2, 8, 128, 128), f32)
    p_fc = np.zeros((1, 2, 2, DFF), f32)
    s_lat = np.zeros((1, 16, 16, 512), f32)
    s_kr = np.zeros((1, 16, 16, 64), f32)
    s_gc = np.zeros((1, 16, 3, 3072), f32)
    s_S = np.zeros((1, 16, 8, 128, 128), f32)
    s_fc = np.zeros((1, 16, 2, DFF), f32)
    own_r0 = (NT - 16) * 128
    for c in range(8):
        b, q = c // 4, c % 4
        rc = r[c]
        sl = slice(q * 2048, (q + 1) * 2048)
        y_p[b, sl] = np.asarray(rc['p_oy'])
        p_lat[0, b, sl] = np.asarray(rc['p_olat'])[own_r0:]
        p_kr[0, b, sl] = np.asarray(rc['p_okr'])[own_r0:]
        if q == 3:
            p_gc[0, b] = np.asarray(rc['p_ogconv'])
            p_S[0, b] = np.asarray(rc['p_oS'])
            p_fc[0, b] = np.asarray(rc['p_ofconv'])
        for j in range(2):
            sb_ = 2 * c + j
            nm = 's%d' % j
            y_s[sb_] = np.asarray(rc[nm + '_oy'])
            s_lat[0, sb_] = np.asarray(rc[nm + '_olat'])
            s_kr[0, sb_] = np.asarray(rc[nm + '_okr'])
            s_gc[0, sb_] = np.asarray(rc[nm + '_ogconv'])
            s_S[0, sb_] = np.asarray(rc[nm + '_oS'])
            s_fc[0, sb_] = np.asarray(rc[nm + '_ofconv'])
    return (y_p, y_s, p_lat, p_kr, p_gc, p_S, p_fc, s_lat, s_kr, s_gc, s_S, s_fc)
```
